# Optimizing a Trainium2 kernel written in Bass

```python
import jax, jax.numpy as jnp
from jax import lax
import numpy as np

D_MODEL = 1024
BATCH = 4
SEQ = 8192
DEPTH = 2

GRID_W = 64
CTX_LEN = 256
FOUR_GROUPS = 4
FOUR_GC = D_MODEL // 8
FOUR_WIDTH = FOUR_GROUPS * FOUR_GC
NA_HEADS = 8
NA_HEAD_DIM = D_MODEL // 16
NA_WIDTH = NA_HEADS * NA_HEAD_DIM
NA_WIN_ROWS = 8
NA_WIN_COLS = 16
POOL_SIZES = (2, 4, 8, 16)
POOL_GC = D_MODEL // len(POOL_SIZES)
MLP_HIDDEN = 4 * D_MODEL
N_MOD = 6
RMS_EPS = 1e-6
N_EVEN = (DEPTH + 1) // 2
N_ODD = DEPTH // 2

kernel_name = "hybrid_fnet_natten_pool_dit_block"


def rms_norm(x, g):
    xf = x.astype(jnp.float32)
    y = xf * lax.rsqrt(jnp.mean(xf * xf, axis=-1, keepdims=True) + RMS_EPS)
    return (y * g.astype(jnp.float32)).astype(x.dtype)


def modulate(xn, shift, scale):
    return xn * (1.0 + scale) + shift


def sq_relu_mlp(u, w1, w2):
    a = jnp.maximum(u @ w1, 0.0)
    return (a * a) @ w2


def fourier_mix(f, w_four):
    B, N, _ = f.shape
    fg = f.reshape(B, N, FOUR_GROUPS, FOUR_GC).transpose(0, 2, 1, 3)
    mixed = jnp.fft.fft2(fg.astype(jnp.float32), norm="ortho").real.astype(f.dtype)
    out = jnp.einsum('bgnc,gcd->bngd', mixed, w_four)
    return out.reshape(B, N, FOUR_WIDTH)


def neighbourhood_attention(q, k, v, k_ctx, v_ctx, rpb):
    B, N, H, dh = q.shape
    rows = N // GRID_W
    wr = min(NA_WIN_ROWS, rows)

    def to_grid(a):
        return a.reshape(B, rows, GRID_W, H, dh).transpose(0, 3, 1, 2, 4)

    qg, kg, vg = to_grid(q), to_grid(k), to_grid(v)
    r = jnp.arange(rows)
    row_start = jnp.clip(r - wr // 2, 0, rows - wr)
    row_idx = row_start[:, None] + jnp.arange(wr)[None, :]
    kb = kg[:, :, row_idx]
    vb = vg[:, :, row_idx]
    cidx = jnp.arange(GRID_W)
    col_start = jnp.clip(cidx - NA_WIN_COLS // 2, 0, GRID_W - NA_WIN_COLS)
    col_mask = (cidx[None, :] >= col_start[:, None]) & (cidx[None, :] < col_start[:, None] + NA_WIN_COLS)
    dr = row_idx - r[:, None] + (NA_WIN_ROWS - 1)
    dc = jnp.clip(cidx[None, :] - cidx[:, None] + (NA_WIN_COLS - 1), 0, 2 * NA_WIN_COLS - 2)
    bias = rpb[:, dr[:, None, :, None], dc[None, :, None, :]]
    scale = dh ** -0.5
    s_win = jnp.einsum('bhrqd,bhrjkd->bhrqjk', qg, kb).astype(jnp.float32) * scale + bias.astype(jnp.float32)
    s_win = jnp.where(col_mask[:, None, :], s_win, -jnp.inf)
    s_win = s_win.reshape(B, H, rows, GRID_W, wr * GRID_W)
    s_ctx = jnp.einsum('bhrqd,blhd->bhrql', qg, k_ctx).astype(jnp.float32) * scale
    p = jax.nn.softmax(jnp.concatenate([s_win, s_ctx], axis=-1), axis=-1)
    p_win = p[..., :wr * GRID_W].reshape(B, H, rows, GRID_W, wr, GRID_W).astype(v.dtype)
    p_ctx = p[..., wr * GRID_W:].astype(v.dtype)
    o = jnp.einsum('bhrqjk,bhrjkd->bhrqd', p_win, vb) + jnp.einsum('bhrql,blhd->bhrqd', p_ctx, v_ctx)
    return o.transpose(0, 2, 3, 1, 4).reshape(B, N, H * dh)


def context_attention(qc, kc, vc):
    B, L, H, dh = qc.shape
    s = jnp.einsum('blhd,bmhd->bhlm', qc, kc).astype(jnp.float32) * (dh ** -0.5)
    p = jax.nn.softmax(s, axis=-1).astype(vc.dtype)
    return jnp.einsum('bhlm,bmhd->blhd', p, vc).reshape(B, L, H * dh)


def split_heads(a):
    return a.reshape(a.shape[0], a.shape[1], NA_HEADS, NA_HEAD_DIM)


def even_mixer(u, uc, w_in, w_four, q_g, k_g, rpb, w_out, update_ctx):
    p = u @ w_in
    f, q, k, v = jnp.split(p, [FOUR_WIDTH, FOUR_WIDTH + NA_WIDTH, FOUR_WIDTH + 2 * NA_WIDTH], axis=-1)
    q = rms_norm(split_heads(q), q_g)
    k = rms_norm(split_heads(k), k_g)
    v = split_heads(v)
    kvc = uc @ w_in[:, FOUR_WIDTH + NA_WIDTH:]
    kc, vc = jnp.split(kvc, [NA_WIDTH], axis=-1)
    kc = rms_norm(split_heads(kc), k_g)
    vc = split_heads(vc)
    o_four = fourier_mix(f, w_four)
    o_na = neighbourhood_attention(q, k, v, kc, vc, rpb)
    y = jnp.concatenate([o_four, o_na], axis=-1) @ w_out
    if not update_ctx:
        return y, None
    fc, qc = jnp.split(uc @ w_in[:, :FOUR_WIDTH + NA_WIDTH], [FOUR_WIDTH], axis=-1)
    qc = rms_norm(split_heads(qc), q_g)
    yc = jnp.concatenate([fourier_mix(fc, w_four), context_attention(qc, kc, vc)], axis=-1) @ w_out
    return y, yc


def multiscale_pool(u):
    N = u.shape[1]
    uf = u.astype(jnp.float32)
    zeros = jnp.zeros((uf.shape[0], 1, uf.shape[2]), jnp.float32)
    cs = jnp.concatenate([zeros, jnp.cumsum(uf, axis=1)], axis=1)
    t = jnp.arange(N)
    outs = []
    for gi, w in enumerate(POOL_SIZES):
        sl = slice(gi * POOL_GC, (gi + 1) * POOL_GC)
        lo = jnp.clip(t - w // 2, 0, N)
        hi = jnp.clip(t + w - w // 2, 0, N)
        csg = cs[..., sl]
        s = jnp.take(csg, hi, axis=1) - jnp.take(csg, lo, axis=1)
        cnt = (hi - lo).astype(jnp.float32)[None, :, None]
        outs.append(s / cnt - uf[..., sl])
    return jnp.concatenate(outs, axis=-1).astype(u.dtype)


def pool_mixer(u, w_pool, pool_scale):
    B, N, _ = u.shape
    pooled = multiscale_pool(u).reshape(B, N, len(POOL_SIZES), POOL_GC)
    y = jnp.einsum('bngc,gcd->bngd', pooled, w_pool).reshape(B, N, D_MODEL)
    return y * pool_scale


def setup_inputs(seed: int = 0) -> dict:
    key = jax.random.key(seed)
    ks = jax.random.split(key, 20)
    f32 = jnp.float32
    n = lambda k, s: jax.random.normal(k, s, f32)
    return {
        "x": n(ks[0], (BATCH, SEQ, D_MODEL)),
        "c": n(ks[1], (BATCH, D_MODEL)),
        "ctx": n(ks[2], (BATCH, CTX_LEN, D_MODEL)),
        "c_ctx": n(ks[3], (D_MODEL,)),
        "w_mod": n(ks[4], (DEPTH, D_MODEL, N_MOD * D_MODEL)) * D_MODEL ** -0.5,
        "b_mod": n(ks[5], (DEPTH, N_MOD * D_MODEL)) * 0.02,
        "norm1_g": 1.0 + 0.02 * n(ks[6], (DEPTH, D_MODEL)),
        "norm2_g": 1.0 + 0.02 * n(ks[7], (DEPTH, D_MODEL)),
        "w_in_even": n(ks[8], (N_EVEN, D_MODEL, FOUR_WIDTH + 3 * NA_WIDTH)) * D_MODEL ** -0.5,
        "w_four": n(ks[9], (N_EVEN, FOUR_GROUPS, FOUR_GC, FOUR_GC)) * FOUR_GC ** -0.5,
        "q_norm_g": 1.0 + 0.02 * n(ks[10], (N_EVEN, NA_HEAD_DIM)),
        "k_norm_g": 1.0 + 0.02 * n(ks[11], (N_EVEN, NA_HEAD_DIM)),
        "rpb": 0.1 * n(ks[12], (N_EVEN, NA_HEADS, 2 * NA_WIN_ROWS - 1, 2 * NA_WIN_COLS - 1)),
        "w_out_even": n(ks[13], (N_EVEN, FOUR_WIDTH + NA_WIDTH, D_MODEL)) * (FOUR_WIDTH + NA_WIDTH) ** -0.5,
        "w_pool": n(ks[14], (N_ODD, len(POOL_SIZES), POOL_GC, POOL_GC)) * POOL_GC ** -0.5,
        "pool_scale": 1.0 + 0.1 * n(ks[15], (N_ODD, D_MODEL)),
        "w_mlp1": n(ks[16], (DEPTH, D_MODEL, MLP_HIDDEN)) * D_MODEL ** -0.5,
        "w_mlp2": n(ks[17], (DEPTH, MLP_HIDDEN, D_MODEL)) * MLP_HIDDEN ** -0.5,
    }


def reference(x, c, ctx, c_ctx, w_mod, b_mod, norm1_g, norm2_g, w_in_even, w_four, q_norm_g, k_norm_g,
              rpb, w_out_even, w_pool, pool_scale, w_mlp1, w_mlp2):
    h, hc = x, ctx
    sc = jax.nn.silu(c)
    scc = jax.nn.silu(c_ctx)
    for l in range(DEPTH):
        update_ctx = any(j % 2 == 0 for j in range(l + 1, DEPTH))
        need_ctx_in = (l % 2 == 0) or update_ctx
        mod = (sc @ w_mod[l] + b_mod[l])[:, None, :]
        sh1, sc1, g1, sh2, sc2, g2 = jnp.split(mod, N_MOD, axis=-1)
        u = modulate(rms_norm(h, norm1_g[l]), sh1, sc1)
        if need_ctx_in:
            mod_c = scc @ w_mod[l] + b_mod[l]
            csh1, csc1, cg1, csh2, csc2, cg2 = jnp.split(mod_c, N_MOD, axis=-1)
            uc = modulate(rms_norm(hc, norm1_g[l]), csh1, csc1)
        if l % 2 == 0:
            e = l // 2
            y, yc = even_mixer(u, uc, w_in_even[e], w_four[e], q_norm_g[e], k_norm_g[e], rpb[e],
                               w_out_even[e], update_ctx)
        else:
            o = l // 2
            y = pool_mixer(u, w_pool[o], pool_scale[o])
            yc = pool_mixer(uc, w_pool[o], pool_scale[o]) if update_ctx else None
        h = h + g1 * y
        h = h + g2 * sq_relu_mlp(modulate(rms_norm(h, norm2_g[l]), sh2, sc2), w_mlp1[l], w_mlp2[l])
        if update_ctx:
            hc = hc + cg1 * yc
            hc = hc + cg2 * sq_relu_mlp(modulate(rms_norm(hc, norm2_g[l]), csh2, csc2), w_mlp1[l], w_mlp2[l])
    return h
```

```python
import numpy as np
import concourse.bass as bass
import concourse.mybir as mybir
from contextlib import ExitStack

F32 = mybir.dt.float32
BF16 = mybir.dt.bfloat16
AF = mybir.ActivationFunctionType
ALU = mybir.AluOpType
AX = mybir.AxisListType

PE, ACT, DVE, POOL, SP = "pe", "act", "dve", "pool", "sp"
COMPUTE = (PE, ACT, DVE, POOL)
N_DMA_SEMS = 24


class Buf:
    __slots__ = ("name", "ap", "writers", "readers")

    def __init__(self, name, ap):
        self.name = name
        self.ap = ap
        self.writers = {}
        self.readers = {}

    def __getitem__(self, idx):
        return V(self, self.ap[idx])

    def v(self):
        return V(self, self.ap)


class V:
    __slots__ = ("buf", "ap")

    def __init__(self, buf, ap):
        self.buf = buf
        self.ap = ap

    def __getitem__(self, idx):
        return V(self.buf, self.ap[idx])

    def re(self, pattern, **kw):
        return V(self.buf, self.ap.rearrange(pattern, **kw))


def _ap(x):
    return x.ap if isinstance(x, (V,)) else (x.ap if isinstance(x, Buf) else x)


def _buf(x):
    if isinstance(x, V):
        return x.buf
    if isinstance(x, Buf):
        return x
    return None


class Prog:
    def __init__(self, nc):
        self.nc = nc
        self.stack = ExitStack()
        self.ops = {e: [] for e in (PE, ACT, DVE, POOL, SP)}
        self.know = {e: {} for e in (PE, ACT, DVE, POOL, SP)}
        self.sems = {}
        for e in COMPUTE:
            self.sems[e] = self.stack.enter_context(nc.semaphore("c_" + e))
        self.dsems = [self.stack.enter_context(nc.semaphore("d%d" % i)) for i in range(N_DMA_SEMS)]
        self.dcount = [0] * N_DMA_SEMS
        self.dlast_issuer = [None] * N_DMA_SEMS
        self.drr = 0
        self.n_alloc = 0
        self.out_tokens = []

    def sbuf(self, name, shape, dtype):
        t = self.stack.enter_context(self.nc.sbuf_tensor(name, list(shape), dtype))
        return t

    def psum(self, name, shape, dtype=F32):
        t = self.stack.enter_context(self.nc.psum_tensor(name, list(shape), dtype))
        return t

    def sbuf_buf(self, name, shape, dtype):
        t = self.sbuf(name, shape, dtype)
        return Buf(name, t[tuple(slice(None) for _ in shape)])

    def psum_buf(self, name, shape, dtype=F32):
        t = self.psum(name, shape, dtype)
        return Buf(name, t[tuple(slice(None) for _ in shape)])

    def dram(self, name, shape, dtype, kind="Internal"):
        t = self.nc.dram_tensor(name, list(shape), dtype, kind=kind)
        return t

    def _deps(self, eng, reads, writes):
        toks = []
        for r in reads:
            b = _buf(r)
            if b is None:
                continue
            for k, t in b.writers.items():
                toks.append((t, "raw"))
        for w in writes:
            b = _buf(w)
            if b is None:
                continue
            for k, t in b.writers.items():
                toks.append((t, "waw"))
            for k, t in b.readers.items():
                toks.append((t, "war"))
        need = {}
        for t, kind in toks:
            if t[0] == "c":
                src = t[1]
                if src == eng:
                    if eng == PE:
                        continue
                key = ("c", src)
            else:
                key = ("d", t[1])
            if need.get(key, -1) < t[2]:
                need[key] = t[2]
        waits = []
        kn = self.know[eng]
        for key, val in need.items():
            if kn.get(key, -1) >= val:
                continue
            kn[key] = val
            waits.append((key, val))
        return waits

    def _mark(self, tok, reads, writes):
        for r in reads:
            b = _buf(r)
            if b is None:
                continue
            k = tok[:2]
            b.readers[k] = tok
        for w in writes:
            b = _buf(w)
            if b is None:
                continue
            b.writers = {tok[:2]: tok}
            b.readers = {}

    def op(self, eng, fn, reads=(), writes=()):
        waits = self._deps(eng, reads, writes)
        idx = len(self.ops[eng])
        rec = dict(fn=fn, waits=waits, signal=False, dma=None)
        self.ops[eng].append(rec)
        for key, val in waits:
            if key[0] == "c":
                self.ops[key[1]][val]["signal"] = True
        tok = ("c", eng, idx)
        self._mark(tok, reads, writes)
        return tok

    def dma(self, out, in_, queue=SP, **kw):
        reads, writes = [in_], [out]
        waits = self._deps(queue, reads, writes)
        s = self.drr
        self.drr = (self.drr + 1) % N_DMA_SEMS
        prev = self.dcount[s]
        if prev > 0:
            kn = self.know[queue]
            if kn.get(("d", s), -1) < prev:
                kn[("d", s)] = prev
                waits.append((("d", s), prev))
        self.dcount[s] = prev + 1
        o, i = _ap(out), _ap(in_)
        rec = dict(fn=lambda e: e.dma_start(out=o, in_=i, **kw), waits=waits, signal=False,
                   dma=(s, prev + 1))
        for key, val in waits:
            if key[0] == "c":
                self.ops[key[1]][val]["signal"] = True
        self.ops[queue].append(rec)
        tok = ("d", s, prev + 1)
        self._mark(tok, reads, writes)
        return tok

    def final_wait(self, eng, bufs):
        waits = self._deps(eng, bufs, [])
        rec = dict(fn=None, waits=waits, signal=False, dma=None)
        for key, val in waits:
            if key[0] == "c":
                self.ops[key[1]][val]["signal"] = True
        self.ops[eng].append(rec)

    def emit(self):
        nc = self.nc
        sigord = {}
        for e in COMPUTE:
            c = 0
            for i, rec in enumerate(self.ops[e]):
                if rec["signal"]:
                    c += 1
                    sigord[(e, i)] = c
        engobj = {PE: "tensor", ACT: "scalar", DVE: "vector", POOL: "gpsimd", SP: "sync"}

        def run(engname):
            def body(eng):
                for i, rec in enumerate(self.ops[engname]):
                    for key, val in rec["waits"]:
                        if key[0] == "c":
                            eng.wait_ge(self.sems[key[1]], sigord[(key[1], val)])
                        else:
                            eng.wait_ge(self.dsems[key[1]], 16 * val)
                    if rec["fn"] is None:
                        continue
                    ins = rec["fn"](eng)
                    if rec["dma"] is not None:
                        ins.then_inc(self.dsems[rec["dma"][0]], 16)
                    elif rec["signal"]:
                        ins.then_inc(self.sems[engname], 1)
            return body

        with nc.Block() as block:
            block.tensor(run(PE))
            block.scalar(run(ACT))
            block.vector(run(DVE))
            block.gpsimd(run(POOL))
            block.sync(run(SP))
        self.stack.close()

    def mm(self, out, lhsT, rhs, start=True, stop=True, **kw):
        o, l, r = _ap(out), _ap(lhsT), _ap(rhs)
        return self.op(PE, lambda e: e.matmul(o, l, r, start=start, stop=stop, **kw),
                       reads=[lhsT, rhs], writes=[out])

    def tr(self, out, in_, ident):
        o, i, d = _ap(out), _ap(in_), _ap(ident)
        return self.op(PE, lambda e: e.transpose(o, i, d), reads=[in_, ident], writes=[out])

    def act(self, out, in_, func, bias=None, scale=None, accum_out=None, eng=ACT):
        o, i = _ap(out), _ap(in_)
        kw = {}
        reads = [in_]
        writes = [out]
        if bias is not None:
            kw["bias"] = _ap(bias)
            reads.append(bias)
        if scale is not None:
            kw["scale"] = _ap(scale)
            reads.append(scale)
        if accum_out is not None:
            kw["accum_out"] = _ap(accum_out)
            writes.append(accum_out)
        return self.op(ACT, lambda e: e.activation(o, i, func, **kw), reads=reads, writes=writes)

    def ts(self, eng, out, in0, s1, s2, op0, op1=None, accum_out=None):
        o, i = _ap(out), _ap(in0)
        a1, a2 = _ap(s1), _ap(s2)
        reads = [in0, s1, s2]
        writes = [out]
        kw = {}
        if op1 is not None:
            kw["op1"] = op1
        if accum_out is not None:
            kw["accum_out"] = _ap(accum_out)
            writes.append(accum_out)
        return self.op(eng, lambda e: e.tensor_scalar(o, i, a1, a2, op0, **kw), reads=reads, writes=writes)

    def tt(self, eng, out, in0, in1, op):
        o, a, b = _ap(out), _ap(in0), _ap(in1)
        return self.op(eng, lambda e: e.tensor_tensor(o, a, b, op), reads=[in0, in1], writes=[out])

    def stt(self, eng, out, in0, scalar, in1, op0, op1):
        o, a, s, b = _ap(out), _ap(in0), _ap(scalar), _ap(in1)
        return self.op(eng, lambda e: e.scalar_tensor_tensor(o, a, s, b, op0, op1),
                       reads=[in0, scalar, in1], writes=[out])

    def copy(self, eng, out, in_):
        o, i = _ap(out), _ap(in_)
        if eng == ACT:
            return self.op(ACT, lambda e: e.copy(o, i), reads=[in_], writes=[out])
        return self.op(eng, lambda e: e.tensor_copy(o, i), reads=[in_], writes=[out])

    def memset(self, eng, out, val):
        o = _ap(out)
        return self.op(eng, lambda e: e.memset(o, val), reads=[], writes=[out])

    def recip(self, out, in_):
        o, i = _ap(out), _ap(in_)
        return self.op(DVE, lambda e: e.reciprocal(o, i), reads=[in_], writes=[out])


ARENA_BYTES = 207 * 1024
NSLOT = 33
NKV = 38
POOL_SIZES = (2, 4, 8, 16)


def _arena_setup(p):
    p.arena_t = p.sbuf("arena", [128, ARENA_BYTES // 4], F32)
    p.arena_off = 0

    def alloc(name, shape, dtype):
        esz = 4 if dtype == F32 else 2
        n = 1
        for s in shape[1:]:
            n *= s
        nbytes = (n * esz + 63) // 64 * 64
        off = p.arena_off
        assert off + nbytes <= ARENA_BYTES, ("arena overflow", name, off, nbytes)
        p.arena_off = off + nbytes
        p.arena_peak = max(getattr(p, "arena_peak", 0), p.arena_off)
        ap = p.arena_t[0:shape[0], off // 4:(off + nbytes) // 4]
        if dtype != F32:
            ap = ap.bitcast(dtype)
        ap = ap[:, 0:n]
        if len(shape) == 3:
            ap = ap.rearrange("p (a b) -> p a b", a=shape[1])
        elif len(shape) == 4:
            ap = ap.rearrange("p (a b c) -> p a b c", a=shape[1], b=shape[2])
        elif len(shape) == 5:
            ap = ap.rearrange("p (a b c d) -> p a b c d", a=shape[1], b=shape[2], c=shape[3])
        return Buf(name, ap)

    p.alloc = alloc


def _barrier(p):
    for e in (PE, ACT, DVE, POOL, SP):
        waits = []
        kn = p.know[e]
        for src in COMPUTE:
            if src == e or not p.ops[src]:
                continue
            idx = len(p.ops[src]) - 1
            while idx >= 0 and p.ops[src][idx]["fn"] is None:
                idx -= 1
            if idx < 0:
                continue
            if kn.get(("c", src), -1) < idx:
                kn[("c", src)] = idx
                waits.append((("c", src), idx))
                p.ops[src][idx]["signal"] = True
        for s in range(N_DMA_SEMS):
            if p.dcount[s] > 0 and kn.get(("d", s), -1) < p.dcount[s]:
                kn[("d", s)] = p.dcount[s]
                waits.append((("d", s), p.dcount[s]))
        p.ops[e].append(dict(fn=None, waits=waits, signal=False, dma=None))


def _dbuf(t, *idx):
    return t


def build_program(debug=False, upto=99):
    nc = bass.Bass("TRN2", target_bir_lowering=False)
    p = Prog(nc)
    _arena_setup(p)
    alloc = p.alloc

    def din(name, shape, dt=F32):
        return nc.dram_tensor(name, list(shape), dt, kind="ExternalInput")

    def D(name, t):
        return Buf(name, t.ap())

    xb_t = din("xb", [8192, 1024]); xb = D("xb", xb_t)
    xkv_t = din("xkv", [NKV * 128, 1024]); xkv = D("xkv", xkv_t)
    ctx_t = din("ctxb", [256, 1024]); ctxb = D("ctxb", ctx_t)
    cvec_t = din("cvec", [128, 16]); cvec = D("cvec", cvec_t)
    gvec_t = din("gvec", [128, 32]); gvec = D("gvec", gvec_t)
    qkg_t = din("qkg", [128, 2]); qkg = D("qkg", qkg_t)
    wmod_t = din("w_mod", [2, 1024, 6144]); wmod = D("w_mod", wmod_t)
    bmod_t = din("b_mod", [2, 6144]); bmod = D("b_mod", bmod_t)
    n1g_t = din("norm1_g", [2, 1024]); n1g = D("norm1_g", n1g_t)
    psc_t = din("pool_scale", [1, 1024]); psc = D("pool_scale", psc_t)
    win_t = din("w_in", [1024, 2048]); win = D("w_in", win_t)
    wfour_t = din("w_four", [4, 128, 128]); wfour = D("w_four", wfour_t)
    wout_t = din("w_out", [1024, 1024]); wout = D("w_out", wout_t)
    wpool_t = din("w_pool", [4, 256, 256]); wpool = D("w_pool", wpool_t)
    w1_t = din("w_mlp1", [2, 1024, 4096]); w1d = D("w_mlp1", w1_t)
    w2_t = din("w_mlp2", [2, 4096, 1024]); w2d = D("w_mlp2", w2_t)
    biasT_t = din("biasT", [128, 7, 8, 128]); biasT = D("biasT", biasT_t)
    cmask_t = din("cmask", [128, 128]); cmask = D("cmask", cmask_t)
    vm_t = din("vm", [128, NSLOT * 14]); vmd = D("vm", vm_t)
    ident_t = din("ident", [128, 128], BF16); identd = D("ident", ident_t)
    blk_t = din("blk", [128, 128], BF16); blkd = D("blk", blk_t)
    dftc_t = din("dftc", [128, 256], BF16); dftcd = D("dftc", dftc_t)
    dft1_t = din("dft1", [128, 256], BF16); dft1d = D("dft1", dft1_t)
    gtab_t = din("gtab", [128, 2, 128, NSLOT], BF16); gtabd = D("gtab", gtab_t)
    pm_t = din("pm", [128, 4, 3, 128], BF16); pmd = D("pm", pm_t)
    pmf_t = din("pmf", [128, 4, 128], BF16); pmfd = D("pmf", pmf_t)
    pml_t = din("pml", [128, 4, 128], BF16); pmld = D("pml", pml_t)

    out_t = nc.dram_tensor("out33", [NSLOT * 128, 1024], F32, kind="ExternalOutput"); outd = D("out33", out_t)

    modrow_t = nc.dram_tensor("modrow", [2, 2, 6144], F32); modrow = D("modrow", modrow_t)
    ofd_t = nc.dram_tensor("ofd", [4, 128, NSLOT * 128], BF16)
    ofd = [Buf("ofd%d" % g, ofd_t.ap()[g]) for g in range(4)]
    hmid_t = nc.dram_tensor("hmid", [NSLOT * 128, 1024], F32)
    hmid = [Buf("hmid%d" % s, hmid_t.ap()[s * 128:(s + 1) * 128, :]) for s in range(NSLOT)]
    h1_t = nc.dram_tensor("h1", [NSLOT * 128, 1024], F32)
    h1 = [Buf("h1_%d" % s, h1_t.ap()[s * 128:(s + 1) * 128, :]) for s in range(NSLOT)]
    outs = [Buf("out_%d" % s, out_t.ap()[s * 128:(s + 1) * 128, :]) for s in range(NSLOT)]
    dbg = {}
    if debug:
        def dout(name, shape, dt=F32):
            t = nc.dram_tensor(name, list(shape), dt, kind="ExternalOutput")
            dbg[name] = D(name, t)
            return dbg[name]
        dout("d_mod", [2, 2, 6144])
        dout("d_fT", [128, 4, 8192], BF16)
        dout("d_of", [4, 128, NSLOT * 128], BF16)
        dout("d_hmid", [NSLOT * 128, 1024])
        dout("d_h1", [NSLOT * 128, 1024])
        dout("d_hmid1", [NSLOT * 128, 1024])
        dout("d_kT", [128, 4, NKV * 128], BF16)
        dout("d_ona", [NSLOT * 128, 512], BF16)

    banks = [p.psum_buf("bank%d" % i, [128, 512], F32) for i in range(8)]

    def bank_bf(i):
        return V(banks[i], banks[i].ap.bitcast(BF16))

    ident = alloc("ident", [128, 128], BF16)
    blk = alloc("blk", [128, 128], BF16)
    cv = alloc("cv", [128, 16], F32)
    sT = alloc("sT", [128, 16], F32)
    gv = alloc("gv", [128, 32], F32)
    qk_g = alloc("qk_g", [128, 2], F32)
    epsb = alloc("epsb", [128, 1], F32)
    modT = alloc("modT", [128, 3, 48], F32)
    avec = alloc("avec", [128, 5, 8], F32)
    p.dma(ident, identd)
    p.dma(blk, blkd)
    p.dma(cv, cvec)
    p.dma(gv, gvec)
    p.dma(qk_g, qkg)
    p.memset(DVE, epsb, 1e-6)

    mk0 = p.arena_off
    p.act(sT, cv, AF.Silu)
    wm = [alloc("wm%d" % i, [128, 3072], F32) for i in range(2)]
    mrow = alloc("mrow", [2, 6144], F32)
    brow = alloc("brow", [2, 6144], F32)
    it = 0
    for l in range(2):
        p.dma(brow, V(bmod, bmod_t.ap()[l, :].partition_broadcast(2)))
        for half in range(2):
            for c in range(8):
                w = wm[it % 2]; it += 1
                p.dma(w, V(wmod, wmod_t.ap()[l, c * 128:(c + 1) * 128, half * 3072:(half + 1) * 3072]))
                for n in range(6):
                    p.mm(banks[n][0:2, :], sT[:, c:c + 9:8], w[:, n * 512:(n + 1) * 512],
                         start=(c == 0), stop=(c == 7))
            for n in range(6):
                col = half * 3072 + n * 512
                p.tt(DVE, mrow[:, col:col + 512], banks[n][0:2, :], brow[:, col:col + 512], ALU.add)
        p.dma(V(modrow, modrow_t.ap()[l]), mrow)
        if debug:
            p.dma(V(dbg["d_mod"], dbg["d_mod"].ap[l]), mrow)
    for i, (l, r) in enumerate(((0, 0), (0, 1), (1, 0))):
        src = modrow_t.ap()[l, r, :].rearrange("(j q) -> q j", q=128)
        p.dma(modT[:, i, :], V(modrow, src), allow_slow_non_contiguous=True)
    def mk_a(dst, gcol, mi, scoff):
        p.stt(DVE, avec[:, dst, :], modT[:, mi, scoff:scoff + 8], 1.0, gv[:, gcol:gcol + 8], ALU.add, ALU.mult)
    mk_a(0, 0, 0, 8)
    mk_a(1, 0, 1, 8)
    mk_a(2, 16, 0, 32)
    mk_a(3, 8, 2, 8)
    mk_a(4, 24, 2, 32)
    _barrier(p)
    p.arena_off = mk0

    if upto == 0:
        p.final_wait(SP, outs + list(dbg.values()))
        p.emit()
        return nc

    def front_end(src_views, a_v, b_v, uT, work, tok_major=None):
        nt = len(src_views)
        xts = []
        ss = work["ss"][work["i"] % 2]; work["i"] += 1
        for t, sv in enumerate(src_views):
            xt = work["xt"][work["xi"] % len(work["xt"])]; work["xi"] += 1
            p.dma(xt, sv)
            jk = work["junk"][work["ji"] % 3]; work["ji"] += 1
            p.act(jk, xt, AF.Square, accum_out=ss[:, t:t + 1])
            xts.append(xt)
        rs = work["rs"][work["i"] % 2]
        p.act(rs[:, 0:nt], ss[:, 0:nt], AF.Sqrt, bias=epsb, scale=1.0 / 1024)
        p.recip(rs[:, 0:nt], rs[:, 0:nt])
        for t, xt in enumerate(xts):
            xn = work["xn"][work["ni"] % 2]; work["ni"] += 1
            eng = DVE if (t % 2 == 0) else POOL
            p.ts(eng, xn, xt, rs[:, t:t + 1], None, ALU.mult)
            bk = work["trbank"][work["ti"] % len(work["trbank"])]; work["ti"] += 1
            pv = bank_bf(bk)
            for c in range(8):
                p.tr(pv[:, c * 128:(c + 1) * 128], xn[:, c * 128:(c + 1) * 128], ident)
            for c in range(8):
                p.act(uT[:, c, t * 128:(t + 1) * 128], pv[:, c * 128:(c + 1) * 128], AF.Identity,
                      bias=b_v[:, c:c + 1], scale=a_v[:, c:c + 1])
        return xts, rs

    def mk_work(nxt, trbanks):
        return dict(xt=[alloc("xt%d" % i, [128, 1024], F32) for i in range(nxt)],
                    junk=[alloc("junk%d" % i, [128, 1024], BF16) for i in range(3)], ji=0,
                    xn=[alloc("xn%d" % i, [128, 1024], BF16) for i in range(2)],
                    ss=[alloc("ss%d" % i, [128, 4], F32) for i in range(2)],
                    rs=[alloc("rs%d" % i, [128, 4], F32) for i in range(2)],
                    trbank=trbanks, i=0, xi=0, ni=0, ti=0)

    def load_cast(dst_view, src_view, stg_list, cnt, shape_note=None):
        stg = stg_list[cnt[0] % len(stg_list)]
        eng = (DVE, POOL, ACT)[cnt[0] % 3]
        cnt[0] += 1
        return stg, eng

    fT = alloc("fT", [128, 4, 8192], BF16)
    mk1 = p.arena_off
    work = mk_work(6, [6, 7])
    winf_s = alloc("winf_s", [128, 8, 512], F32)
    winf = alloc("winf", [128, 8, 512], BF16)
    p.dma(winf_s, V(win, win_t.ap()[:, 0:512].rearrange("(c q) n -> q c n", q=128)))
    p.copy(POOL, winf, winf_s)
    uTs = [alloc("uT%d" % i, [128, 8, 512], BF16) for i in range(2)]
    for grp in range(16):
        uT = uTs[grp % 2]
        svs = [V(xb, xb_t.ap()[(grp * 4 + t) * 128:(grp * 4 + t + 1) * 128, :]) for t in range(4)]
        front_end(svs, avec[:, 0, :], modT[:, 0, 0:8], uT, work)
        for g in range(4):
            bk = banks[g % 4]
            for c in range(8):
                p.mm(bk, winf[:, c, g * 128:(g + 1) * 128], uT[:, c, :], start=(c == 0), stop=(c == 7))
            if g % 2 == 0:
                p.copy(DVE, fT[:, g, grp * 512:(grp + 1) * 512], bk)
            else:
                p.copy(ACT, fT[:, g, grp * 512:(grp + 1) * 512], bk)
    if debug:
        p.dma(dbg["d_fT"], fT)
    _barrier(p)
    p.arena_off = mk1

    if upto == 1:
        p.final_wait(SP, outs + list(dbg.values()))
        p.emit()
        return nc

    dftc = alloc("dftc", [128, 256], BF16)
    dft1 = alloc("dft1", [128, 256], BF16)
    gtab = alloc("gtab", [128, 2, 128, NSLOT], BF16)
    wf_s = alloc("wf_s", [128, 4, 128], F32)
    wf_b = alloc("wf_b", [128, 4, 128], BF16)
    Mg = alloc("Mg", [128, 4, 256], BF16)
    AB = alloc("AB", [128, 64, 256], BF16)
    P1 = [alloc("P1_%d" % i, [128, 128, 128], BF16) for i in range(2)]
    ofs = [alloc("ofs%d" % i, [128, NSLOT, 128], BF16) for i in range(2)]
    p.dma(dftc, dftcd)
    p.dma(dft1, dft1d)
    p.dma(gtab, gtabd)
    p.dma(wf_s, V(wfour, wfour_t.ap().rearrange("g c d -> c g d")))
    p.copy(DVE, wf_b, wf_s)
    for g in range(4):
        for cs in range(2):
            p.mm(banks[0][:, cs * 128:(cs + 1) * 128], dftc[:, cs * 128:(cs + 1) * 128], wf_b[:, g, :])
        p.copy(DVE, Mg[:, g, :], banks[0][:, 0:256])
    ev = 0
    for g in range(4):
        for n2p in range(32):
            bk = banks[n2p % 2]
            for k in range(2):
                n2 = n2p * 2 + k
                p.mm(bk[:, k * 256:(k + 1) * 256], fT[:, g, n2:8192:64], Mg[:, g, :])
            p.copy(DVE if n2p % 2 == 0 else ACT, AB[:, n2p * 2:n2p * 2 + 2, :].re("q a b -> q (a b)"), bk)
        ofsg = ofs[g % 2]
        for half in range(2):
            P1h = P1[half]
            for d4 in range(32):
                bk = banks[2 + d4 % 2]
                for k in range(4):
                    d = d4 * 4 + k
                    p.mm(bk[:, k * 128:(k + 1) * 128], AB[:, :, d:256:128].re("q a b -> q (a b)"),
                         dft1[:, half * 128:(half + 1) * 128])
                p.copy(DVE if d4 % 2 == 0 else ACT, P1h[:, d4 * 4:d4 * 4 + 4, :].re("q a b -> q (a b)"), bk)
            for k8 in range(8):
                bk = banks[4 + k8 % 2]
                for kk in range(8):
                    k1h = k8 * 8 + kk
                    k1 = half * 64 + k1h
                    for cs in range(2):
                        p.mm(bk[:, kk * NSLOT:(kk + 1) * NSLOT], P1h[:, :, cs * 64 + k1h],
                             gtab[:, cs, k1, :], start=(cs == 0), stop=(cs == 1))
                k1b = half * 64 + k8 * 8
                src = bk[:, 0:8 * NSLOT].re("q (a b) -> q b a", a=8)
                p.copy(DVE if k8 % 2 == 0 else POOL if False else ACT, ofsg[:, :, k1b:k1b + 8], src)
        p.dma(ofd[g], ofsg.v().re("q a b -> q (a b)"))
        if debug:
            p.dma(V(dbg["d_of"], dbg["d_of"].ap[g]), ofsg.v().re("q a b -> q (a b)"))
    _barrier(p)
    p.arena_off = mk1 - 0
    p.arena_off = mk0

    if upto == 2:
        p.final_wait(SP, outs + list(dbg.values()))
        p.emit()
        return nc

    kT = alloc("kT", [128, 4, NKV * 128], BF16)
    qT = alloc("qT", [128, 4, 36 * 128], BF16)
    Vx = alloc("Vx", [128, NKV, 8, 66], BF16)
    kcT = alloc("kcT", [128, 4, 256], BF16)
    Vc = alloc("Vc", [128, 2, 8, 66], BF16)
    mk3 = p.arena_off
    work = mk_work(5, [6, 7])
    wq_s = [alloc("wq_s%d" % i, [128, 512], F32) for i in range(3)]
    wqkv = alloc("wqkv", [128, 8, 1536], BF16)
    cc = 0
    for c in range(8):
        for j in range(3):
            stg = wq_s[cc % 3]
            p.dma(stg, V(win, win_t.ap()[c * 128:(c + 1) * 128, 512 + j * 512:1024 + j * 512]))
            p.copy((DVE, POOL, ACT)[cc % 3], wqkv[:, c, j * 512:(j + 1) * 512], stg)
            cc += 1
    p.memset(POOL, Vx[:, :, :, 64:66], 1.0)
    p.memset(POOL, Vc[:, :, :, 64:66], 1.0)
    uTs = [alloc("uT%d" % i, [128, 8, 512], BF16) for i in range(2)]
    sqb = [alloc("sqb%d" % i, [128, 512], BF16) for i in range(2)]
    rqk = [alloc("rqk%d" % i, [128, 512], F32) for i in range(2)]
    groups = [(i * 4, 4) for i in range(9)] + [(36, 2)]
    gi = 0

    def qk_proj(uT, ncol, woff, gcol, dstT, dcol):
        nonlocal gi
        for hp in range(4):
            bk = banks[hp % 2]
            for c in range(8):
                p.mm(bk[:, 0:ncol], wqkv[:, c, woff + hp * 128:woff + (hp + 1) * 128], uT[:, c, 0:ncol],
                     start=(c == 0), stop=(c == 7))
            sq = sqb[gi % 2]; rq = rqk[gi % 2]; gi += 1
            p.act(sq[:, 0:ncol], bk[:, 0:ncol], AF.Square)
            b2 = banks[2 + hp % 2]
            p.mm(b2[:, 0:ncol], blk, sq[:, 0:ncol])
            p.act(rq[:, 0:ncol], b2[:, 0:ncol], AF.Sqrt, bias=epsb, scale=1.0 / 64)
            p.recip(rq[:, 0:ncol], rq[:, 0:ncol])
            p.stt(DVE, dstT[:, hp, dcol:dcol + ncol], bk[:, 0:ncol], qk_g[:, gcol:gcol + 1], rq[:, 0:ncol],
                  ALU.mult, ALU.mult)

    def v_proj(uT, nt, dstV, slot0):
        for t in range(nt):
            bk = banks[4 + t % 2]
            for c in range(8):
                p.mm(bk, uT[:, c, t * 128:(t + 1) * 128], wqkv[:, c, 1024:1536], start=(c == 0), stop=(c == 7))
            p.copy(ACT if t % 2 == 0 else DVE, dstV[:, slot0 + t, :, 0:64], bk.v().re("q (h e) -> q h e", h=8))

    for gidx, (t0, nt) in enumerate(groups):
        uT = uTs[gidx % 2]
        svs = [V(xkv, xkv_t.ap()[(t0 + t) * 128:(t0 + t + 1) * 128, :]) for t in range(nt)]
        front_end(svs, avec[:, 0, :], modT[:, 0, 0:8], uT, work)
        ncol = nt * 128
        if t0 < 36:
            qk_proj(uT, ncol, 0, 0, qT, t0 * 128)
        qk_proj(uT, ncol, 512, 1, kT, t0 * 128)
        v_proj(uT, nt, Vx, t0)
    uT = uTs[0]
    svs = [V(ctxb, ctx_t.ap()[t * 128:(t + 1) * 128, :]) for t in range(2)]
    front_end(svs, avec[:, 1, :], modT[:, 1, 0:8], uT, work)
    qk_proj(uT, 256, 512, 1, kcT, 0)
    v_proj(uT, 2, Vc, 0)
    if debug:
        p.dma(dbg["d_kT"], kT)
    _barrier(p)
    p.arena_off = mk3

    if upto == 3:
        p.final_wait(SP, outs + list(dbg.values()))
        p.emit()
        return nc

    TE = alloc("TE", [128, 7, 8, 128], BF16)
    vm = alloc("vm", [128, NSLOT * 14], F32)
    cm = alloc("cm", [128, 128], F32)
    wo_b = alloc("wo_b", [128, 8, 1024], BF16)
    g1bc = alloc("g1bc", [128, 1024], F32)
    PT = [alloc("PT%d" % i, [128, 8, 8, 128], BF16) for i in range(1)]
    eS = [alloc("eS%d" % i, [128, 8, 128], BF16) for i in range(2)]
    xts = [alloc("xq%d" % i, [128, 1024], F32) for i in range(2)]
    hms = [alloc("hm%d" % i, [128, 1024], F32) for i in range(2)]
    qm = [alloc("qm%d" % i, [128, 4, 2, 128], BF16) for i in range(2)]
    ofT = [alloc("ofT%d" % i, [128, 4, 128], BF16) for i in range(2)]
    otok = alloc("otok", [128, 8, 64], BF16)
    onaT = alloc("onaT", [128, 4, 128], BF16)
    rden = alloc("rden", [128, 8], F32)
    p.dma(vm, vmd)
    p.dma(cm, cmask)
    p.dma(g1bc, V(modrow, modrow_t.ap()[0, 0, 2048:3072].partition_broadcast(128)))
    for c in range(8):
        stg = hms[c % 2]
        p.dma(stg, V(wout, wout_t.ap()[c * 128:(c + 1) * 128, :]))
        p.copy((DVE, POOL)[c % 2], wo_b[:, c, :], stg)
    for dl in range(7):
        stg = hms[dl % 2].v().re("q (a b) -> q a b", a=8)
        p.dma(stg, V(biasT, biasT_t.ap()[:, dl]))
        p.act(stg, stg, AF.Exp)
        p.tt(DVE, TE[:, dl], stg, V(cm, cm.ap.unsqueeze(1).broadcast_to([128, 8, 128])), ALU.mult)
    for qq in qm:
        p.memset(POOL, qq, 0.0)
    ei = 0
    for s in range(NSLOT):
        if s <= 1:
            dls = [-2, -1, 0, 1, 2, 3]
        elif s >= NSLOT - 2:
            dls = [-3, -2, -1, 0, 1, 2]
        else:
            dls = [-2, -1, 0, 1, 2]
        chunks = [("w", dl) for dl in dls] + [("c", 0), ("c", 1)]
        pt = PT[0]
        qs = s + 3
        xq = xts[s % 2]; hm = hms[s % 2]; of_t = ofT[s % 2]
        qms = qm[s % 2]
        p.copy(POOL, qms[0:64, :, 0, :], qT[0:64, :, qs * 128:(qs + 1) * 128])
        p.copy(POOL, qms[64:128, :, 1, :], qT[64:128, :, qs * 128:(qs + 1) * 128])
        p.dma(xq, V(xkv, xkv_t.ap()[qs * 128:(qs + 1) * 128, :]))
        for g in range(4):
            p.dma(of_t[:, g, :], V(ofd[g], ofd[g].ap[:, s * 128:(s + 1) * 128]))
        for j, (kind, dl) in enumerate(chunks):
            pair = (ei % 2) * 2
            e_s = eS[ei % 2]; ei += 1
            for h in range(8):
                hp, hh = h // 2, h % 2
                bk = banks[pair + h // 4]
                if kind == "w":
                    ks = s + 3 + dl
                    lhs = kT[:, hp, ks * 128:(ks + 1) * 128]
                else:
                    lhs = kcT[:, hp, dl * 128:(dl + 1) * 128]
                p.mm(bk[:, (h % 4) * 128:(h % 4 + 1) * 128], lhs, qms[:, hp, hh, :])
            for hb in range(2):
                dst = e_s if kind == "w" else pt[:, j]
                p.act(dst[:, hb * 4:(hb + 1) * 4, :], banks[pair + hb].v().re("q (h e) -> q h e", h=4),
                      AF.Exp, scale=0.125)
            if kind == "w":
                for qr in range(2):
                    col = s * 14 + (dl + 3) * 2 + qr
                    p.stt(DVE, pt[:, j, :, qr * 64:(qr + 1) * 64],
                          e_s[:, :, qr * 64:(qr + 1) * 64], vm[:, col:col + 1],
                          TE[:, dl + 3, :, qr * 64:(qr + 1) * 64], ALU.mult, ALU.mult)
        nch = len(chunks)
        for h in range(8):
            bk = banks[4 + h // 4]
            for j, (kind, dl) in enumerate(chunks):
                if kind == "w":
                    rhs = Vx[:, s + 3 + dl, h, :]
                else:
                    rhs = Vc[:, dl, h, :]
                p.mm(bk[:, (h % 4) * 66:(h % 4 + 1) * 66], pt[:, j, h, :], rhs, start=(j == 0), stop=(j == nch - 1))
        for hb in range(2):
            bv = banks[4 + hb][:, 0:264].re("q (h e) -> q h e", h=4)
            p.recip(rden[:, hb * 4:(hb + 1) * 4], bv[:, :, 64])
            p.tt(DVE, otok[:, hb * 4:(hb + 1) * 4, :], bv[:, :, 0:64],
                 V(rden, rden.ap[:, hb * 4:(hb + 1) * 4].unsqueeze(2).broadcast_to([128, 4, 64])), ALU.mult)
        if debug:
            p.dma(V(dbg["d_ona"], dbg["d_ona"].ap[s * 128:(s + 1) * 128, :]), otok.v().re("q h e -> q (h e)"))
        ptr_v = bank_bf(6)
        for c in range(4):
            p.tr(ptr_v[:, c * 128:(c + 1) * 128], otok.v().re("q h e -> q (h e)")[:, c * 128:(c + 1) * 128], ident)
        p.copy(ACT, onaT.v().re("q a b -> q (a b)"), ptr_v[:, 0:512])
        for nh in range(2):
            bk = banks[6 + nh]
            for c in range(8):
                lhs = of_t[:, c, :] if c < 4 else onaT[:, c - 4, :]
                p.mm(bk, lhs, wo_b[:, c, nh * 512:(nh + 1) * 512], start=(c == 0), stop=(c == 7))
            p.tt(DVE, hm[:, nh * 512:(nh + 1) * 512], bk, g1bc[:, nh * 512:(nh + 1) * 512], ALU.mult)
        p.tt(POOL, hm, hm, xq, ALU.add)
        p.dma(hmid[s], hm)
        if debug:
            p.dma(V(dbg["d_hmid"], dbg["d_hmid"].ap[s * 128:(s + 1) * 128, :]), hm)
    _barrier(p)
    p.arena_off = mk0

    if upto == 4:
        p.final_wait(SP, outs + list(dbg.values()))
        p.emit()
        return nc

    def mlp_phase(l, src_tiles, dst_tiles, a_idx, mod_i, dbg_out=None):
        mk = p.arena_off
        w1b = alloc("w1b", [128, 8, 4096], BF16)
        w2b = alloc("w2b", [128, 32, 1024], BF16)
        aT = alloc("aT", [128, 32, 384], BF16)
        aT_flat = aT.ap.rearrange("q a b -> q (a b)").bitcast(F32)
        stg = [Buf("stg%d" % i, aT_flat[:, i * 2048:(i + 1) * 2048]) for i in range(3)]
        g2bc = alloc("g2bc", [128, 1024], F32)
        rl = [alloc("rl%d" % i, [128, 384], BF16) for i in range(2)]
        uTm = [alloc("uTm%d" % i, [128, 8, 384], BF16) for i in range(2)]
        outb = [alloc("outb%d" % i, [128, 1024], F32) for i in range(2)]
        work = mk_work(4, [6, 7])
        p.dma(g2bc, V(modrow, modrow_t.ap()[l, 0, 5120:6144].partition_broadcast(128)))
        cc = 0
        for c in range(8):
            for hh in range(2):
                s_ = stg[cc % 3]
                p.dma(s_, V(w1d, w1_t.ap()[l, c * 128:(c + 1) * 128, hh * 2048:(hh + 1) * 2048]))
                p.copy((DVE, POOL, ACT)[cc % 3], w1b[:, c, hh * 2048:(hh + 1) * 2048], s_)
                cc += 1
        for j2 in range(16):
            s_ = stg[cc % 3]
            p.dma(s_.v().re("q (a b) -> q a b", a=2),
                  V(w2d, w2_t.ap()[l, j2 * 256:(j2 + 1) * 256, :].rearrange("(a q) n -> q a n", q=128)))
            p.copy((DVE, POOL, ACT)[cc % 3], w2b[:, j2 * 2:j2 * 2 + 2, :].re("q a b -> q (a b)"), s_)
            cc += 1
        for s_ in stg:
            aT.readers.update(s_.readers)
            aT.writers.update(s_.writers)
        ntile = len(src_tiles)
        for t0 in range(0, ntile, 3):
            uT = uTm[(t0 // 3) % 2]
            svs = [src_tiles[t0 + t] for t in range(3)]
            xt3, _ = front_end(svs, avec[:, a_idx, :], modT[:, mod_i, 24:32], uT, work)
            for j in range(32):
                bk = banks[j % 4]
                for c in range(8):
                    p.mm(bk[:, 0:384], w1b[:, c, j * 128:(j + 1) * 128], uT[:, c, :], start=(c == 0), stop=(c == 7))
                r = rl[j % 2]
                p.act(r, bk[:, 0:384], AF.Relu)
                p.tt(POOL if j % 2 == 0 else DVE, aT[:, j, :], r, r, ALU.mult)
            for t in range(3):
                ob = outb[(t0 + t) % 2]
                for nh in range(2):
                    bk = banks[4 + nh]
                    for j in range(32):
                        p.mm(bk, aT[:, j, t * 128:(t + 1) * 128], w2b[:, j, nh * 512:(nh + 1) * 512],
                             start=(j == 0), stop=(j == 31))
                    p.tt(DVE, ob[:, nh * 512:(nh + 1) * 512], bk, g2bc[:, nh * 512:(nh + 1) * 512], ALU.mult)
                p.tt(POOL, ob, ob, xt3[t], ALU.add)
                p.dma(dst_tiles[t0 + t], ob)
                if dbg_out is not None:
                    p.dma(V(dbg_out, dbg_out.ap[(t0 + t) * 128:(t0 + t + 1) * 128, :]), ob)
        _barrier(p)
        p.arena_off = mk

    mlp_phase(0, [h.v() for h in hmid], h1, 2, 0, dbg.get("d_h1"))

    if upto == 5:
        p.final_wait(SP, outs + list(dbg.values()))
        p.emit()
        return nc

    mk5 = p.arena_off
    utok = [alloc("utok%d" % s, [128, 1024], BF16) for s in range(NSLOT)]
    a1bc = alloc("a1bc", [128, 1024], F32)
    b1bc = alloc("b1bc", [128, 1024], F32)
    gpbc = alloc("gpbc", [128, 1024], F32)
    tmp1 = alloc("tmp1", [128, 1024], F32)
    pm = alloc("pm", [128, 4, 3, 128], BF16)
    pmf = alloc("pmf", [128, 4, 128], BF16)
    pml = alloc("pml", [128, 4, 128], BF16)
    wp_s = alloc("wp_s", [128, 8, 256], F32)
    wp_b = alloc("wp_b", [128, 8, 256], BF16)
    junk = [alloc("junk1_%d" % i, [128, 1024], BF16) for i in range(3)]
    ssl = alloc("ssl", [128, NSLOT], F32)
    rsl = alloc("rsl", [128, NSLOT], F32)
    x1 = [alloc("x1_%d" % i, [128, 1024], F32) for i in range(4)]
    tf = [alloc("tf%d" % i, [128, 1024], F32) for i in range(2)]
    pooledT = [alloc("pooledT%d" % i, [128, 8, 128], BF16) for i in range(2)]
    hm1 = [alloc("hm1_%d" % i, [128, 1024], F32) for i in range(2)]
    ty1 = [alloc("ty1_%d" % i, [128, 1024], F32) for i in range(2)]
    p.dma(pm, pmd); p.dma(pmf, pmfd); p.dma(pml, pmld)
    p.dma(wp_s, V(wpool, wpool_t.ap().rearrange("g (k q) n -> q (g k) n", q=128)))
    p.copy(DVE, wp_b, wp_s)
    p.dma(a1bc, V(modrow, modrow_t.ap()[1, 0, 1024:2048].partition_broadcast(128)))
    p.dma(tmp1, V(n1g, n1g_t.ap()[1, :].partition_broadcast(128)))
    p.stt(DVE, a1bc, a1bc, 1.0, tmp1, ALU.add, ALU.mult)
    p.dma(b1bc, V(modrow, modrow_t.ap()[1, 0, 0:1024].partition_broadcast(128)))
    p.dma(gpbc, V(modrow, modrow_t.ap()[1, 0, 2048:3072].partition_broadcast(128)))
    tmp2 = alloc("tmp2", [128, 1024], F32)
    p.dma(tmp2, V(psc, psc_t.ap()[0, :].partition_broadcast(128)))
    p.tt(DVE, gpbc, gpbc, tmp2, ALU.mult)
    for s in range(NSLOT):
        xt = x1[s % 4]
        p.dma(xt, h1[s])
        p.act(junk[s % 3], xt, AF.Square, accum_out=ssl[:, s:s + 1])
        p.act(rsl[:, s:s + 1], ssl[:, s:s + 1], AF.Sqrt, bias=epsb, scale=1.0 / 1024)
        p.recip(rsl[:, s:s + 1], rsl[:, s:s + 1])
        t_ = tf[s % 2]
        p.stt(DVE, t_, xt, rsl[:, s:s + 1], a1bc, ALU.mult, ALU.mult)
        p.tt(POOL, utok[s], t_, b1bc, ALU.add)
    for s in range(NSLOT):
        pT = pooledT[s % 2]
        for c in range(8):
            w = c // 2
            bk = banks[(s % 2) * 2 + c // 4]
            dst = bk[:, (c % 4) * 128:(c % 4 + 1) * 128]
            parts = []
            if s > 0:
                parts.append((utok[s - 1], pm[:, w, 0, :]))
            if s == 0:
                parts.append((utok[s], pmf[:, w, :]))
            elif s == NSLOT - 1:
                parts.append((utok[s], pml[:, w, :]))
            else:
                parts.append((utok[s], pm[:, w, 1, :]))
            if s < NSLOT - 1:
                parts.append((utok[s + 1], pm[:, w, 2, :]))
            for i, (ut, mat) in enumerate(parts):
                p.mm(dst, ut[:, c * 128:(c + 1) * 128], mat, start=(i == 0), stop=(i == len(parts) - 1))
        for hb in range(2):
            p.copy(ACT if hb == 0 else DVE, pT[:, hb * 4:(hb + 1) * 4, :].re("q a b -> q (a b)"),
                   banks[(s % 2) * 2 + hb])
        for nh in range(2):
            bk = banks[4 + (s % 2) * 2 + nh]
            for gg in range(2):
                g = nh * 2 + gg
                for kk in range(2):
                    p.mm(bk[:, gg * 256:(gg + 1) * 256], pT[:, 2 * g + kk, :], wp_b[:, 2 * g + kk, :],
                         start=(kk == 0), stop=(kk == 1))
            p.tt(DVE, ty1[s % 2][:, nh * 512:(nh + 1) * 512], bk, gpbc[:, nh * 512:(nh + 1) * 512], ALU.mult)
        xt = x1[s % 4]
        p.dma(xt, h1[s])
        p.tt(POOL, hm1[s % 2], ty1[s % 2], xt, ALU.add)
        p.dma(hmid[s], hm1[s % 2])
        if debug:
            p.dma(V(dbg["d_hmid1"], dbg["d_hmid1"].ap[s * 128:(s + 1) * 128, :]), hm1[s % 2])
    _barrier(p)
    p.arena_off = mk5

    if upto == 6:
        p.final_wait(SP, outs + list(dbg.values()))
        p.emit()
        return nc

    mlp_phase(1, [h.v() for h in hmid], outs, 4, 2, None)

    p.final_wait(SP, outs + list(dbg.values()))
    p.emit()
    return nc


import ml_dtypes
from concourse.bass_utils import run_bass_kernel_spmd

_BF = ml_dtypes.bfloat16
_PROG_CACHE = {}


def _const_tables():
    t = {}
    t["ident"] = np.eye(128, dtype=np.float32).astype(_BF)
    blk = np.zeros((128, 128), np.float32)
    blk[:64, :64] = 1.0
    blk[64:, 64:] = 1.0
    t["blk"] = blk.astype(_BF)
    i = np.arange(128, dtype=np.float64)
    ang = 2 * np.pi * ((i[:, None] * i[None, :]) % 128) / 128.0
    t["dftc"] = np.concatenate([np.cos(ang) / 1024.0, np.sin(ang) / 1024.0], axis=1).astype(np.float32).astype(_BF)
    d1 = np.zeros((128, 2, 2, 64), np.float64)
    for half in range(2):
        d1[:, half, 0, :] = np.cos(ang[:, half * 64:(half + 1) * 64])
        d1[:, half, 1, :] = np.sin(ang[:, half * 64:(half + 1) * 64])
    t["dft1"] = d1.reshape(128, 256).astype(np.float32).astype(_BF)
    qc = np.arange(64)
    cs = np.clip(qc - 8, 0, 48)
    kc = np.arange(64)
    m = ((kc[:, None] >= cs[None, :]) & (kc[:, None] < cs[None, :] + 16)).astype(np.float32)
    t["cmask"] = np.tile(m, (2, 2)).astype(np.float32)
    pm = np.zeros((128, 4, 3, 128), np.float64)
    pmf = np.zeros((128, 4, 128), np.float64)
    pml = np.zeros((128, 4, 128), np.float64)
    tt = np.arange(128)
    for wi, w in enumerate(POOL_SIZES):
        for tq in range(128):
            lo, hi = tq - w // 2, tq + w - w // 2
            for tp in range(lo, hi):
                if tp < 0:
                    pm[tp + 128, wi, 0, tq] += 1.0 / w
                elif tp >= 128:
                    pm[tp - 128, wi, 2, tq] += 1.0 / w
                else:
                    pm[tp, wi, 1, tq] += 1.0 / w
            pm[tq, wi, 1, tq] -= 1.0
            lo_c = max(lo, 0)
            cnt = hi - lo_c
            for tp in range(lo_c, min(hi, 128)):
                pmf[tp, wi, tq] += 1.0 / cnt
            pmf[tq, wi, tq] -= 1.0
            hi_c = min(hi, 128)
            cnt = hi_c - lo
            for tp in range(max(lo, 0), hi_c):
                pml[tp, wi, tq] += 1.0 / cnt
            pml[tq, wi, tq] -= 1.0
    t["pm"] = pm.astype(np.float32).astype(_BF)
    t["pmf_first"] = pmf.astype(np.float32).astype(_BF)
    t["pml_last"] = pml.astype(np.float32).astype(_BF)
    t["pm_main"] = np.ascontiguousarray(pm[:, :, 1, :]).astype(np.float32).astype(_BF)
    return t


def _core_tables(hf):
    qb = 0 if hf == 0 else 31
    n2 = np.arange(64, dtype=np.int64)
    k1 = np.arange(128, dtype=np.int64)
    k2 = qb + np.arange(NSLOT, dtype=np.int64)
    k = k1[:, None] + 128 * k2[None, :]
    beta = 2 * np.pi * ((n2[:, None, None] * k[None]) % 8192) / 8192.0
    cb, sb = np.cos(beta), np.sin(beta)
    g = np.zeros((64, 2, 2, 128, NSLOT), np.float64)
    g[:, 0, 0] = cb
    g[:, 1, 0] = -sb
    g[:, 0, 1] = -sb
    g[:, 1, 1] = -cb
    gtab = g.reshape(128, 2, 128, NSLOT).astype(np.float32).astype(_BF)
    vm = np.zeros((128, NSLOT, 7, 2), np.float32)
    for s in range(NSLOT):
        gt = qb + s
        for di, dl in enumerate(range(-3, 4)):
            for qr in range(2):
                qrow = 2 * gt + qr
                rs = min(max(qrow - 4, 0), 120)
                for jr in range(2):
                    krow = 2 * (gt + dl) + jr
                    ok = (0 <= krow <= 127) and (rs <= krow < rs + 8) and (0 <= qrow <= 127)
                    vm[jr * 64:(jr + 1) * 64, s, di, qr] = 1.0 if ok else 0.0
    return gtab, vm.reshape(128, NSLOT * 14)


def _chunks(v):
    return np.ascontiguousarray(np.asarray(v, np.float32).reshape(8, 128).T)


def _prepare_inputs(inputs):
    f32 = lambda a: np.ascontiguousarray(np.asarray(a, dtype=np.float32))
    x = f32(inputs["x"]); c = f32(inputs["c"]); ctx = f32(inputs["ctx"]); c_ctx = f32(inputs["c_ctx"])
    rpb = f32(inputs["rpb"])[0]
    ct = _const_tables()
    jr = np.arange(128) // 64
    kc = np.arange(128) % 64
    qr = np.arange(128) // 64
    qc = np.arange(128) % 64
    dl = np.arange(-3, 4)
    dr = np.clip(2 * dl[None, :, None] + jr[:, None, None] - qr[None, None, :] + 7, 0, 14)
    dc = np.clip(kc[:, None] - qc[None, :] + 15, 0, 30)
    biasT = rpb[:, dr, dc[:, None, :]]
    biasT = np.ascontiguousarray(biasT.transpose(1, 2, 0, 3)).astype(np.float32)
    gvec = np.concatenate([_chunks(inputs["norm1_g"][0]), _chunks(inputs["norm1_g"][1]),
                           _chunks(inputs["norm2_g"][0]), _chunks(inputs["norm2_g"][1])], axis=1)
    qkg = np.stack([np.tile(f32(inputs["q_norm_g"])[0], 2), np.tile(f32(inputs["k_norm_g"])[0], 2)], axis=1)
    shared = {
        "gvec": np.ascontiguousarray(gvec), "qkg": np.ascontiguousarray(qkg.astype(np.float32)),
        "w_mod": f32(inputs["w_mod"]), "b_mod": f32(inputs["b_mod"]),
        "norm1_g": f32(inputs["norm1_g"]), "pool_scale": f32(inputs["pool_scale"]),
        "w_in": f32(inputs["w_in_even"])[0], "w_four": f32(inputs["w_four"])[0],
        "w_out": f32(inputs["w_out_even"])[0], "w_pool": f32(inputs["w_pool"])[0],
        "w_mlp1": f32(inputs["w_mlp1"]), "w_mlp2": f32(inputs["w_mlp2"]),
        "biasT": biasT, "cmask": ct["cmask"],
        "ident": ct["ident"], "blk": ct["blk"], "dftc": ct["dftc"], "dft1": ct["dft1"], "pm": ct["pm"],
    }
    per_hf = [_core_tables(0), _core_tables(1)]
    in_maps = []
    for core in range(8):
        b, hf = core // 2, core % 2
        qb = 0 if hf == 0 else 31
        xkv = np.zeros((NKV * 128, 1024), np.float32)
        lo_t, hi_t = qb - 3, qb - 3 + NKV
        a, e = max(lo_t, 0), min(hi_t, 64)
        xkv[(a - lo_t) * 128:(e - lo_t) * 128] = x[b, a * 128:e * 128]
        cvec = np.concatenate([_chunks(c[b]), _chunks(c_ctx)], axis=1)
        m = dict(shared)
        m.update({
            "xb": x[b], "xkv": xkv, "ctxb": ctx[b], "cvec": np.ascontiguousarray(cvec),
            "gtab": per_hf[hf][0], "vm": per_hf[hf][1],
            "pmf": ct["pmf_first"] if hf == 0 else ct["pm_main"],
            "pml": ct["pm_main"] if hf == 0 else ct["pml_last"],
        })
        in_maps.append(m)
    return in_maps


def kernel(**inputs):
    if "prog" not in _PROG_CACHE:
        _PROG_CACHE["prog"] = build_program(debug=False)
    nc = _PROG_CACHE["prog"]
    in_maps = _prepare_inputs(inputs)
    res = run_bass_kernel_spmd(nc, in_maps, core_ids=list(range(8)))
    out = np.empty((4, 8192, 1024), np.float32)
    for core in range(8):
        b, hf = core // 2, core % 2
        o = res.results[core]["out33"]
        if hf == 0:
            out[b, 0:4096] = o[0:4096]
        else:
            out[b, 4096:8192] = o[128:4224]
    return out
```

```python
import numpy as np
import concourse.bass as bass
import concourse.mybir as mybir
from contextlib import ExitStack

F32 = mybir.dt.float32
BF16 = mybir.dt.bfloat16
AF = mybir.ActivationFunctionType
ALU = mybir.AluOpType
AX = mybir.AxisListType

PE, ACT, DVE, POOL, SP = "pe", "act", "dve", "pool", "sp"
COMPUTE = (PE, ACT, DVE, POOL)
N_DMA_SEMS = 24


class Buf:
    __slots__ = ("name", "ap", "writers", "readers")

    def __init__(self, name, ap):
        self.name = name
        self.ap = ap
        self.writers = {}
        self.readers = {}

    def __getitem__(self, idx):
        return V(self, self.ap[idx])

    def v(self):
        return V(self, self.ap)


class V:
    __slots__ = ("buf", "ap")

    def __init__(self, buf, ap):
        self.buf = buf
        self.ap = ap

    def __getitem__(self, idx):
        return V(self.buf, self.ap[idx])

    def re(self, pattern, **kw):
        return V(self.buf, self.ap.rearrange(pattern, **kw))


def _ap(x):
    return x.ap if isinstance(x, (V,)) else (x.ap if isinstance(x, Buf) else x)


def _buf(x):
    if isinstance(x, V):
        return x.buf
    if isinstance(x, Buf):
        return x
    return None


class Prog:
    def __init__(self, nc):
        self.nc = nc
        self.stack = ExitStack()
        self.ops = {e: [] for e in (PE, ACT, DVE, POOL, SP)}
        self.know = {e: {} for e in (PE, ACT, DVE, POOL, SP)}
        self.sems = {}
        for e in COMPUTE:
            self.sems[e] = self.stack.enter_context(nc.semaphore("c_" + e))
        self.dsems = [self.stack.enter_context(nc.semaphore("d%d" % i)) for i in range(N_DMA_SEMS)]
        self.dcount = [0] * N_DMA_SEMS
        self.dlast_issuer = [None] * N_DMA_SEMS
        self.drr = 0
        self.n_alloc = 0
        self.out_tokens = []

    def sbuf(self, name, shape, dtype):
        t = self.stack.enter_context(self.nc.sbuf_tensor(name, list(shape), dtype))
        return t

    def psum(self, name, shape, dtype=F32):
        t = self.stack.enter_context(self.nc.psum_tensor(name, list(shape), dtype))
        return t

    def sbuf_buf(self, name, shape, dtype):
        t = self.sbuf(name, shape, dtype)
        return Buf(name, t[tuple(slice(None) for _ in shape)])

    def psum_buf(self, name, shape, dtype=F32):
        t = self.psum(name, shape, dtype)
        return Buf(name, t[tuple(slice(None) for _ in shape)])

    def dram(self, name, shape, dtype, kind="Internal"):
        t = self.nc.dram_tensor(name, list(shape), dtype, kind=kind)
        return t

    def _deps(self, eng, reads, writes):
        toks = []
        for r in reads:
            b = _buf(r)
            if b is None:
                continue
            for k, t in b.writers.items():
                toks.append((t, "raw"))
        for w in writes:
            b = _buf(w)
            if b is None:
                continue
            for k, t in b.writers.items():
                toks.append((t, "waw"))
            for k, t in b.readers.items():
                toks.append((t, "war"))
        need = {}
        for t, kind in toks:
            if t[0] == "c":
                src = t[1]
                if src == eng:
                    if eng == PE:
                        continue
                key = ("c", src)
            else:
                key = ("d", t[1])
            if need.get(key, -1) < t[2]:
                need[key] = t[2]
        waits = []
        kn = self.know[eng]
        for key, val in need.items():
            if kn.get(key, -1) >= val:
                continue
            kn[key] = val
            waits.append((key, val))
        return waits

    def _mark(self, tok, reads, writes):
        for r in reads:
            b = _buf(r)
            if b is None:
                continue
            k = tok[:2]
            b.readers[k] = tok
        for w in writes:
            b = _buf(w)
            if b is None:
                continue
            b.writers = {tok[:2]: tok}
            b.readers = {}

    def op(self, eng, fn, reads=(), writes=()):
        waits = self._deps(eng, reads, writes)
        idx = len(self.ops[eng])
        rec = dict(fn=fn, waits=waits, signal=False, dma=None)
        self.ops[eng].append(rec)
        for key, val in waits:
            if key[0] == "c":
                self.ops[key[1]][val]["signal"] = True
        tok = ("c", eng, idx)
        self._mark(tok, reads, writes)
        return tok

    def dma(self, out, in_, queue=SP, **kw):
        reads, writes = [in_], [out]
        waits = self._deps(queue, reads, writes)
        s = self.drr
        self.drr = (self.drr + 1) % N_DMA_SEMS
        prev = self.dcount[s]
        if prev > 0:
            kn = self.know[queue]
            if kn.get(("d", s), -1) < prev:
                kn[("d", s)] = prev
                waits.append((("d", s), prev))
        self.dcount[s] = prev + 1
        o, i = _ap(out), _ap(in_)
        rec = dict(fn=lambda e: e.dma_start(out=o, in_=i, **kw), waits=waits, signal=False,
                   dma=(s, prev + 1))
        for key, val in waits:
            if key[0] == "c":
                self.ops[key[1]][val]["signal"] = True
        self.ops[queue].append(rec)
        tok = ("d", s, prev + 1)
        self._mark(tok, reads, writes)
        return tok

    def final_wait(self, eng, bufs):
        waits = self._deps(eng, bufs, [])
        rec = dict(fn=None, waits=waits, signal=False, dma=None)
        for key, val in waits:
            if key[0] == "c":
                self.ops[key[1]][val]["signal"] = True
        self.ops[eng].append(rec)

    def emit(self):
        nc = self.nc
        sigord = {}
        for e in COMPUTE:
            c = 0
            for i, rec in enumerate(self.ops[e]):
                if rec["signal"]:
                    c += 1
                    sigord[(e, i)] = c
        engobj = {PE: "tensor", ACT: "scalar", DVE: "vector", POOL: "gpsimd", SP: "sync"}

        def run(engname):
            def body(eng):
                for i, rec in enumerate(self.ops[engname]):
                    for key, val in rec["waits"]:
                        if key[0] == "c":
                            eng.wait_ge(self.sems[key[1]], sigord[(key[1], val)])
                        else:
                            eng.wait_ge(self.dsems[key[1]], 16 * val)
                    if rec["fn"] is None:
                        continue
                    ins = rec["fn"](eng)
                    if rec["dma"] is not None:
                        ins.then_inc(self.dsems[rec["dma"][0]], 16)
                    elif rec["signal"]:
                        ins.then_inc(self.sems[engname], 1)
            return body

        with nc.Block() as block:
            block.tensor(run(PE))
            block.scalar(run(ACT))
            block.vector(run(DVE))
            block.gpsimd(run(POOL))
            block.sync(run(SP))
        self.stack.close()

    def mm(self, out, lhsT, rhs, start=True, stop=True, **kw):
        o, l, r = _ap(out), _ap(lhsT), _ap(rhs)
        return self.op(PE, lambda e: e.matmul(o, l, r, start=start, stop=stop, **kw),
                       reads=[lhsT, rhs], writes=[out])

    def tr(self, out, in_, ident):
        o, i, d = _ap(out), _ap(in_), _ap(ident)
        return self.op(PE, lambda e: e.transpose(o, i, d), reads=[in_, ident], writes=[out])

    def act(self, out, in_, func, bias=None, scale=None, accum_out=None, eng=ACT):
        o, i = _ap(out), _ap(in_)
        kw = {}
        reads = [in_]
        writes = [out]
        if bias is not None:
            kw["bias"] = _ap(bias)
            reads.append(bias)
        if scale is not None:
            kw["scale"] = _ap(scale)
            reads.append(scale)
        if accum_out is not None:
            kw["accum_out"] = _ap(accum_out)
            writes.append(accum_out)
        return self.op(ACT, lambda e: e.activation(o, i, func, **kw), reads=reads, writes=writes)

    def ts(self, eng, out, in0, s1, s2, op0, op1=None, accum_out=None):
        o, i = _ap(out), _ap(in0)
        a1, a2 = _ap(s1), _ap(s2)
        reads = [in0, s1, s2]
        writes = [out]
        kw = {}
        if op1 is not None:
            kw["op1"] = op1
        if accum_out is not None:
            kw["accum_out"] = _ap(accum_out)
            writes.append(accum_out)
        return self.op(eng, lambda e: e.tensor_scalar(o, i, a1, a2, op0, **kw), reads=reads, writes=writes)

    def tt(self, eng, out, in0, in1, op):
        o, a, b = _ap(out), _ap(in0), _ap(in1)
        return self.op(eng, lambda e: e.tensor_tensor(o, a, b, op), reads=[in0, in1], writes=[out])

    def stt(self, eng, out, in0, scalar, in1, op0, op1):
        o, a, s, b = _ap(out), _ap(in0), _ap(scalar), _ap(in1)
        return self.op(eng, lambda e: e.scalar_tensor_tensor(o, a, s, b, op0, op1),
                       reads=[in0, scalar, in1], writes=[out])

    def copy(self, eng, out, in_):
        o, i = _ap(out), _ap(in_)
        if eng == ACT:
            return self.op(ACT, lambda e: e.copy(o, i), reads=[in_], writes=[out])
        return self.op(eng, lambda e: e.tensor_copy(o, i), reads=[in_], writes=[out])

    def memset(self, eng, out, val):
        o = _ap(out)
        return self.op(eng, lambda e: e.memset(o, val), reads=[], writes=[out])

    def recip(self, out, in_):
        o, i = _ap(out), _ap(in_)
        return self.op(DVE, lambda e: e.reciprocal(o, i), reads=[in_], writes=[out])


ARENA_BYTES = 207 * 1024
NSLOT = 33
NKV = 38
POOL_SIZES = (2, 4, 8, 16)


def _arena_setup(p):
    p.arena_t = p.sbuf("arena", [128, ARENA_BYTES // 4], F32)
    p.arena_off = 0

    def alloc(name, shape, dtype):
        esz = 4 if dtype == F32 else 2
        n = 1
        for s in shape[1:]:
            n *= s
        nbytes = (n * esz + 63) // 64 * 64
        off = p.arena_off
        assert off + nbytes <= ARENA_BYTES, ("arena overflow", name, off, nbytes)
        p.arena_off = off + nbytes
        p.arena_peak = max(getattr(p, "arena_peak", 0), p.arena_off)
        ap = p.arena_t[0:shape[0], off // 4:(off + nbytes) // 4]
        if dtype != F32:
            ap = ap.bitcast(dtype)
        ap = ap[:, 0:n]
        if len(shape) == 3:
            ap = ap.rearrange("p (a b) -> p a b", a=shape[1])
        elif len(shape) == 4:
            ap = ap.rearrange("p (a b c) -> p a b c", a=shape[1], b=shape[2])
        elif len(shape) == 5:
            ap = ap.rearrange("p (a b c d) -> p a b c d", a=shape[1], b=shape[2], c=shape[3])
        return Buf(name, ap)

    p.alloc = alloc


def _barrier(p):
    for e in (PE, ACT, DVE, POOL, SP):
        waits = []
        kn = p.know[e]
        for src in COMPUTE:
            if src == e or not p.ops[src]:
                continue
            idx = len(p.ops[src]) - 1
            while idx >= 0 and p.ops[src][idx]["fn"] is None:
                idx -= 1
            if idx < 0:
                continue
            if kn.get(("c", src), -1) < idx:
                kn[("c", src)] = idx
                waits.append((("c", src), idx))
                p.ops[src][idx]["signal"] = True
        for s in range(N_DMA_SEMS):
            if p.dcount[s] > 0 and kn.get(("d", s), -1) < p.dcount[s]:
                kn[("d", s)] = p.dcount[s]
                waits.append((("d", s), p.dcount[s]))
        p.ops[e].append(dict(fn=None, waits=waits, signal=False, dma=None))


def _dbuf(t, *idx):
    return t


def build_program(debug=False, upto=99):
    nc = bass.Bass("TRN2", target_bir_lowering=False)
    p = Prog(nc)
    _arena_setup(p)
    alloc = p.alloc

    def din(name, shape, dt=F32):
        return nc.dram_tensor(name, list(shape), dt, kind="ExternalInput")

    def D(name, t):
        return Buf(name, t.ap())

    xb_t = din("xb", [8192, 1024]); xb = D("xb", xb_t)
    xkv_t = din("xkv", [NKV * 128, 1024]); xkv = D("xkv", xkv_t)
    ctx_t = din("ctxb", [256, 1024]); ctxb = D("ctxb", ctx_t)
    cvec_t = din("cvec", [128, 16]); cvec = D("cvec", cvec_t)
    gvec_t = din("gvec", [128, 32]); gvec = D("gvec", gvec_t)
    qkg_t = din("qkg", [128, 2]); qkg = D("qkg", qkg_t)
    wmod_t = din("w_mod", [2, 1024, 6144]); wmod = D("w_mod", wmod_t)
    bmod_t = din("b_mod", [2, 6144]); bmod = D("b_mod", bmod_t)
    n1g_t = din("norm1_g", [2, 1024]); n1g = D("norm1_g", n1g_t)
    psc_t = din("pool_scale", [1, 1024]); psc = D("pool_scale", psc_t)
    win_t = din("w_in", [1024, 2048]); win = D("w_in", win_t)
    wfour_t = din("w_four", [4, 128, 128]); wfour = D("w_four", wfour_t)
    wout_t = din("w_out", [1024, 1024]); wout = D("w_out", wout_t)
    wpool_t = din("w_pool", [4, 256, 256]); wpool = D("w_pool", wpool_t)
    w1_t = din("w_mlp1", [2, 1024, 4096]); w1d = D("w_mlp1", w1_t)
    w2_t = din("w_mlp2", [2, 4096, 1024]); w2d = D("w_mlp2", w2_t)
    biasT_t = din("biasT", [128, 7, 8, 128]); biasT = D("biasT", biasT_t)
    cmask_t = din("cmask", [128, 128]); cmask = D("cmask", cmask_t)
    vm_t = din("vm", [128, NSLOT * 14]); vmd = D("vm", vm_t)
    ident_t = din("ident", [128, 128], BF16); identd = D("ident", ident_t)
    blk_t = din("blk", [128, 128], BF16); blkd = D("blk", blk_t)
    dftc_t = din("dftc", [128, 256], BF16); dftcd = D("dftc", dftc_t)
    dft1_t = din("dft1", [128, 256], BF16); dft1d = D("dft1", dft1_t)
    gtab_t = din("gtab", [128, 2, 128, NSLOT], BF16); gtabd = D("gtab", gtab_t)
    pm_t = din("pm", [128, 4, 3, 128], BF16); pmd = D("pm", pm_t)
    pmf_t = din("pmf", [128, 4, 128], BF16); pmfd = D("pmf", pmf_t)
    pml_t = din("pml", [128, 4, 128], BF16); pmld = D("pml", pml_t)

    out_t = nc.dram_tensor("out33", [NSLOT * 128, 1024], F32, kind="ExternalOutput"); outd = D("out33", out_t)

    modrow_t = nc.dram_tensor("modrow", [2, 2, 6144], F32); modrow = D("modrow", modrow_t)
    ofd_t = nc.dram_tensor("ofd", [4, 128, NSLOT * 128], BF16)
    ofd = [Buf("ofd%d" % g, ofd_t.ap()[g]) for g in range(4)]
    hmid_t = nc.dram_tensor("hmid", [NSLOT * 128, 1024], F32)
    hmid = [Buf("hmid%d" % s, hmid_t.ap()[s * 128:(s + 1) * 128, :]) for s in range(NSLOT)]
    h1_t = nc.dram_tensor("h1", [NSLOT * 128, 1024], F32)
    h1 = [Buf("h1_%d" % s, h1_t.ap()[s * 128:(s + 1) * 128, :]) for s in range(NSLOT)]
    outs = [Buf("out_%d" % s, out_t.ap()[s * 128:(s + 1) * 128, :]) for s in range(NSLOT)]
    dbg = {}
    if debug:
        def dout(name, shape, dt=F32):
            t = nc.dram_tensor(name, list(shape), dt, kind="ExternalOutput")
            dbg[name] = D(name, t)
            return dbg[name]
        dout("d_mod", [2, 2, 6144])
        dout("d_fT", [128, 4, 8192], BF16)
        dout("d_of", [4, 128, NSLOT * 128], BF16)
        dout("d_hmid", [NSLOT * 128, 1024])
        dout("d_h1", [NSLOT * 128, 1024])
        dout("d_hmid1", [NSLOT * 128, 1024])
        dout("d_kT", [128, 4, NKV * 128], BF16)
        dout("d_ona", [NSLOT * 128, 512], BF16)

    banks = [p.psum_buf("bank%d" % i, [128, 512], F32) for i in range(8)]

    def bank_bf(i):
        return V(banks[i], banks[i].ap.bitcast(BF16))

    ident = alloc("ident", [128, 128], BF16)
    blk = alloc("blk", [128, 128], BF16)
    cv = alloc("cv", [128, 16], F32)
    sT = alloc("sT", [128, 16], F32)
    gv = alloc("gv", [128, 32], F32)
    qk_g = alloc("qk_g", [128, 2], F32)
    epsb = alloc("epsb", [128, 1], F32)
    modT = alloc("modT", [128, 3, 48], F32)
    avec = alloc("avec", [128, 5, 8], F32)
    p.dma(ident, identd)
    p.dma(blk, blkd)
    p.dma(cv, cvec)
    p.dma(gv, gvec)
    p.dma(qk_g, qkg)
    p.memset(DVE, epsb, 1e-6)

    mk0 = p.arena_off
    p.act(sT, cv, AF.Silu)
    wm = [alloc("wm%d" % i, [128, 3072], F32) for i in range(2)]
    mrow = alloc("mrow", [2, 6144], F32)
    brow = alloc("brow", [2, 6144], F32)
    it = 0
    for l in range(2):
        p.dma(brow, V(bmod, bmod_t.ap()[l, :].partition_broadcast(2)))
        for half in range(2):
            for c in range(8):
                w = wm[it % 2]; it += 1
                p.dma(w, V(wmod, wmod_t.ap()[l, c * 128:(c + 1) * 128, half * 3072:(half + 1) * 3072]))
                for n in range(6):
                    p.mm(banks[n][0:2, :], sT[:, c:c + 9:8], w[:, n * 512:(n + 1) * 512],
                         start=(c == 0), stop=(c == 7))
            for n in range(6):
                col = half * 3072 + n * 512
                p.tt(DVE, mrow[:, col:col + 512], banks[n][0:2, :], brow[:, col:col + 512], ALU.add)
        p.dma(V(modrow, modrow_t.ap()[l]), mrow)
        if debug:
            p.dma(V(dbg["d_mod"], dbg["d_mod"].ap[l]), mrow)
    for i, (l, r) in enumerate(((0, 0), (0, 1), (1, 0))):
        src = modrow_t.ap()[l, r, :].rearrange("(j q) -> q j", q=128)
        p.dma(modT[:, i, :], V(modrow, src), allow_slow_non_contiguous=True)
    def mk_a(dst, gcol, mi, scoff):
        p.stt(DVE, avec[:, dst, :], modT[:, mi, scoff:scoff + 8], 1.0, gv[:, gcol:gcol + 8], ALU.add, ALU.mult)
    mk_a(0, 0, 0, 8)
    mk_a(1, 0, 1, 8)
    mk_a(2, 16, 0, 32)
    mk_a(3, 8, 2, 8)
    mk_a(4, 24, 2, 32)
    _barrier(p)
    p.arena_off = mk0

    if upto == 0:
        p.final_wait(SP, outs + list(dbg.values()))
        p.emit()
        return nc

    def fe_a(src_views, work):
        nt = len(src_views)
        ss = work["ss"][work["i"] % 2]
        rs = work["rs"][work["i"] % 2]
        work["i"] += 1
        xts, xns = [], []
        for t, sv in enumerate(src_views):
            xt = work["xt"][work["xi"] % len(work["xt"])]; work["xi"] += 1
            xn = work["xn"][work["ni"] % len(work["xn"])]; work["ni"] += 1
            p.dma(xt, sv)
            p.act(xn, xt, AF.Square, accum_out=ss[:, t:t + 1])
            xts.append(xt); xns.append(xn)
        p.act(rs[:, 0:nt], ss[:, 0:nt], AF.Sqrt, bias=epsb, scale=1.0 / 1024)
        p.recip(rs[:, 0:nt], rs[:, 0:nt])
        for t in range(nt):
            p.act(xns[t], xts[t], AF.Identity, scale=rs[:, t:t + 1])
        return xns, xts

    def fe_b(xns, a_v, b_v, uT, work):
        for t, xn in enumerate(xns):
            bk = work["trbank"][work["ti"] % len(work["trbank"])]; work["ti"] += 1
            pv = bank_bf(bk)
            for c in range(8):
                p.tr(pv[:, c * 128:(c + 1) * 128], xn[:, c * 128:(c + 1) * 128], ident)
            for c in range(8):
                p.act(uT[:, c, t * 128:(t + 1) * 128], pv[:, c * 128:(c + 1) * 128], AF.Identity,
                      bias=b_v[:, c:c + 1], scale=a_v[:, c:c + 1])

    def front_end(src_views, a_v, b_v, uT, work, tok_major=None):
        xns, xts = fe_a(src_views, work)
        fe_b(xns, a_v, b_v, uT, work)
        return xts, None

    def mk_work(nxt, trbanks, nxn=4):
        return dict(xt=[alloc("xt%d" % i, [128, 1024], F32) for i in range(nxt)],
                    xn=[alloc("xn%d" % i, [128, 1024], BF16) for i in range(nxn)],
                    ss=[alloc("ss%d" % i, [128, 4], F32) for i in range(2)],
                    rs=[alloc("rs%d" % i, [128, 4], F32) for i in range(2)],
                    trbank=trbanks, i=0, xi=0, ni=0, ti=0)

    def load_cast(dst_view, src_view, stg_list, cnt, shape_note=None):
        stg = stg_list[cnt[0] % len(stg_list)]
        eng = (DVE, POOL, ACT)[cnt[0] % 3]
        cnt[0] += 1
        return stg, eng

    fT = alloc("fT", [128, 4, 8192], BF16)
    mk1 = p.arena_off
    work = mk_work(6, [6, 7])
    winf_s = alloc("winf_s", [128, 8, 512], F32)
    winf = alloc("winf", [128, 8, 512], BF16)
    p.dma(winf_s, V(win, win_t.ap()[:, 0:512].rearrange("(c q) n -> q c n", q=128)))
    p.copy(DVE, winf, winf_s)
    uTs = [alloc("uT%d" % i, [128, 8, 512], BF16) for i in range(2)]
    for grp in range(16):
        uT = uTs[grp % 2]
        svs = [V(xb, xb_t.ap()[(grp * 4 + t) * 128:(grp * 4 + t + 1) * 128, :]) for t in range(4)]
        front_end(svs, avec[:, 0, :], modT[:, 0, 0:8], uT, work)
        for g in range(4):
            bk = banks[g % 4]
            for c in range(8):
                p.mm(bk, winf[:, c, g * 128:(g + 1) * 128], uT[:, c, :], start=(c == 0), stop=(c == 7))
            if g % 2 == 0:
                p.copy(DVE, fT[:, g, grp * 512:(grp + 1) * 512], bk)
            else:
                p.copy(ACT, fT[:, g, grp * 512:(grp + 1) * 512], bk)
    if debug:
        p.dma(dbg["d_fT"], fT)
    _barrier(p)
    p.arena_off = mk1

    if upto == 1:
        p.final_wait(SP, outs + list(dbg.values()))
        p.emit()
        return nc

    dftc = alloc("dftc", [128, 256], BF16)
    dft1 = alloc("dft1", [128, 256], BF16)
    gtab = alloc("gtab", [128, 2, 128, NSLOT], BF16)
    wf_s = alloc("wf_s", [128, 4, 128], F32)
    wf_b = alloc("wf_b", [128, 4, 128], BF16)
    Mg = alloc("Mg", [128, 4, 256], BF16)
    AB = alloc("AB", [128, 64, 256], BF16)
    P1 = [alloc("P1_%d" % i, [128, 128, 128], BF16) for i in range(2)]
    ofs = [alloc("ofs%d" % i, [128, NSLOT, 128], BF16) for i in range(2)]
    p.dma(dftc, dftcd)
    p.dma(dft1, dft1d)
    p.dma(gtab, gtabd)
    p.dma(wf_s, V(wfour, wfour_t.ap().rearrange("g c d -> c g d")))
    p.copy(DVE, wf_b, wf_s)
    for g in range(4):
        for cs in range(2):
            p.mm(banks[0][:, cs * 128:(cs + 1) * 128], dftc[:, cs * 128:(cs + 1) * 128], wf_b[:, g, :])
        p.copy(DVE, Mg[:, g, :], banks[0][:, 0:256])
    ev = 0
    for g in range(4):
        for n2p in range(32):
            bk = banks[n2p % 2]
            for k in range(2):
                n2 = n2p * 2 + k
                p.mm(bk[:, k * 256:(k + 1) * 256], fT[:, g, n2:8192:64], Mg[:, g, :])
            p.copy(DVE if n2p % 2 == 0 else ACT, AB[:, n2p * 2:n2p * 2 + 2, :].re("q a b -> q (a b)"), bk)
        ofsg = ofs[g % 2]
        for half in range(2):
            P1h = P1[half]
            for d4 in range(32):
                bk = banks[2 + d4 % 2]
                for k in range(4):
                    d = d4 * 4 + k
                    p.mm(bk[:, k * 128:(k + 1) * 128], AB[:, :, d:256:128].re("q a b -> q (a b)"),
                         dft1[:, half * 128:(half + 1) * 128])
                p.copy(DVE if d4 % 2 == 0 else ACT, P1h[:, d4 * 4:d4 * 4 + 4, :].re("q a b -> q (a b)"), bk)
            for k8 in range(8):
                bk = banks[4 + k8 % 2]
                for kk in range(8):
                    k1h = k8 * 8 + kk
                    k1 = half * 64 + k1h
                    for cs in range(2):
                        p.mm(bk[:, kk * NSLOT:(kk + 1) * NSLOT], P1h[:, :, cs * 64 + k1h],
                             gtab[:, cs, k1, :], start=(cs == 0), stop=(cs == 1))
                k1b = half * 64 + k8 * 8
                src = bk[:, 0:8 * NSLOT].re("q (a b) -> q b a", a=8)
                p.copy(DVE if k8 % 2 == 0 else POOL if False else ACT, ofsg[:, :, k1b:k1b + 8], src)
        p.dma(ofd[g], ofsg.v().re("q a b -> q (a b)"))
        if debug:
            p.dma(V(dbg["d_of"], dbg["d_of"].ap[g]), ofsg.v().re("q a b -> q (a b)"))
    _barrier(p)
    p.arena_off = mk1 - 0
    p.arena_off = mk0

    if upto == 2:
        p.final_wait(SP, outs + list(dbg.values()))
        p.emit()
        return nc

    kT = alloc("kT", [128, 4, NKV * 128], BF16)
    qT = alloc("qT", [128, 4, 36 * 128], BF16)
    Vx = alloc("Vx", [128, NKV, 8, 66], BF16)
    kcT = alloc("kcT", [128, 4, 256], BF16)
    Vc = alloc("Vc", [128, 2, 8, 66], BF16)
    mk3 = p.arena_off
    work = mk_work(5, [6, 7])
    wq_s = [alloc("wq_s%d" % i, [128, 512], F32) for i in range(3)]
    wqkv = alloc("wqkv", [128, 8, 1536], BF16)
    cc = 0
    for c in range(8):
        for j in range(3):
            stg = wq_s[cc % 3]
            p.dma(stg, V(win, win_t.ap()[c * 128:(c + 1) * 128, 512 + j * 512:1024 + j * 512]))
            p.copy((DVE, ACT)[cc % 2], wqkv[:, c, j * 512:(j + 1) * 512], stg)
            cc += 1
    p.memset(POOL, Vx[:, :, :, 64:66], 1.0)
    p.memset(POOL, Vc[:, :, :, 64:66], 1.0)
    uTs = [alloc("uT%d" % i, [128, 8, 512], BF16) for i in range(2)]
    sqb = [alloc("sqb%d" % i, [128, 512], BF16) for i in range(2)]
    rqk = [alloc("rqk%d" % i, [128, 512], F32) for i in range(2)]
    groups = [(i * 4, 4) for i in range(9)] + [(36, 2)]
    gi = 0

    def qk_proj(uT, ncol, woff, gcol, dstT, dcol):
        nonlocal gi
        for hp in range(4):
            bk = banks[hp % 2]
            for c in range(8):
                p.mm(bk[:, 0:ncol], wqkv[:, c, woff + hp * 128:woff + (hp + 1) * 128], uT[:, c, 0:ncol],
                     start=(c == 0), stop=(c == 7))
            sq = sqb[gi % 2]; rq = rqk[gi % 2]; gi += 1
            p.act(sq[:, 0:ncol], bk[:, 0:ncol], AF.Square)
            b2 = banks[2 + hp % 2]
            p.mm(b2[:, 0:ncol], blk, sq[:, 0:ncol])
            p.act(rq[:, 0:ncol], b2[:, 0:ncol], AF.Sqrt, bias=epsb, scale=1.0 / 64)
            p.recip(rq[:, 0:ncol], rq[:, 0:ncol])
            p.stt(DVE, dstT[:, hp, dcol:dcol + ncol], bk[:, 0:ncol], qk_g[:, gcol:gcol + 1], rq[:, 0:ncol],
                  ALU.mult, ALU.mult)

    def v_proj(uT, nt, dstV, slot0):
        for t in range(nt):
            bk = banks[4 + t % 2]
            for c in range(8):
                p.mm(bk, uT[:, c, t * 128:(t + 1) * 128], wqkv[:, c, 1024:1536], start=(c == 0), stop=(c == 7))
            p.copy(ACT if t % 2 == 0 else DVE, dstV[:, slot0 + t, :, 0:64], bk.v().re("q (h e) -> q h e", h=8))

    for gidx, (t0, nt) in enumerate(groups):
        uT = uTs[gidx % 2]
        svs = [V(xkv, xkv_t.ap()[(t0 + t) * 128:(t0 + t + 1) * 128, :]) for t in range(nt)]
        front_end(svs, avec[:, 0, :], modT[:, 0, 0:8], uT, work)
        ncol = nt * 128
        if t0 < 36:
            qk_proj(uT, ncol, 0, 0, qT, t0 * 128)
        qk_proj(uT, ncol, 512, 1, kT, t0 * 128)
        v_proj(uT, nt, Vx, t0)
    uT = uTs[0]
    svs = [V(ctxb, ctx_t.ap()[t * 128:(t + 1) * 128, :]) for t in range(2)]
    front_end(svs, avec[:, 1, :], modT[:, 1, 0:8], uT, work)
    qk_proj(uT, 256, 512, 1, kcT, 0)
    v_proj(uT, 2, Vc, 0)
    if debug:
        p.dma(dbg["d_kT"], kT)
    _barrier(p)
    p.arena_off = mk3

    if upto == 3:
        p.final_wait(SP, outs + list(dbg.values()))
        p.emit()
        return nc

    TE = alloc("TE", [128, 7, 8, 128], BF16)
    vm = alloc("vm", [128, NSLOT * 14], F32)
    cm = alloc("cm", [128, 128], F32)
    wo_b = alloc("wo_b", [128, 8, 1024], BF16)
    g1bc = alloc("g1bc", [128, 1024], F32)
    PT = [alloc("PT%d" % i, [128, 8, 8, 128], BF16) for i in range(1)]
    eS = [alloc("eS%d" % i, [128, 8, 128], BF16) for i in range(2)]
    xts = [alloc("xq%d" % i, [128, 1024], F32) for i in range(2)]
    hms = [alloc("hm%d" % i, [128, 1024], F32) for i in range(2)]
    qm = [alloc("qm%d" % i, [128, 4, 2, 128], BF16) for i in range(2)]
    ofT = [alloc("ofT%d" % i, [128, 4, 128], BF16) for i in range(2)]
    otok = alloc("otok", [128, 8, 64], BF16)
    onaT = alloc("onaT", [128, 4, 128], BF16)
    rden = alloc("rden", [128, 8], F32)
    p.dma(vm, vmd)
    p.dma(cm, cmask)
    p.dma(g1bc, V(modrow, modrow_t.ap()[0, 0, 2048:3072].partition_broadcast(128)))
    for c in range(8):
        stg = hms[c % 2]
        p.dma(stg, V(wout, wout_t.ap()[c * 128:(c + 1) * 128, :]))
        p.tt(DVE, wo_b[:, c, :], stg, g1bc, ALU.mult)
    for dl in range(7):
        stg = hms[dl % 2].v().re("q (a b) -> q a b", a=8)
        p.dma(stg, V(biasT, biasT_t.ap()[:, dl]))
        p.act(stg, stg, AF.Exp)
        p.tt(DVE, TE[:, dl], stg, V(cm, cm.ap.unsqueeze(1).broadcast_to([128, 8, 128])), ALU.mult)
    for qq in qm:
        p.memset(POOL, qq, 0.0)
    ei = 0
    for s in range(NSLOT):
        if s <= 1:
            dls = [-2, -1, 0, 1, 2, 3]
        elif s >= NSLOT - 2:
            dls = [-3, -2, -1, 0, 1, 2]
        else:
            dls = [-2, -1, 0, 1, 2]
        chunks = [("w", dl) for dl in dls] + [("c", 0), ("c", 1)]
        pt = PT[0]
        qs = s + 3
        xq = xts[s % 2]; hm = hms[s % 2]; of_t = ofT[s % 2]
        qms = qm[s % 2]
        p.copy(POOL, qms[0:64, :, 0, :], qT[0:64, :, qs * 128:(qs + 1) * 128])
        p.copy(POOL, qms[64:128, :, 1, :], qT[64:128, :, qs * 128:(qs + 1) * 128])
        p.dma(xq, V(xkv, xkv_t.ap()[qs * 128:(qs + 1) * 128, :]))
        for g in range(4):
            p.dma(of_t[:, g, :], V(ofd[g], ofd[g].ap[:, s * 128:(s + 1) * 128]))
        for j, (kind, dl) in enumerate(chunks):
            pair = (ei % 2) * 2
            e_s = eS[ei % 2]; ei += 1
            for h in range(8):
                hp, hh = h // 2, h % 2
                bk = banks[pair + h // 4]
                if kind == "w":
                    ks = s + 3 + dl
                    lhs = kT[:, hp, ks * 128:(ks + 1) * 128]
                else:
                    lhs = kcT[:, hp, dl * 128:(dl + 1) * 128]
                p.mm(bk[:, (h % 4) * 128:(h % 4 + 1) * 128], lhs, qms[:, hp, hh, :])
            for hb in range(2):
                dst = e_s if kind == "w" else pt[:, j]
                p.act(dst[:, hb * 4:(hb + 1) * 4, :], banks[pair + hb].v().re("q (h e) -> q h e", h=4),
                      AF.Exp, scale=0.125)
            if kind == "w":
                for qr in range(2):
                    col = s * 14 + (dl + 3) * 2 + qr
                    p.stt(DVE, pt[:, j, :, qr * 64:(qr + 1) * 64],
                          e_s[:, :, qr * 64:(qr + 1) * 64], vm[:, col:col + 1],
                          TE[:, dl + 3, :, qr * 64:(qr + 1) * 64], ALU.mult, ALU.mult)
        nch = len(chunks)
        for h in range(8):
            bk = banks[4 + h // 4]
            for j, (kind, dl) in enumerate(chunks):
                if kind == "w":
                    rhs = Vx[:, s + 3 + dl, h, :]
                else:
                    rhs = Vc[:, dl, h, :]
                p.mm(bk[:, (h % 4) * 66:(h % 4 + 1) * 66], pt[:, j, h, :], rhs, start=(j == 0), stop=(j == nch - 1))
        for hb in range(2):
            bv = banks[4 + hb][:, 0:264].re("q (h e) -> q h e", h=4)
            p.recip(rden[:, hb * 4:(hb + 1) * 4], bv[:, :, 64])
            p.tt(DVE, otok[:, hb * 4:(hb + 1) * 4, :], bv[:, :, 0:64],
                 V(rden, rden.ap[:, hb * 4:(hb + 1) * 4].unsqueeze(2).broadcast_to([128, 4, 64])), ALU.mult)
        if debug:
            p.dma(V(dbg["d_ona"], dbg["d_ona"].ap[s * 128:(s + 1) * 128, :]), otok.v().re("q h e -> q (h e)"))
        ptr_v = bank_bf(6)
        for c in range(4):
            p.tr(ptr_v[:, c * 128:(c + 1) * 128], otok.v().re("q h e -> q (h e)")[:, c * 128:(c + 1) * 128], ident)
        p.copy(ACT, onaT.v().re("q a b -> q (a b)"), ptr_v[:, 0:512])
        for nh in range(2):
            bk = banks[6 + nh]
            for c in range(8):
                lhs = of_t[:, c, :] if c < 4 else onaT[:, c - 4, :]
                p.mm(bk, lhs, wo_b[:, c, nh * 512:(nh + 1) * 512], start=(c == 0), stop=(c == 7))
            p.tt(DVE, hm[:, nh * 512:(nh + 1) * 512], bk, xq[:, nh * 512:(nh + 1) * 512], ALU.add)
        p.dma(hmid[s], hm)
        if debug:
            p.dma(V(dbg["d_hmid"], dbg["d_hmid"].ap[s * 128:(s + 1) * 128, :]), hm)
    _barrier(p)
    p.arena_off = mk0

    if upto == 4:
        p.final_wait(SP, outs + list(dbg.values()))
        p.emit()
        return nc

    def mlp_phase(l, src_tiles, dst_tiles, a_idx, mod_i, dbg_out=None):
        mk = p.arena_off
        w1b = alloc("w1b", [128, 8, 4096], BF16)
        w2b = alloc("w2b", [128, 32, 1024], BF16)
        aT = alloc("aT", [128, 32, 384], BF16)
        aT_flat = aT.ap.rearrange("q a b -> q (a b)").bitcast(F32)
        stg = [Buf("stg%d" % i, aT_flat[:, i * 2048:(i + 1) * 2048]) for i in range(3)]
        rl = [alloc("rl%d" % i, [128, 384], BF16) for i in range(2)]
        uTm = [alloc("uTm%d" % i, [128, 8, 384], BF16) for i in range(2)]
        outb = [alloc("outb%d" % i, [128, 1024], F32) for i in range(2)]
        xr = [alloc("xr%d" % i, [128, 1024], F32) for i in range(2)]
        work = mk_work(3, [6, 7], nxn=4)
        g2bc = outb[0]
        p.dma(g2bc, V(modrow, modrow_t.ap()[l, 0, 5120:6144].partition_broadcast(128)))
        cc = 0
        for c in range(8):
            for hh in range(2):
                s_ = stg[cc % 3]
                p.dma(s_, V(w1d, w1_t.ap()[l, c * 128:(c + 1) * 128, hh * 2048:(hh + 1) * 2048]))
                p.copy((DVE, ACT)[cc % 2], w1b[:, c, hh * 2048:(hh + 1) * 2048], s_)
                cc += 1
        g2v = V(g2bc, g2bc.ap.unsqueeze(1).broadcast_to([128, 2, 1024]))
        for j2 in range(16):
            s_ = stg[cc % 3]
            p.dma(s_.v().re("q (a b) -> q a b", a=2),
                  V(w2d, w2_t.ap()[l, j2 * 256:(j2 + 1) * 256, :].rearrange("(a q) n -> q a n", q=128)))
            p.tt(DVE, w2b[:, j2 * 2:j2 * 2 + 2, :], s_.v().re("q (a b) -> q a b", a=2), g2v, ALU.mult)
            cc += 1
        for s_ in stg:
            aT.readers.update(s_.readers)
            aT.writers.update(s_.writers)
        ntile = len(src_tiles)
        ngrp = ntile // 3
        a_v, b_v = avec[:, a_idx, :], modT[:, mod_i, 24:32]
        xns0, _ = fe_a([src_tiles[t] for t in range(3)], work)
        fe_b(xns0, a_v, b_v, uTm[0], work)
        nxt = None
        for g in range(ngrp):
            t0 = g * 3
            uT = uTm[g % 2]
            if g + 1 < ngrp:
                nxt, _ = fe_a([src_tiles[t0 + 3 + t] for t in range(3)], work)
            for j in range(32):
                bk = banks[j % 4]
                for c in range(8):
                    p.mm(bk[:, 0:384], w1b[:, c, j * 128:(j + 1) * 128], uT[:, c, :], start=(c == 0), stop=(c == 7))
                r = rl[j % 2]
                p.act(r, bk[:, 0:384], AF.Relu)
                p.tt(DVE, aT[:, j, :], r, r, ALU.mult)
            if g + 1 < ngrp:
                fe_b(nxt, a_v, b_v, uTm[(g + 1) % 2], work)
            for t in range(3):
                ob = outb[(t0 + t) % 2]
                xx = xr[(t0 + t) % 2]
                p.dma(xx, src_tiles[t0 + t])
                for nh in range(2):
                    bk = banks[4 + nh]
                    for j in range(32):
                        p.mm(bk, aT[:, j, t * 128:(t + 1) * 128], w2b[:, j, nh * 512:(nh + 1) * 512],
                             start=(j == 0), stop=(j == 31))
                    p.tt(DVE, ob[:, nh * 512:(nh + 1) * 512], bk, xx[:, nh * 512:(nh + 1) * 512], ALU.add)
                p.dma(dst_tiles[t0 + t], ob)
                if dbg_out is not None:
                    p.dma(V(dbg_out, dbg_out.ap[(t0 + t) * 128:(t0 + t + 1) * 128, :]), ob)
        _barrier(p)
        p.arena_off = mk

    mlp_phase(0, [h.v() for h in hmid], h1, 2, 0, dbg.get("d_h1"))

    if upto == 5:
        p.final_wait(SP, outs + list(dbg.values()))
        p.emit()
        return nc

    mk5 = p.arena_off
    utok = [alloc("utok%d" % s, [128, 1024], BF16) for s in range(NSLOT)]
    a1bc = alloc("a1bc", [128, 1024], F32)
    b1bc = alloc("b1bc", [128, 1024], F32)
    gpbc = alloc("gpbc", [128, 1024], F32)
    tmp1 = alloc("tmp1", [128, 1024], F32)
    pm = alloc("pm", [128, 4, 3, 128], BF16)
    pmf = alloc("pmf", [128, 4, 128], BF16)
    pml = alloc("pml", [128, 4, 128], BF16)
    wp_s = alloc("wp_s", [128, 8, 256], F32)
    wp_b = alloc("wp_b", [128, 8, 256], BF16)
    ssl = alloc("ssl", [128, NSLOT], F32)
    rsl = alloc("rsl", [128, NSLOT], F32)
    x1 = [alloc("x1_%d" % i, [128, 1024], F32) for i in range(4)]
    tf = [alloc("tf%d" % i, [128, 1024], F32) for i in range(2)]
    pooledT = [alloc("pooledT%d" % i, [128, 8, 128], BF16) for i in range(2)]
    hm1 = [alloc("hm1_%d" % i, [128, 1024], F32) for i in range(2)]
    p.dma(pm, pmd); p.dma(pmf, pmfd); p.dma(pml, pmld)
    p.dma(wp_s, V(wpool, wpool_t.ap().rearrange("g (k q) n -> q (g k) n", q=128)))
    p.dma(a1bc, V(modrow, modrow_t.ap()[1, 0, 1024:2048].partition_broadcast(128)))
    p.dma(tmp1, V(n1g, n1g_t.ap()[1, :].partition_broadcast(128)))
    p.stt(DVE, a1bc, a1bc, 1.0, tmp1, ALU.add, ALU.mult)
    p.dma(b1bc, V(modrow, modrow_t.ap()[1, 0, 0:1024].partition_broadcast(128)))
    p.dma(gpbc, V(modrow, modrow_t.ap()[1, 0, 2048:3072].partition_broadcast(128)))
    tmp2 = alloc("tmp2", [128, 1024], F32)
    p.dma(tmp2, V(psc, psc_t.ap()[0, :].partition_broadcast(128)))
    p.tt(DVE, gpbc, gpbc, tmp2, ALU.mult)
    p.tt(DVE, wp_b.v().re("q (g k) n -> q g k n", g=4), wp_s.v().re("q (g k) n -> q g k n", g=4),
         V(gpbc, gpbc.ap.rearrange("q (g n) -> q g n", g=4).unsqueeze(2).broadcast_to([128, 4, 2, 256])), ALU.mult)
    for s in range(NSLOT):
        xt = x1[s % 4]
        p.dma(xt, h1[s])
        p.act(utok[s], xt, AF.Square, accum_out=ssl[:, s:s + 1])
        p.act(rsl[:, s:s + 1], ssl[:, s:s + 1], AF.Sqrt, bias=epsb, scale=1.0 / 1024)
        p.recip(rsl[:, s:s + 1], rsl[:, s:s + 1])
        t_ = tf[s % 2]
        p.stt(DVE, t_, xt, rsl[:, s:s + 1], a1bc, ALU.mult, ALU.mult)
        p.tt(DVE, utok[s], t_, b1bc, ALU.add)
    for s in range(NSLOT):
        pT = pooledT[s % 2]
        for c in range(8):
            w = c // 2
            bk = banks[(s % 2) * 2 + c // 4]
            dst = bk[:, (c % 4) * 128:(c % 4 + 1) * 128]
            parts = []
            if s > 0:
                parts.append((utok[s - 1], pm[:, w, 0, :]))
            if s == 0:
                parts.append((utok[s], pmf[:, w, :]))
            elif s == NSLOT - 1:
                parts.append((utok[s], pml[:, w, :]))
            else:
                parts.append((utok[s], pm[:, w, 1, :]))
            if s < NSLOT - 1:
                parts.append((utok[s + 1], pm[:, w, 2, :]))
            for i, (ut, mat) in enumerate(parts):
                p.mm(dst, ut[:, c * 128:(c + 1) * 128], mat, start=(i == 0), stop=(i == len(parts) - 1))
        for hb in range(2):
            p.copy(ACT if hb == 0 else DVE, pT[:, hb * 4:(hb + 1) * 4, :].re("q a b -> q (a b)"),
                   banks[(s % 2) * 2 + hb])
        xts1 = []
        for nh in range(2):
            bk = banks[4 + (s % 2) * 2 + nh]
            for gg in range(2):
                g = nh * 2 + gg
                for kk in range(2):
                    p.mm(bk[:, gg * 256:(gg + 1) * 256], pT[:, 2 * g + kk, :], wp_b[:, 2 * g + kk, :],
                         start=(kk == 0), stop=(kk == 1))
            xts1.append(bk)
        xt = x1[s % 4]
        p.dma(xt, h1[s])
        for nh in range(2):
            p.tt(DVE, hm1[s % 2][:, nh * 512:(nh + 1) * 512], xts1[nh], xt[:, nh * 512:(nh + 1) * 512], ALU.add)
        p.dma(hmid[s], hm1[s % 2])
        if debug:
            p.dma(V(dbg["d_hmid1"], dbg["d_hmid1"].ap[s * 128:(s + 1) * 128, :]), hm1[s % 2])
    _barrier(p)
    p.arena_off = mk5

    if upto == 6:
        p.final_wait(SP, outs + list(dbg.values()))
        p.emit()
        return nc

    mlp_phase(1, [h.v() for h in hmid], outs, 4, 2, None)

    p.final_wait(SP, outs + list(dbg.values()))
    p.emit()
    return nc


import ml_dtypes
from concourse.bass_utils import run_bass_kernel_spmd

_BF = ml_dtypes.bfloat16
_PROG_CACHE = {}


def _const_tables():
    t = {}
    t["ident"] = np.eye(128, dtype=np.float32).astype(_BF)
    blk = np.zeros((128, 128), np.float32)
    blk[:64, :64] = 1.0
    blk[64:, 64:] = 1.0
    t["blk"] = blk.astype(_BF)
    i = np.arange(128, dtype=np.float64)
    ang = 2 * np.pi * ((i[:, None] * i[None, :]) % 128) / 128.0
    t["dftc"] = np.concatenate([np.cos(ang) / 1024.0, np.sin(ang) / 1024.0], axis=1).astype(np.float32).astype(_BF)
    d1 = np.zeros((128, 2, 2, 64), np.float64)
    for half in range(2):
        d1[:, half, 0, :] = np.cos(ang[:, half * 64:(half + 1) * 64])
        d1[:, half, 1, :] = np.sin(ang[:, half * 64:(half + 1) * 64])
    t["dft1"] = d1.reshape(128, 256).astype(np.float32).astype(_BF)
    qc = np.arange(64)
    cs = np.clip(qc - 8, 0, 48)
    kc = np.arange(64)
    m = ((kc[:, None] >= cs[None, :]) & (kc[:, None] < cs[None, :] + 16)).astype(np.float32)
    t["cmask"] = np.tile(m, (2, 2)).astype(np.float32)
    pm = np.zeros((128, 4, 3, 128), np.float64)
    pmf = np.zeros((128, 4, 128), np.float64)
    pml = np.zeros((128, 4, 128), np.float64)
    tt = np.arange(128)
    for wi, w in enumerate(POOL_SIZES):
        for tq in range(128):
            lo, hi = tq - w // 2, tq + w - w // 2
            for tp in range(lo, hi):
                if tp < 0:
                    pm[tp + 128, wi, 0, tq] += 1.0 / w
                elif tp >= 128:
                    pm[tp - 128, wi, 2, tq] += 1.0 / w
                else:
                    pm[tp, wi, 1, tq] += 1.0 / w
            pm[tq, wi, 1, tq] -= 1.0
            lo_c = max(lo, 0)
            cnt = hi - lo_c
            for tp in range(lo_c, min(hi, 128)):
                pmf[tp, wi, tq] += 1.0 / cnt
            pmf[tq, wi, tq] -= 1.0
            hi_c = min(hi, 128)
            cnt = hi_c - lo
            for tp in range(max(lo, 0), hi_c):
                pml[tp, wi, tq] += 1.0 / cnt
            pml[tq, wi, tq] -= 1.0
    t["pm"] = pm.astype(np.float32).astype(_BF)
    t["pmf_first"] = pmf.astype(np.float32).astype(_BF)
    t["pml_last"] = pml.astype(np.float32).astype(_BF)
    t["pm_main"] = np.ascontiguousarray(pm[:, :, 1, :]).astype(np.float32).astype(_BF)
    return t


def _core_tables(hf):
    qb = 0 if hf == 0 else 31
    n2 = np.arange(64, dtype=np.int64)
    k1 = np.arange(128, dtype=np.int64)
    k2 = qb + np.arange(NSLOT, dtype=np.int64)
    k = k1[:, None] + 128 * k2[None, :]
    beta = 2 * np.pi * ((n2[:, None, None] * k[None]) % 8192) / 8192.0
    cb, sb = np.cos(beta), np.sin(beta)
    g = np.zeros((64, 2, 2, 128, NSLOT), np.float64)
    g[:, 0, 0] = cb
    g[:, 1, 0] = -sb
    g[:, 0, 1] = -sb
    g[:, 1, 1] = -cb
    gtab = g.reshape(128, 2, 128, NSLOT).astype(np.float32).astype(_BF)
    vm = np.zeros((128, NSLOT, 7, 2), np.float32)
    for s in range(NSLOT):
        gt = qb + s
        for di, dl in enumerate(range(-3, 4)):
            for qr in range(2):
                qrow = 2 * gt + qr
                rs = min(max(qrow - 4, 0), 120)
                for jr in range(2):
                    krow = 2 * (gt + dl) + jr
                    ok = (0 <= krow <= 127) and (rs <= krow < rs + 8) and (0 <= qrow <= 127)
                    vm[jr * 64:(jr + 1) * 64, s, di, qr] = 1.0 if ok else 0.0
    return gtab, vm.reshape(128, NSLOT * 14)


def _chunks(v):
    return np.ascontiguousarray(np.asarray(v, np.float32).reshape(8, 128).T)


def _prepare_inputs(inputs):
    f32 = lambda a: np.ascontiguousarray(np.asarray(a, dtype=np.float32))
    x = f32(inputs["x"]); c = f32(inputs["c"]); ctx = f32(inputs["ctx"]); c_ctx = f32(inputs["c_ctx"])
    rpb = f32(inputs["rpb"])[0]
    ct = _const_tables()
    jr = np.arange(128) // 64
    kc = np.arange(128) % 64
    qr = np.arange(128) // 64
    qc = np.arange(128) % 64
    dl = np.arange(-3, 4)
    dr = np.clip(2 * dl[None, :, None] + jr[:, None, None] - qr[None, None, :] + 7, 0, 14)
    dc = np.clip(kc[:, None] - qc[None, :] + 15, 0, 30)
    biasT = rpb[:, dr, dc[:, None, :]]
    biasT = np.ascontiguousarray(biasT.transpose(1, 2, 0, 3)).astype(np.float32)
    gvec = np.concatenate([_chunks(inputs["norm1_g"][0]), _chunks(inputs["norm1_g"][1]),
                           _chunks(inputs["norm2_g"][0]), _chunks(inputs["norm2_g"][1])], axis=1)
    qkg = np.stack([np.tile(f32(inputs["q_norm_g"])[0], 2), np.tile(f32(inputs["k_norm_g"])[0], 2)], axis=1)
    shared = {
        "gvec": np.ascontiguousarray(gvec), "qkg": np.ascontiguousarray(qkg.astype(np.float32)),
        "w_mod": f32(inputs["w_mod"]), "b_mod": f32(inputs["b_mod"]),
        "norm1_g": f32(inputs["norm1_g"]), "pool_scale": f32(inputs["pool_scale"]),
        "w_in": f32(inputs["w_in_even"])[0], "w_four": f32(inputs["w_four"])[0],
        "w_out": f32(inputs["w_out_even"])[0], "w_pool": f32(inputs["w_pool"])[0],
        "w_mlp1": f32(inputs["w_mlp1"]), "w_mlp2": f32(inputs["w_mlp2"]),
        "biasT": biasT, "cmask": ct["cmask"],
        "ident": ct["ident"], "blk": ct["blk"], "dftc": ct["dftc"], "dft1": ct["dft1"], "pm": ct["pm"],
    }
    per_hf = [_core_tables(0), _core_tables(1)]
    in_maps = []
    for core in range(8):
        b, hf = core // 2, core % 2
        qb = 0 if hf == 0 else 31
        xkv = np.zeros((NKV * 128, 1024), np.float32)
        lo_t, hi_t = qb - 3, qb - 3 + NKV
        a, e = max(lo_t, 0), min(hi_t, 64)
        xkv[(a - lo_t) * 128:(e - lo_t) * 128] = x[b, a * 128:e * 128]
        cvec = np.concatenate([_chunks(c[b]), _chunks(c_ctx)], axis=1)
        m = dict(shared)
        m.update({
            "xb": x[b], "xkv": xkv, "ctxb": ctx[b], "cvec": np.ascontiguousarray(cvec),
            "gtab": per_hf[hf][0], "vm": per_hf[hf][1],
            "pmf": ct["pmf_first"] if hf == 0 else ct["pm_main"],
            "pml": ct["pm_main"] if hf == 0 else ct["pml_last"],
        })
        in_maps.append(m)
    return in_maps


def kernel(**inputs):
    if "prog" not in _PROG_CACHE:
        _PROG_CACHE["prog"] = build_program(debug=False)
    nc = _PROG_CACHE["prog"]
    in_maps = _prepare_inputs(inputs)
    res = run_bass_kernel_spmd(nc, in_maps, core_ids=list(range(8)))
    out = np.empty((4, 8192, 1024), np.float32)
    for core in range(8):
        b, hf = core // 2, core % 2
        o = res.results[core]["out33"]
        if hf == 0:
            out[b, 0:4096] = o[0:4096]
        else:
            out[b, 4096:8192] = o[128:4224]
    return out
```

```python
import numpy as np
import concourse.bass as bass
import concourse.mybir as mybir
from contextlib import ExitStack

F32 = mybir.dt.float32
BF16 = mybir.dt.bfloat16
AF = mybir.ActivationFunctionType
ALU = mybir.AluOpType
AX = mybir.AxisListType

PE, ACT, DVE, POOL, SP = "pe", "act", "dve", "pool", "sp"
COMPUTE = (PE, ACT, DVE, POOL)
N_DMA_SEMS = 24


class Buf:
    __slots__ = ("name", "ap", "writers", "readers")

    def __init__(self, name, ap):
        self.name = name
        self.ap = ap
        self.writers = {}
        self.readers = {}

    def __getitem__(self, idx):
        return V(self, self.ap[idx])

    def v(self):
        return V(self, self.ap)


class V:
    __slots__ = ("buf", "ap")

    def __init__(self, buf, ap):
        self.buf = buf
        self.ap = ap

    def __getitem__(self, idx):
        return V(self.buf, self.ap[idx])

    def re(self, pattern, **kw):
        return V(self.buf, self.ap.rearrange(pattern, **kw))


def _ap(x):
    return x.ap if isinstance(x, (V,)) else (x.ap if isinstance(x, Buf) else x)


def _buf(x):
    if isinstance(x, V):
        return x.buf
    if isinstance(x, Buf):
        return x
    return None


class Prog:
    def __init__(self, nc):
        self.nc = nc
        self.stack = ExitStack()
        self.ops = {e: [] for e in (PE, ACT, DVE, POOL, SP)}
        self.know = {e: {} for e in (PE, ACT, DVE, POOL, SP)}
        self.sems = {}
        for e in COMPUTE:
            self.sems[e] = self.stack.enter_context(nc.semaphore("c_" + e))
        self.dsems = [self.stack.enter_context(nc.semaphore("d%d" % i)) for i in range(N_DMA_SEMS)]
        self.dcount = [0] * N_DMA_SEMS
        self.dlast_issuer = [None] * N_DMA_SEMS
        self.drr = 0
        self.n_alloc = 0
        self.out_tokens = []

    def sbuf(self, name, shape, dtype):
        t = self.stack.enter_context(self.nc.sbuf_tensor(name, list(shape), dtype))
        return t

    def psum(self, name, shape, dtype=F32):
        t = self.stack.enter_context(self.nc.psum_tensor(name, list(shape), dtype))
        return t

    def sbuf_buf(self, name, shape, dtype):
        t = self.sbuf(name, shape, dtype)
        return Buf(name, t[tuple(slice(None) for _ in shape)])

    def psum_buf(self, name, shape, dtype=F32):
        t = self.psum(name, shape, dtype)
        return Buf(name, t[tuple(slice(None) for _ in shape)])

    def dram(self, name, shape, dtype, kind="Internal"):
        t = self.nc.dram_tensor(name, list(shape), dtype, kind=kind)
        return t

    def _deps(self, eng, reads, writes):
        toks = []
        for r in reads:
            b = _buf(r)
            if b is None:
                continue
            for k, t in b.writers.items():
                toks.append((t, "raw"))
        for w in writes:
            b = _buf(w)
            if b is None:
                continue
            for k, t in b.writers.items():
                toks.append((t, "waw"))
            for k, t in b.readers.items():
                toks.append((t, "war"))
        need = {}
        for t, kind in toks:
            if t[0] == "c":
                src = t[1]
                if src == eng:
                    if eng == PE:
                        continue
                key = ("c", src)
            else:
                key = ("d", t[1])
            if need.get(key, -1) < t[2]:
                need[key] = t[2]
        waits = []
        kn = self.know[eng]
        for key, val in need.items():
            if kn.get(key, -1) >= val:
                continue
            kn[key] = val
            waits.append((key, val))
        return waits

    def _mark(self, tok, reads, writes):
        for r in reads:
            b = _buf(r)
            if b is None:
                continue
            k = tok[:2]
            b.readers[k] = tok
        for w in writes:
            b = _buf(w)
            if b is None:
                continue
            b.writers = {tok[:2]: tok}
            b.readers = {}

    def op(self, eng, fn, reads=(), writes=()):
        waits = self._deps(eng, reads, writes)
        idx = len(self.ops[eng])
        rec = dict(fn=fn, waits=waits, signal=False, dma=None)
        self.ops[eng].append(rec)
        for key, val in waits:
            if key[0] == "c":
                self.ops[key[1]][val]["signal"] = True
        tok = ("c", eng, idx)
        self._mark(tok, reads, writes)
        return tok

    def dma(self, out, in_, queue=SP, **kw):
        reads, writes = [in_], [out]
        waits = self._deps(queue, reads, writes)
        s = self.drr
        self.drr = (self.drr + 1) % N_DMA_SEMS
        prev = self.dcount[s]
        if prev > 0:
            kn = self.know[queue]
            if kn.get(("d", s), -1) < prev:
                kn[("d", s)] = prev
                waits.append((("d", s), prev))
        self.dcount[s] = prev + 1
        o, i = _ap(out), _ap(in_)
        rec = dict(fn=lambda e: e.dma_start(out=o, in_=i, **kw), waits=waits, signal=False,
                   dma=(s, prev + 1))
        for key, val in waits:
            if key[0] == "c":
                self.ops[key[1]][val]["signal"] = True
        self.ops[queue].append(rec)
        tok = ("d", s, prev + 1)
        self._mark(tok, reads, writes)
        return tok

    def final_wait(self, eng, bufs):
        waits = self._deps(eng, bufs, [])
        rec = dict(fn=None, waits=waits, signal=False, dma=None)
        for key, val in waits:
            if key[0] == "c":
                self.ops[key[1]][val]["signal"] = True
        self.ops[eng].append(rec)

    def emit(self):
        nc = self.nc
        sigord = {}
        for e in COMPUTE:
            c = 0
            for i, rec in enumerate(self.ops[e]):
                if rec["signal"]:
                    c += 1
                    sigord[(e, i)] = c
        engobj = {PE: "tensor", ACT: "scalar", DVE: "vector", POOL: "gpsimd", SP: "sync"}

        def run(engname):
            def body(eng):
                for i, rec in enumerate(self.ops[engname]):
                    for key, val in rec["waits"]:
                        if key[0] == "c":
                            eng.wait_ge(self.sems[key[1]], sigord[(key[1], val)])
                        else:
                            eng.wait_ge(self.dsems[key[1]], 16 * val)
                    if rec["fn"] is None:
                        continue
                    ins = rec["fn"](eng)
                    if rec["dma"] is not None:
                        ins.then_inc(self.dsems[rec["dma"][0]], 16)
                    elif rec["signal"]:
                        ins.then_inc(self.sems[engname], 1)
            return body

        with nc.Block() as block:
            block.tensor(run(PE))
            block.scalar(run(ACT))
            block.vector(run(DVE))
            block.gpsimd(run(POOL))
            block.sync(run(SP))
        self.stack.close()

    def mm(self, out, lhsT, rhs, start=True, stop=True, **kw):
        o, l, r = _ap(out), _ap(lhsT), _ap(rhs)
        return self.op(PE, lambda e: e.matmul(o, l, r, start=start, stop=stop, **kw),
                       reads=[lhsT, rhs], writes=[out])

    def tr(self, out, in_, ident):
        o, i, d = _ap(out), _ap(in_), _ap(ident)
        return self.op(PE, lambda e: e.transpose(o, i, d), reads=[in_, ident], writes=[out])

    def act(self, out, in_, func, bias=None, scale=None, accum_out=None, eng=ACT):
        o, i = _ap(out), _ap(in_)
        kw = {}
        reads = [in_]
        writes = [out]
        if bias is not None:
            kw["bias"] = _ap(bias)
            reads.append(bias)
        if scale is not None:
            kw["scale"] = _ap(scale)
            reads.append(scale)
        if accum_out is not None:
            kw["accum_out"] = _ap(accum_out)
            writes.append(accum_out)
        return self.op(ACT, lambda e: e.activation(o, i, func, **kw), reads=reads, writes=writes)

    def ts(self, eng, out, in0, s1, s2, op0, op1=None, accum_out=None):
        o, i = _ap(out), _ap(in0)
        a1, a2 = _ap(s1), _ap(s2)
        reads = [in0, s1, s2]
        writes = [out]
        kw = {}
        if op1 is not None:
            kw["op1"] = op1
        if accum_out is not None:
            kw["accum_out"] = _ap(accum_out)
            writes.append(accum_out)
        return self.op(eng, lambda e: e.tensor_scalar(o, i, a1, a2, op0, **kw), reads=reads, writes=writes)

    def tt(self, eng, out, in0, in1, op):
        o, a, b = _ap(out), _ap(in0), _ap(in1)
        return self.op(eng, lambda e: e.tensor_tensor(o, a, b, op), reads=[in0, in1], writes=[out])

    def stt(self, eng, out, in0, scalar, in1, op0, op1):
        o, a, s, b = _ap(out), _ap(in0), _ap(scalar), _ap(in1)
        return self.op(eng, lambda e: e.scalar_tensor_tensor(o, a, s, b, op0, op1),
                       reads=[in0, scalar, in1], writes=[out])

    def copy(self, eng, out, in_):
        o, i = _ap(out), _ap(in_)
        if eng == ACT:
            return self.op(ACT, lambda e: e.copy(o, i), reads=[in_], writes=[out])
        return self.op(eng, lambda e: e.tensor_copy(o, i), reads=[in_], writes=[out])

    def memset(self, eng, out, val):
        o = _ap(out)
        return self.op(eng, lambda e: e.memset(o, val), reads=[], writes=[out])

    def recip(self, out, in_):
        o, i = _ap(out), _ap(in_)
        return self.op(DVE, lambda e: e.reciprocal(o, i), reads=[in_], writes=[out])


ARENA_BYTES = 207 * 1024
VERBOSE = False
NSLOT = 33
NKV = 38
POOL_SIZES = (2, 4, 8, 16)


def _arena_setup(p):
    p.arena_t = p.sbuf("arena", [128, ARENA_BYTES // 4], F32)
    p.arena_off = 0

    def alloc(name, shape, dtype):
        esz = 4 if dtype == F32 else 2
        n = 1
        for s in shape[1:]:
            n *= s
        nbytes = (n * esz + 63) // 64 * 64
        off = p.arena_off
        assert off + nbytes <= ARENA_BYTES, ("arena overflow", name, off, nbytes)
        p.arena_off = off + nbytes
        p.arena_peak = max(getattr(p, "arena_peak", 0), p.arena_off)
        ap = p.arena_t[0:shape[0], off // 4:(off + nbytes) // 4]
        if dtype != F32:
            ap = ap.bitcast(dtype)
        ap = ap[:, 0:n]
        if len(shape) == 3:
            ap = ap.rearrange("p (a b) -> p a b", a=shape[1])
        elif len(shape) == 4:
            ap = ap.rearrange("p (a b c) -> p a b c", a=shape[1], b=shape[2])
        elif len(shape) == 5:
            ap = ap.rearrange("p (a b c d) -> p a b c d", a=shape[1], b=shape[2], c=shape[3])
        return Buf(name, ap)

    p.alloc = alloc


def _barrier(p):
    if VERBOSE:
        print("arena peak at barrier: %.1f KB" % (getattr(p, "arena_peak", 0) / 1024.0))
        p.arena_peak = 0
    for e in (PE, ACT, DVE, POOL, SP):
        waits = []
        kn = p.know[e]
        for src in COMPUTE:
            if src == e or not p.ops[src]:
                continue
            idx = len(p.ops[src]) - 1
            while idx >= 0 and p.ops[src][idx]["fn"] is None:
                idx -= 1
            if idx < 0:
                continue
            if kn.get(("c", src), -1) < idx:
                kn[("c", src)] = idx
                waits.append((("c", src), idx))
                p.ops[src][idx]["signal"] = True
        for s in range(N_DMA_SEMS):
            if p.dcount[s] > 0 and kn.get(("d", s), -1) < p.dcount[s]:
                kn[("d", s)] = p.dcount[s]
                waits.append((("d", s), p.dcount[s]))
        p.ops[e].append(dict(fn=None, waits=waits, signal=False, dma=None))


def _dbuf(t, *idx):
    return t


def build_program(debug=False, upto=99):
    nc = bass.Bass("TRN2", target_bir_lowering=False)
    p = Prog(nc)
    _arena_setup(p)
    alloc = p.alloc

    def din(name, shape, dt=F32):
        return nc.dram_tensor(name, list(shape), dt, kind="ExternalInput")

    def D(name, t):
        return Buf(name, t.ap())

    xb_t = din("xb", [8192, 1024]); xb = D("xb", xb_t)
    xkv_t = din("xkv", [NKV * 128, 1024]); xkv = D("xkv", xkv_t)
    ctx_t = din("ctxb", [256, 1024]); ctxb = D("ctxb", ctx_t)
    cvec_t = din("cvec", [128, 16]); cvec = D("cvec", cvec_t)
    gvec_t = din("gvec", [128, 32]); gvec = D("gvec", gvec_t)
    qkg_t = din("qkg", [128, 2]); qkg = D("qkg", qkg_t)
    wmod_t = din("w_mod", [2, 1024, 6144]); wmod = D("w_mod", wmod_t)
    bmod_t = din("b_mod", [2, 6144]); bmod = D("b_mod", bmod_t)
    n1g_t = din("norm1_g", [2, 1024]); n1g = D("norm1_g", n1g_t)
    psc_t = din("pool_scale", [1, 1024]); psc = D("pool_scale", psc_t)
    win_t = din("w_in", [1024, 2048]); win = D("w_in", win_t)
    wfour_t = din("w_four", [4, 128, 128]); wfour = D("w_four", wfour_t)
    wout_t = din("w_out", [1024, 1024]); wout = D("w_out", wout_t)
    wpool_t = din("w_pool", [4, 256, 256]); wpool = D("w_pool", wpool_t)
    w1_t = din("w_mlp1", [2, 1024, 4096]); w1d = D("w_mlp1", w1_t)
    w2_t = din("w_mlp2", [2, 4096, 1024]); w2d = D("w_mlp2", w2_t)
    biasT_t = din("biasT", [128, 7, 8, 128]); biasT = D("biasT", biasT_t)
    cmask_t = din("cmask", [128, 128]); cmask = D("cmask", cmask_t)
    vm_t = din("vm", [128, NSLOT * 14]); vmd = D("vm", vm_t)
    ident_t = din("ident", [128, 128], BF16); identd = D("ident", ident_t)
    blk_t = din("blk", [128, 128], BF16); blkd = D("blk", blk_t)
    dftc_t = din("dftc", [128, 256], BF16); dftcd = D("dftc", dftc_t)
    dft1_t = din("dft1", [128, 256], BF16); dft1d = D("dft1", dft1_t)
    gtab_t = din("gtab", [128, 2, 128, NSLOT], BF16); gtabd = D("gtab", gtab_t)
    pm_t = din("pm", [128, 4, 3, 128], BF16); pmd = D("pm", pm_t)
    pmf_t = din("pmf", [128, 4, 128], BF16); pmfd = D("pmf", pmf_t)
    pml_t = din("pml", [128, 4, 128], BF16); pmld = D("pml", pml_t)

    out_t = nc.dram_tensor("out33", [NSLOT * 128, 1024], F32, kind="ExternalOutput"); outd = D("out33", out_t)

    modrow_t = nc.dram_tensor("modrow", [2, 2, 6144], F32); modrow = D("modrow", modrow_t)
    ofd_t = nc.dram_tensor("ofd", [4, 128, NSLOT * 128], BF16)
    ofd = [Buf("ofd%d" % g, ofd_t.ap()[g]) for g in range(4)]
    hmid_t = nc.dram_tensor("hmid", [NSLOT * 128, 1024], F32)
    hmid = [Buf("hmid%d" % s, hmid_t.ap()[s * 128:(s + 1) * 128, :]) for s in range(NSLOT)]
    h1_t = nc.dram_tensor("h1", [NSLOT * 128, 1024], F32)
    h1 = [Buf("h1_%d" % s, h1_t.ap()[s * 128:(s + 1) * 128, :]) for s in range(NSLOT)]
    outs = [Buf("out_%d" % s, out_t.ap()[s * 128:(s + 1) * 128, :]) for s in range(NSLOT)]
    dbg = {}
    if debug:
        def dout(name, shape, dt=F32):
            t = nc.dram_tensor(name, list(shape), dt, kind="ExternalOutput")
            dbg[name] = D(name, t)
            return dbg[name]
        dout("d_mod", [2, 2, 6144])
        dout("d_fT", [128, 4, 8192], BF16)
        dout("d_of", [4, 128, NSLOT * 128], BF16)
        dout("d_hmid", [NSLOT * 128, 1024])
        dout("d_h1", [NSLOT * 128, 1024])
        dout("d_hmid1", [NSLOT * 128, 1024])
        dout("d_kT", [128, 4, NKV * 128], BF16)
        dout("d_ona", [NSLOT * 128, 512], BF16)

    banks = [p.psum_buf("bank%d" % i, [128, 512], F32) for i in range(8)]

    def bank_bf(i):
        return V(banks[i], banks[i].ap.bitcast(BF16))

    ident = alloc("ident", [128, 128], BF16)
    blk = alloc("blk", [128, 128], BF16)
    cv = alloc("cv", [128, 16], F32)
    sT = alloc("sT", [128, 16], F32)
    gv = alloc("gv", [128, 32], F32)
    qk_g = alloc("qk_g", [128, 2], F32)
    epsb = alloc("epsb", [128, 1], F32)
    modT = alloc("modT", [128, 3, 48], F32)
    avec = alloc("avec", [128, 5, 8], F32)
    p.dma(ident, identd)
    p.dma(blk, blkd)
    p.dma(cv, cvec)
    p.dma(gv, gvec)
    p.dma(qk_g, qkg)
    p.memset(DVE, epsb, 1e-6)

    mk0 = p.arena_off
    p.act(sT, cv, AF.Silu)
    wm = [alloc("wm%d" % i, [128, 3072], F32) for i in range(2)]
    mrow = alloc("mrow", [2, 6144], F32)
    brow = alloc("brow", [2, 6144], F32)
    it = 0
    for l in range(2):
        p.dma(brow, V(bmod, bmod_t.ap()[l, :].partition_broadcast(2)))
        for half in range(2):
            for c in range(8):
                w = wm[it % 2]; it += 1
                p.dma(w, V(wmod, wmod_t.ap()[l, c * 128:(c + 1) * 128, half * 3072:(half + 1) * 3072]))
                for n in range(6):
                    p.mm(banks[n][0:2, :], sT[:, c:c + 9:8], w[:, n * 512:(n + 1) * 512],
                         start=(c == 0), stop=(c == 7))
            for n in range(6):
                col = half * 3072 + n * 512
                p.tt(DVE, mrow[:, col:col + 512], banks[n][0:2, :], brow[:, col:col + 512], ALU.add)
        p.dma(V(modrow, modrow_t.ap()[l]), mrow)
        if debug:
            p.dma(V(dbg["d_mod"], dbg["d_mod"].ap[l]), mrow)
    for i, (l, r) in enumerate(((0, 0), (0, 1), (1, 0))):
        src = modrow_t.ap()[l, r, :].rearrange("(j q) -> q j", q=128)
        p.dma(modT[:, i, :], V(modrow, src), allow_slow_non_contiguous=True)
    def mk_a(dst, gcol, mi, scoff):
        p.stt(DVE, avec[:, dst, :], modT[:, mi, scoff:scoff + 8], 1.0, gv[:, gcol:gcol + 8], ALU.add, ALU.mult)
    mk_a(0, 0, 0, 8)
    mk_a(1, 0, 1, 8)
    mk_a(2, 16, 0, 32)
    mk_a(3, 8, 2, 8)
    mk_a(4, 24, 2, 32)
    _barrier(p)
    p.arena_off = mk0

    if upto == 0:
        p.final_wait(SP, outs + list(dbg.values()))
        p.emit()
        return nc

    def fe_a(src_views, work):
        nt = len(src_views)
        ss = work["ss"][work["i"] % 2]
        rs = work["rs"][work["i"] % 2]
        work["i"] += 1
        xts, xns = [], []
        for t, sv in enumerate(src_views):
            xt = work["xt"][work["xi"] % len(work["xt"])]; work["xi"] += 1
            xn = work["xn"][work["ni"] % len(work["xn"])]; work["ni"] += 1
            p.dma(xt, sv)
            p.act(xn, xt, AF.Square, accum_out=ss[:, t:t + 1])
            xts.append(xt); xns.append(xn)
        p.act(rs[:, 0:nt], ss[:, 0:nt], AF.Sqrt, bias=epsb, scale=1.0 / 1024)
        p.recip(rs[:, 0:nt], rs[:, 0:nt])
        for t in range(nt):
            p.ts(DVE, xns[t], xts[t], rs[:, t:t + 1], None, ALU.mult)
        return xns, xts

    def fe_b(xns, a_v, b_v, uT, work):
        for t, xn in enumerate(xns):
            bk = work["trbank"][work["ti"] % len(work["trbank"])]; work["ti"] += 1
            pv = bank_bf(bk)
            for c in range(8):
                p.tr(pv[:, c * 128:(c + 1) * 128], xn[:, c * 128:(c + 1) * 128], ident)
            for c in range(8):
                p.act(uT[:, c, t * 128:(t + 1) * 128], pv[:, c * 128:(c + 1) * 128], AF.Identity,
                      bias=b_v[:, c:c + 1], scale=a_v[:, c:c + 1])

    def front_end(src_views, a_v, b_v, uT, work, tok_major=None):
        xns, xts = fe_a(src_views, work)
        fe_b(xns, a_v, b_v, uT, work)
        return xts, None

    def mk_work(nxt, trbanks, nxn=4, ntu=2):
        return dict(xt=[alloc("xt%d" % i, [128, 1024], F32) for i in range(nxt)],
                    xn=[alloc("xn%d" % i, [128, 1024], BF16) for i in range(nxn)],
                    ss=[alloc("ss%d" % i, [128, 4], F32) for i in range(2)],
                    rs=[alloc("rs%d" % i, [128, 4], F32) for i in range(2)],
                    tu=[alloc("tu%d" % i, [128, 4, 128], F32) for i in range(ntu)],
                    trbank=trbanks, i=0, xi=0, ni=0, ti=0)

    def load_cast(dst_view, src_view, stg_list, cnt, shape_note=None):
        stg = stg_list[cnt[0] % len(stg_list)]
        eng = (DVE, POOL, ACT)[cnt[0] % 3]
        cnt[0] += 1
        return stg, eng

    fT = alloc("fT", [128, 4, 8192], BF16)
    mk1 = p.arena_off
    work = mk_work(6, [6, 7])
    winf_s = alloc("winf_s", [128, 8, 512], F32)
    winf = alloc("winf", [128, 8, 512], BF16)
    p.dma(winf_s, V(win, win_t.ap()[:, 0:512].rearrange("(c q) n -> q c n", q=128)))
    p.copy(DVE, winf, winf_s)
    uTs = [alloc("uT%d" % i, [128, 8, 512], BF16) for i in range(2)]
    for grp in range(16):
        uT = uTs[grp % 2]
        svs = [V(xb, xb_t.ap()[(grp * 4 + t) * 128:(grp * 4 + t + 1) * 128, :]) for t in range(4)]
        front_end(svs, avec[:, 0, :], modT[:, 0, 0:8], uT, work)
        for g in range(4):
            bk = banks[g % 4]
            for c in range(8):
                p.mm(bk, winf[:, c, g * 128:(g + 1) * 128], uT[:, c, :], start=(c == 0), stop=(c == 7))
            if g % 2 == 0:
                p.copy(DVE, fT[:, g, grp * 512:(grp + 1) * 512], bk)
            else:
                p.copy(ACT, fT[:, g, grp * 512:(grp + 1) * 512], bk)
    if debug:
        p.dma(dbg["d_fT"], fT)
    _barrier(p)
    p.arena_off = mk1

    if upto == 1:
        p.final_wait(SP, outs + list(dbg.values()))
        p.emit()
        return nc

    dftc = alloc("dftc", [128, 256], BF16)
    dft1 = alloc("dft1", [128, 256], BF16)
    gtab = alloc("gtab", [128, 2, 128, NSLOT], BF16)
    wf_s = alloc("wf_s", [128, 4, 128], F32)
    wf_b = alloc("wf_b", [128, 4, 128], BF16)
    Mg = alloc("Mg", [128, 4, 256], BF16)
    AB = alloc("AB", [128, 64, 256], BF16)
    P1 = [alloc("P1_%d" % i, [128, 128, 128], BF16) for i in range(2)]
    ofs = [alloc("ofs%d" % i, [128, NSLOT, 128], BF16) for i in range(2)]
    p.dma(dftc, dftcd)
    p.dma(dft1, dft1d)
    p.dma(gtab, gtabd)
    p.dma(wf_s, V(wfour, wfour_t.ap().rearrange("g c d -> c g d")))
    p.copy(DVE, wf_b, wf_s)
    for g in range(4):
        for cs in range(2):
            p.mm(banks[0][:, cs * 128:(cs + 1) * 128], dftc[:, cs * 128:(cs + 1) * 128], wf_b[:, g, :])
        p.copy(DVE, Mg[:, g, :], banks[0][:, 0:256])
    ev = 0
    for g in range(4):
        for n2p in range(32):
            bk = banks[n2p % 2]
            for k in range(2):
                n2 = n2p * 2 + k
                p.mm(bk[:, k * 256:(k + 1) * 256], fT[:, g, n2:8192:64], Mg[:, g, :])
            p.copy(DVE if n2p % 2 == 0 else ACT, AB[:, n2p * 2:n2p * 2 + 2, :].re("q a b -> q (a b)"), bk)
        ofsg = ofs[g % 2]
        for half in range(2):
            P1h = P1[half]
            for d4 in range(32):
                bk = banks[2 + d4 % 2]
                for k in range(4):
                    d = d4 * 4 + k
                    p.mm(bk[:, k * 128:(k + 1) * 128], AB[:, :, d:256:128].re("q a b -> q (a b)"),
                         dft1[:, half * 128:(half + 1) * 128])
                p.copy(DVE if d4 % 2 == 0 else ACT, P1h[:, d4 * 4:d4 * 4 + 4, :].re("q a b -> q (a b)"), bk)
            for k8 in range(8):
                bk = banks[4 + k8 % 2]
                for kk in range(8):
                    k1h = k8 * 8 + kk
                    k1 = half * 64 + k1h
                    for cs in range(2):
                        p.mm(bk[:, kk * NSLOT:(kk + 1) * NSLOT], P1h[:, :, cs * 64 + k1h],
                             gtab[:, cs, k1, :], start=(cs == 0), stop=(cs == 1))
                k1b = half * 64 + k8 * 8
                src = bk[:, 0:8 * NSLOT].re("q (a b) -> q b a", a=8)
                p.copy(DVE if k8 % 2 == 0 else POOL if False else ACT, ofsg[:, :, k1b:k1b + 8], src)
        p.dma(ofd[g], ofsg.v().re("q a b -> q (a b)"))
        if debug:
            p.dma(V(dbg["d_of"], dbg["d_of"].ap[g]), ofsg.v().re("q a b -> q (a b)"))
    _barrier(p)
    p.arena_off = mk1 - 0
    p.arena_off = mk0

    if upto == 2:
        p.final_wait(SP, outs + list(dbg.values()))
        p.emit()
        return nc

    kT = alloc("kT", [128, 4, NKV * 128], BF16)
    qT = alloc("qT", [128, 4, 36 * 128], BF16)
    Vx = alloc("Vx", [128, NKV, 8, 66], BF16)
    kcT = alloc("kcT", [128, 4, 256], BF16)
    Vc = alloc("Vc", [128, 2, 8, 66], BF16)
    mk3 = p.arena_off
    work = mk_work(5, [6, 7])
    wq_s = [alloc("wq_s%d" % i, [128, 512], F32) for i in range(3)]
    wqkv = alloc("wqkv", [128, 8, 1536], BF16)
    cc = 0
    for c in range(8):
        for j in range(3):
            stg = wq_s[cc % 3]
            p.dma(stg, V(win, win_t.ap()[c * 128:(c + 1) * 128, 512 + j * 512:1024 + j * 512]))
            p.copy((DVE, ACT)[cc % 2], wqkv[:, c, j * 512:(j + 1) * 512], stg)
            cc += 1
    p.memset(POOL, Vx[:, :, :, 64:66], 1.0)
    p.memset(POOL, Vc[:, :, :, 64:66], 1.0)
    uTs = [alloc("uT%d" % i, [128, 8, 512], BF16) for i in range(2)]
    sqb = [alloc("sqb%d" % i, [128, 512], BF16) for i in range(2)]
    rqk = [alloc("rqk%d" % i, [128, 512], F32) for i in range(2)]
    groups = [(i * 4, 4) for i in range(9)] + [(36, 2)]
    gi = 0

    def qk_proj(uT, ncol, woff, gcol, dstT, dcol):
        nonlocal gi
        for hp in range(4):
            bk = banks[hp % 2]
            for c in range(8):
                p.mm(bk[:, 0:ncol], wqkv[:, c, woff + hp * 128:woff + (hp + 1) * 128], uT[:, c, 0:ncol],
                     start=(c == 0), stop=(c == 7))
            sq = sqb[gi % 2]; rq = rqk[gi % 2]; gi += 1
            p.act(sq[:, 0:ncol], bk[:, 0:ncol], AF.Square)
            b2 = banks[2 + hp % 2]
            p.mm(b2[:, 0:ncol], blk, sq[:, 0:ncol])
            p.act(rq[:, 0:ncol], b2[:, 0:ncol], AF.Sqrt, bias=epsb, scale=1.0 / 64)
            p.recip(rq[:, 0:ncol], rq[:, 0:ncol])
            p.stt(DVE, dstT[:, hp, dcol:dcol + ncol], bk[:, 0:ncol], qk_g[:, gcol:gcol + 1], rq[:, 0:ncol],
                  ALU.mult, ALU.mult)

    def v_proj(uT, nt, dstV, slot0):
        for t in range(nt):
            bk = banks[4 + t % 2]
            for c in range(8):
                p.mm(bk, uT[:, c, t * 128:(t + 1) * 128], wqkv[:, c, 1024:1536], start=(c == 0), stop=(c == 7))
            p.copy(ACT if t % 2 == 0 else DVE, dstV[:, slot0 + t, :, 0:64], bk.v().re("q (h e) -> q h e", h=8))

    for gidx, (t0, nt) in enumerate(groups):
        uT = uTs[gidx % 2]
        svs = [V(xkv, xkv_t.ap()[(t0 + t) * 128:(t0 + t + 1) * 128, :]) for t in range(nt)]
        front_end(svs, avec[:, 0, :], modT[:, 0, 0:8], uT, work)
        ncol = nt * 128
        if t0 < 36:
            qk_proj(uT, ncol, 0, 0, qT, t0 * 128)
        qk_proj(uT, ncol, 512, 1, kT, t0 * 128)
        v_proj(uT, nt, Vx, t0)
    uT = uTs[0]
    svs = [V(ctxb, ctx_t.ap()[t * 128:(t + 1) * 128, :]) for t in range(2)]
    front_end(svs, avec[:, 1, :], modT[:, 1, 0:8], uT, work)
    qk_proj(uT, 256, 512, 1, kcT, 0)
    v_proj(uT, 2, Vc, 0)
    if debug:
        p.dma(dbg["d_kT"], kT)
    _barrier(p)
    p.arena_off = mk3

    if upto == 3:
        p.final_wait(SP, outs + list(dbg.values()))
        p.emit()
        return nc

    TE = alloc("TE", [128, 7, 8, 128], BF16)
    vm = alloc("vm", [128, NSLOT * 14], F32)
    cm = alloc("cm", [128, 128], F32)
    wo_b = alloc("wo_b", [128, 8, 1024], BF16)
    PT = [alloc("PT%d" % i, [128, 8, 8, 128], BF16) for i in range(2)]
    eS = [alloc("eS%d" % i, [128, 8, 128], BF16) for i in range(2)]
    xts = [alloc("xq%d" % i, [128, 1024], F32) for i in range(2)]
    hms = [alloc("hm%d" % i, [128, 1024], F32) for i in range(1)]
    g1bc = xts[0]
    qm = [alloc("qm%d" % i, [128, 4, 2, 128], BF16) for i in range(1)]
    ofT = [alloc("ofT%d" % i, [128, 4, 128], BF16) for i in range(2)]
    otok = alloc("otok", [128, 8, 64], BF16)
    onaT = alloc("onaT", [128, 4, 128], BF16)
    rden = alloc("rden", [128, 8], F32)
    p.dma(vm, vmd)
    p.dma(cm, cmask)
    p.dma(g1bc, V(modrow, modrow_t.ap()[0, 0, 2048:3072].partition_broadcast(128)))
    for c in range(8):
        stg = (hms[0], xts[1])[c % 2]
        p.dma(stg, V(wout, wout_t.ap()[c * 128:(c + 1) * 128, :]))
        p.tt(DVE, wo_b[:, c, :], stg, g1bc, ALU.mult)
    for dl in range(7):
        stg = (hms[0], xts[1])[dl % 2].v().re("q (a b) -> q a b", a=8)
        p.dma(stg, V(biasT, biasT_t.ap()[:, dl]))
        p.act(stg, stg, AF.Exp)
        p.tt(DVE, TE[:, dl], stg, V(cm, cm.ap.unsqueeze(1).broadcast_to([128, 8, 128])), ALU.mult)
    for qq in qm:
        p.memset(POOL, qq, 0.0)
    ei = 0
    for s in range(NSLOT):
        if s <= 1:
            dls = [-2, -1, 0, 1, 2, 3]
        elif s >= NSLOT - 2:
            dls = [-3, -2, -1, 0, 1, 2]
        else:
            dls = [-2, -1, 0, 1, 2]
        chunks = [("w", dl) for dl in dls] + [("c", 0), ("c", 1)]
        pt = PT[s % 2]
        qs = s + 3
        xq = xts[s % 2]; hm = hms[0]; of_t = ofT[s % 2]
        qms = qm[0]
        p.copy(POOL, qms[0:64, :, 0, :], qT[0:64, :, qs * 128:(qs + 1) * 128])
        p.copy(POOL, qms[64:128, :, 1, :], qT[64:128, :, qs * 128:(qs + 1) * 128])
        p.dma(xq, V(xkv, xkv_t.ap()[qs * 128:(qs + 1) * 128, :]))
        for g in range(4):
            p.dma(of_t[:, g, :], V(ofd[g], ofd[g].ap[:, s * 128:(s + 1) * 128]))
        for j, (kind, dl) in enumerate(chunks):
            pair = (ei % 2) * 2
            e_s = eS[ei % 2]; ei += 1
            for h in range(8):
                hp, hh = h // 2, h % 2
                bk = banks[pair + h // 4]
                if kind == "w":
                    ks = s + 3 + dl
                    lhs = kT[:, hp, ks * 128:(ks + 1) * 128]
                else:
                    lhs = kcT[:, hp, dl * 128:(dl + 1) * 128]
                p.mm(bk[:, (h % 4) * 128:(h % 4 + 1) * 128], lhs, qms[:, hp, hh, :])
            for hb in range(2):
                dst = e_s if kind == "w" else pt[:, j]
                p.act(dst[:, hb * 4:(hb + 1) * 4, :], banks[pair + hb].v().re("q (h e) -> q h e", h=4),
                      AF.Exp, scale=0.125)
            if kind == "w":
                for qr in range(2):
                    col = s * 14 + (dl + 3) * 2 + qr
                    p.stt(DVE, pt[:, j, :, qr * 64:(qr + 1) * 64],
                          e_s[:, :, qr * 64:(qr + 1) * 64], vm[:, col:col + 1],
                          TE[:, dl + 3, :, qr * 64:(qr + 1) * 64], ALU.mult, ALU.mult)
        nch = len(chunks)
        for h in range(8):
            bk = banks[4 + h // 4]
            for j, (kind, dl) in enumerate(chunks):
                if kind == "w":
                    rhs = Vx[:, s + 3 + dl, h, :]
                else:
                    rhs = Vc[:, dl, h, :]
                p.mm(bk[:, (h % 4) * 66:(h % 4 + 1) * 66], pt[:, j, h, :], rhs, start=(j == 0), stop=(j == nch - 1))
        for hb in range(2):
            bv = banks[4 + hb][:, 0:264].re("q (h e) -> q h e", h=4)
            p.recip(rden[:, hb * 4:(hb + 1) * 4], bv[:, :, 64])
            p.tt(DVE, otok[:, hb * 4:(hb + 1) * 4, :], bv[:, :, 0:64],
                 V(rden, rden.ap[:, hb * 4:(hb + 1) * 4].unsqueeze(2).broadcast_to([128, 4, 64])), ALU.mult)
        if debug:
            p.dma(V(dbg["d_ona"], dbg["d_ona"].ap[s * 128:(s + 1) * 128, :]), otok.v().re("q h e -> q (h e)"))
        ptr_v = bank_bf(6)
        for c in range(4):
            p.tr(ptr_v[:, c * 128:(c + 1) * 128], otok.v().re("q h e -> q (h e)")[:, c * 128:(c + 1) * 128], ident)
        p.copy(ACT, onaT.v().re("q a b -> q (a b)"), ptr_v[:, 0:512])
        for nh in range(2):
            bk = banks[6 + nh]
            for c in range(8):
                lhs = of_t[:, c, :] if c < 4 else onaT[:, c - 4, :]
                p.mm(bk, lhs, wo_b[:, c, nh * 512:(nh + 1) * 512], start=(c == 0), stop=(c == 7))
            p.tt(DVE, hm[:, nh * 512:(nh + 1) * 512], bk, xq[:, nh * 512:(nh + 1) * 512], ALU.add)
        p.dma(hmid[s], hm)
        if debug:
            p.dma(V(dbg["d_hmid"], dbg["d_hmid"].ap[s * 128:(s + 1) * 128, :]), hm)
    _barrier(p)
    p.arena_off = mk0

    if upto == 4:
        p.final_wait(SP, outs + list(dbg.values()))
        p.emit()
        return nc

    def mlp_phase(l, src_tiles, dst_tiles, a_idx, mod_i, dbg_out=None):
        mk = p.arena_off
        w1b = alloc("w1b", [128, 8, 4096], BF16)
        w2b = alloc("w2b", [128, 32, 1024], BF16)
        aT = alloc("aT", [128, 32, 384], BF16)
        aT_flat = aT.ap.rearrange("q a b -> q (a b)").bitcast(F32)
        stg = [Buf("stg%d" % i, aT_flat[:, i * 2048:(i + 1) * 2048]) for i in range(3)]
        rl = [alloc("rl%d" % i, [128, 384], BF16) for i in range(2)]
        uTm = [alloc("uTm%d" % i, [128, 8, 384], BF16) for i in range(2)]
        outb = [alloc("outb%d" % i, [128, 1024], F32) for i in range(2)]
        xr = [alloc("xr%d" % i, [128, 1024], F32) for i in range(2)]
        work = mk_work(3, [6, 7], nxn=3, ntu=1)
        g2bc = outb[0]
        p.dma(g2bc, V(modrow, modrow_t.ap()[l, 0, 5120:6144].partition_broadcast(128)))
        cc = 0
        for c in range(8):
            for hh in range(2):
                s_ = stg[cc % 3]
                p.dma(s_, V(w1d, w1_t.ap()[l, c * 128:(c + 1) * 128, hh * 2048:(hh + 1) * 2048]))
                p.copy((DVE, ACT)[cc % 2], w1b[:, c, hh * 2048:(hh + 1) * 2048], s_)
                cc += 1
        g2v = V(g2bc, g2bc.ap.unsqueeze(1).broadcast_to([128, 2, 1024]))
        for j2 in range(16):
            s_ = stg[cc % 3]
            p.dma(s_.v().re("q (a b) -> q a b", a=2),
                  V(w2d, w2_t.ap()[l, j2 * 256:(j2 + 1) * 256, :].rearrange("(a q) n -> q a n", q=128)))
            p.tt(DVE, w2b[:, j2 * 2:j2 * 2 + 2, :], s_.v().re("q (a b) -> q a b", a=2), g2v, ALU.mult)
            cc += 1
        for s_ in stg:
            aT.readers.update(s_.readers)
            aT.writers.update(s_.writers)
        ntile = len(src_tiles)
        ngrp = ntile // 3
        a_v, b_v = avec[:, a_idx, :], modT[:, mod_i, 24:32]
        xns0, _ = fe_a([src_tiles[t] for t in range(3)], work)
        fe_b(xns0, a_v, b_v, uTm[0], work)
        nxt = None
        for g in range(ngrp):
            t0 = g * 3
            uT = uTm[g % 2]
            if g + 1 < ngrp:
                nxt, _ = fe_a([src_tiles[t0 + 3 + t] for t in range(3)], work)
            for j in range(32):
                bk = banks[j % 4]
                for c in range(8):
                    p.mm(bk[:, 0:384], w1b[:, c, j * 128:(j + 1) * 128], uT[:, c, :], start=(c == 0), stop=(c == 7))
                r = rl[j % 2]
                p.act(r, bk[:, 0:384], AF.Relu)
                p.tt(DVE, aT[:, j, :], r, r, ALU.mult)
            if g + 1 < ngrp:
                fe_b(nxt, a_v, b_v, uTm[(g + 1) % 2], work)
            for t in range(3):
                ob = outb[(t0 + t) % 2]
                xx = xr[(t0 + t) % 2]
                p.dma(xx, src_tiles[t0 + t])
                for nh in range(2):
                    bk = banks[4 + nh]
                    for j in range(32):
                        p.mm(bk, aT[:, j, t * 128:(t + 1) * 128], w2b[:, j, nh * 512:(nh + 1) * 512],
                             start=(j == 0), stop=(j == 31))
                    p.tt(DVE, ob[:, nh * 512:(nh + 1) * 512], bk, xx[:, nh * 512:(nh + 1) * 512], ALU.add)
                p.dma(dst_tiles[t0 + t], ob)
                if dbg_out is not None:
                    p.dma(V(dbg_out, dbg_out.ap[(t0 + t) * 128:(t0 + t + 1) * 128, :]), ob)
        _barrier(p)
        p.arena_off = mk

    mlp_phase(0, [h.v() for h in hmid], h1, 2, 0, dbg.get("d_h1"))

    if upto == 5:
        p.final_wait(SP, outs + list(dbg.values()))
        p.emit()
        return nc

    mk5 = p.arena_off
    utok = [alloc("utok%d" % s, [128, 1024], BF16) for s in range(NSLOT)]
    a1bc = alloc("a1bc", [128, 1024], F32)
    b1bc = alloc("b1bc", [128, 1024], F32)
    gpbc = alloc("gpbc", [128, 1024], F32)
    tmp1 = alloc("tmp1", [128, 1024], F32)
    pm = alloc("pm", [128, 4, 3, 128], BF16)
    pmf = alloc("pmf", [128, 4, 128], BF16)
    pml = alloc("pml", [128, 4, 128], BF16)
    wp_s = alloc("wp_s", [128, 8, 256], F32)
    wp_b = alloc("wp_b", [128, 8, 256], BF16)
    ssl = alloc("ssl", [128, NSLOT], F32)
    rsl = alloc("rsl", [128, NSLOT], F32)
    x1 = [alloc("x1_%d" % i, [128, 1024], F32) for i in range(4)]
    tf = [alloc("tf%d" % i, [128, 1024], F32) for i in range(2)]
    pooledT = [alloc("pooledT%d" % i, [128, 8, 128], BF16) for i in range(2)]
    hm1 = [alloc("hm1_%d" % i, [128, 1024], F32) for i in range(2)]
    p.dma(pm, pmd); p.dma(pmf, pmfd); p.dma(pml, pmld)
    p.dma(wp_s, V(wpool, wpool_t.ap().rearrange("g (k q) n -> q (g k) n", q=128)))
    p.dma(a1bc, V(modrow, modrow_t.ap()[1, 0, 1024:2048].partition_broadcast(128)))
    p.dma(tmp1, V(n1g, n1g_t.ap()[1, :].partition_broadcast(128)))
    p.stt(DVE, a1bc, a1bc, 1.0, tmp1, ALU.add, ALU.mult)
    p.dma(b1bc, V(modrow, modrow_t.ap()[1, 0, 0:1024].partition_broadcast(128)))
    p.dma(gpbc, V(modrow, modrow_t.ap()[1, 0, 2048:3072].partition_broadcast(128)))
    tmp2 = alloc("tmp2", [128, 1024], F32)
    p.dma(tmp2, V(psc, psc_t.ap()[0, :].partition_broadcast(128)))
    p.tt(DVE, gpbc, gpbc, tmp2, ALU.mult)
    p.tt(DVE, wp_b.v().re("q (g k) n -> q g k n", g=4), wp_s.v().re("q (g k) n -> q g k n", g=4),
         V(gpbc, gpbc.ap.rearrange("q (g n) -> q g n", g=4).unsqueeze(2).broadcast_to([128, 4, 2, 256])), ALU.mult)
    for s in range(NSLOT):
        xt = x1[s % 4]
        p.dma(xt, h1[s])
        p.act(utok[s], xt, AF.Square, accum_out=ssl[:, s:s + 1])
        p.act(rsl[:, s:s + 1], ssl[:, s:s + 1], AF.Sqrt, bias=epsb, scale=1.0 / 1024)
        p.recip(rsl[:, s:s + 1], rsl[:, s:s + 1])
        t_ = tf[s % 2]
        p.stt(DVE, t_, xt, rsl[:, s:s + 1], a1bc, ALU.mult, ALU.mult)
        p.tt(DVE, utok[s], t_, b1bc, ALU.add)
    for s in range(NSLOT):
        pT = pooledT[s % 2]
        for c in range(8):
            w = c // 2
            bk = banks[(s % 2) * 2 + c // 4]
            dst = bk[:, (c % 4) * 128:(c % 4 + 1) * 128]
            parts = []
            if s > 0:
                parts.append((utok[s - 1], pm[:, w, 0, :]))
            if s == 0:
                parts.append((utok[s], pmf[:, w, :]))
            elif s == NSLOT - 1:
                parts.append((utok[s], pml[:, w, :]))
            else:
                parts.append((utok[s], pm[:, w, 1, :]))
            if s < NSLOT - 1:
                parts.append((utok[s + 1], pm[:, w, 2, :]))
            for i, (ut, mat) in enumerate(parts):
                p.mm(dst, ut[:, c * 128:(c + 1) * 128], mat, start=(i == 0), stop=(i == len(parts) - 1))
        for hb in range(2):
            p.copy(ACT if hb == 0 else DVE, pT[:, hb * 4:(hb + 1) * 4, :].re("q a b -> q (a b)"),
                   banks[(s % 2) * 2 + hb])
        xts1 = []
        for nh in range(2):
            bk = banks[4 + (s % 2) * 2 + nh]
            for gg in range(2):
                g = nh * 2 + gg
                for kk in range(2):
                    p.mm(bk[:, gg * 256:(gg + 1) * 256], pT[:, 2 * g + kk, :], wp_b[:, 2 * g + kk, :],
                         start=(kk == 0), stop=(kk == 1))
            xts1.append(bk)
        xt = x1[s % 4]
        p.dma(xt, h1[s])
        for nh in range(2):
            p.tt(DVE, hm1[s % 2][:, nh * 512:(nh + 1) * 512], xts1[nh], xt[:, nh * 512:(nh + 1) * 512], ALU.add)
        p.dma(hmid[s], hm1[s % 2])
        if debug:
            p.dma(V(dbg["d_hmid1"], dbg["d_hmid1"].ap[s * 128:(s + 1) * 128, :]), hm1[s % 2])
    _barrier(p)
    p.arena_off = mk5

    if upto == 6:
        p.final_wait(SP, outs + list(dbg.values()))
        p.emit()
        return nc

    mlp_phase(1, [h.v() for h in hmid], outs, 4, 2, None)

    p.final_wait(SP, outs + list(dbg.values()))
    p.emit()
    return nc


import ml_dtypes
from concourse.bass_utils import run_bass_kernel_spmd

_BF = ml_dtypes.bfloat16
_PROG_CACHE = {}


def _const_tables():
    t = {}
    t["ident"] = np.eye(128, dtype=np.float32).astype(_BF)
    blk = np.zeros((128, 128), np.float32)
    blk[:64, :64] = 1.0
    blk[64:, 64:] = 1.0
    t["blk"] = blk.astype(_BF)
    i = np.arange(128, dtype=np.float64)
    ang = 2 * np.pi * ((i[:, None] * i[None, :]) % 128) / 128.0
    t["dftc"] = np.concatenate([np.cos(ang) / 1024.0, np.sin(ang) / 1024.0], axis=1).astype(np.float32).astype(_BF)
    d1 = np.zeros((128, 2, 2, 64), np.float64)
    for half in range(2):
        d1[:, half, 0, :] = np.cos(ang[:, half * 64:(half + 1) * 64])
        d1[:, half, 1, :] = np.sin(ang[:, half * 64:(half + 1) * 64])
    t["dft1"] = d1.reshape(128, 256).astype(np.float32).astype(_BF)
    qc = np.arange(64)
    cs = np.clip(qc - 8, 0, 48)
    kc = np.arange(64)
    m = ((kc[:, None] >= cs[None, :]) & (kc[:, None] < cs[None, :] + 16)).astype(np.float32)
    t["cmask"] = np.tile(m, (2, 2)).astype(np.float32)
    pm = np.zeros((128, 4, 3, 128), np.float64)
    pmf = np.zeros((128, 4, 128), np.float64)
    pml = np.zeros((128, 4, 128), np.float64)
    tt = np.arange(128)
    for wi, w in enumerate(POOL_SIZES):
        for tq in range(128):
            lo, hi = tq - w // 2, tq + w - w // 2
            for tp in range(lo, hi):
                if tp < 0:
                    pm[tp + 128, wi, 0, tq] += 1.0 / w
                elif tp >= 128:
                    pm[tp - 128, wi, 2, tq] += 1.0 / w
                else:
                    pm[tp, wi, 1, tq] += 1.0 / w
            pm[tq, wi, 1, tq] -= 1.0
            lo_c = max(lo, 0)
            cnt = hi - lo_c
            for tp in range(lo_c, min(hi, 128)):
                pmf[tp, wi, tq] += 1.0 / cnt
            pmf[tq, wi, tq] -= 1.0
            hi_c = min(hi, 128)
            cnt = hi_c - lo
            for tp in range(max(lo, 0), hi_c):
                pml[tp, wi, tq] += 1.0 / cnt
            pml[tq, wi, tq] -= 1.0
    t["pm"] = pm.astype(np.float32).astype(_BF)
    t["pmf_first"] = pmf.astype(np.float32).astype(_BF)
    t["pml_last"] = pml.astype(np.float32).astype(_BF)
    t["pm_main"] = np.ascontiguousarray(pm[:, :, 1, :]).astype(np.float32).astype(_BF)
    return t


def _core_tables(hf):
    qb = 0 if hf == 0 else 31
    n2 = np.arange(64, dtype=np.int64)
    k1 = np.arange(128, dtype=np.int64)
    k2 = qb + np.arange(NSLOT, dtype=np.int64)
    k = k1[:, None] + 128 * k2[None, :]
    beta = 2 * np.pi * ((n2[:, None, None] * k[None]) % 8192) / 8192.0
    cb, sb = np.cos(beta), np.sin(beta)
    g = np.zeros((64, 2, 2, 128, NSLOT), np.float64)
    g[:, 0, 0] = cb
    g[:, 1, 0] = -sb
    g[:, 0, 1] = -sb
    g[:, 1, 1] = -cb
    gtab = g.reshape(128, 2, 128, NSLOT).astype(np.float32).astype(_BF)
    vm = np.zeros((128, NSLOT, 7, 2), np.float32)
    for s in range(NSLOT):
        gt = qb + s
        for di, dl in enumerate(range(-3, 4)):
            for qr in range(2):
                qrow = 2 * gt + qr
                rs = min(max(qrow - 4, 0), 120)
                for jr in range(2):
                    krow = 2 * (gt + dl) + jr
                    ok = (0 <= krow <= 127) and (rs <= krow < rs + 8) and (0 <= qrow <= 127)
                    vm[jr * 64:(jr + 1) * 64, s, di, qr] = 1.0 if ok else 0.0
    return gtab, vm.reshape(128, NSLOT * 14)


def _chunks(v):
    return np.ascontiguousarray(np.asarray(v, np.float32).reshape(8, 128).T)


def _prepare_inputs(inputs):
    f32 = lambda a: np.ascontiguousarray(np.asarray(a, dtype=np.float32))
    x = f32(inputs["x"]); c = f32(inputs["c"]); ctx = f32(inputs["ctx"]); c_ctx = f32(inputs["c_ctx"])
    rpb = f32(inputs["rpb"])[0]
    ct = _const_tables()
    jr = np.arange(128) // 64
    kc = np.arange(128) % 64
    qr = np.arange(128) // 64
    qc = np.arange(128) % 64
    dl = np.arange(-3, 4)
    dr = np.clip(2 * dl[None, :, None] + jr[:, None, None] - qr[None, None, :] + 7, 0, 14)
    dc = np.clip(kc[:, None] - qc[None, :] + 15, 0, 30)
    biasT = rpb[:, dr, dc[:, None, :]]
    biasT = np.ascontiguousarray(biasT.transpose(1, 2, 0, 3)).astype(np.float32)
    gvec = np.concatenate([_chunks(inputs["norm1_g"][0]), _chunks(inputs["norm1_g"][1]),
                           _chunks(inputs["norm2_g"][0]), _chunks(inputs["norm2_g"][1])], axis=1)
    qkg = np.stack([np.tile(f32(inputs["q_norm_g"])[0], 2), np.tile(f32(inputs["k_norm_g"])[0], 2)], axis=1)
    shared = {
        "gvec": np.ascontiguousarray(gvec), "qkg": np.ascontiguousarray(qkg.astype(np.float32)),
        "w_mod": f32(inputs["w_mod"]), "b_mod": f32(inputs["b_mod"]),
        "norm1_g": f32(inputs["norm1_g"]), "pool_scale": f32(inputs["pool_scale"]),
        "w_in": f32(inputs["w_in_even"])[0], "w_four": f32(inputs["w_four"])[0],
        "w_out": f32(inputs["w_out_even"])[0], "w_pool": f32(inputs["w_pool"])[0],
        "w_mlp1": f32(inputs["w_mlp1"]), "w_mlp2": f32(inputs["w_mlp2"]),
        "biasT": biasT, "cmask": ct["cmask"],
        "ident": ct["ident"], "blk": ct["blk"], "dftc": ct["dftc"], "dft1": ct["dft1"], "pm": ct["pm"],
    }
    per_hf = [_core_tables(0), _core_tables(1)]
    in_maps = []
    for core in range(8):
        b, hf = core // 2, core % 2
        qb = 0 if hf == 0 else 31
        xkv = np.zeros((NKV * 128, 1024), np.float32)
        lo_t, hi_t = qb - 3, qb - 3 + NKV
        a, e = max(lo_t, 0), min(hi_t, 64)
        xkv[(a - lo_t) * 128:(e - lo_t) * 128] = x[b, a * 128:e * 128]
        cvec = np.concatenate([_chunks(c[b]), _chunks(c_ctx)], axis=1)
        m = dict(shared)
        m.update({
            "xb": x[b], "xkv": xkv, "ctxb": ctx[b], "cvec": np.ascontiguousarray(cvec),
            "gtab": per_hf[hf][0], "vm": per_hf[hf][1],
            "pmf": ct["pmf_first"] if hf == 0 else ct["pm_main"],
            "pml": ct["pm_main"] if hf == 0 else ct["pml_last"],
        })
        in_maps.append(m)
    return in_maps


def kernel(**inputs):
    if "prog" not in _PROG_CACHE:
        _PROG_CACHE["prog"] = build_program(debug=False)
    nc = _PROG_CACHE["prog"]
    in_maps = _prepare_inputs(inputs)
    res = run_bass_kernel_spmd(nc, in_maps, core_ids=list(range(8)))
    out = np.empty((4, 8192, 1024), np.float32)
    for core in range(8):
        b, hf = core // 2, core % 2
        o = res.results[core]["out33"]
        if hf == 0:
            out[b, 0:4096] = o[0:4096]
        else:
            out[b, 4096:8192] = o[128:4224]
    return out
```

```python
import numpy as np
import concourse.bass as bass
import concourse.mybir as mybir
from contextlib import ExitStack

F32 = mybir.dt.float32
BF16 = mybir.dt.bfloat16
AF = mybir.ActivationFunctionType
ALU = mybir.AluOpType
AX = mybir.AxisListType

PE, ACT, DVE, POOL, SP = "pe", "act", "dve", "pool", "sp"
COMPUTE = (PE, ACT, DVE, POOL)
N_DMA_SEMS = 24


class Buf:
    __slots__ = ("name", "ap", "writers", "readers")

    def __init__(self, name, ap):
        self.name = name
        self.ap = ap
        self.writers = {}
        self.readers = {}

    def __getitem__(self, idx):
        return V(self, self.ap[idx])

    def v(self):
        return V(self, self.ap)


class V:
    __slots__ = ("buf", "ap")

    def __init__(self, buf, ap):
        self.buf = buf
        self.ap = ap

    def __getitem__(self, idx):
        return V(self.buf, self.ap[idx])

    def re(self, pattern, **kw):
        return V(self.buf, self.ap.rearrange(pattern, **kw))


def _ap(x):
    return x.ap if isinstance(x, (V,)) else (x.ap if isinstance(x, Buf) else x)


def _buf(x):
    if isinstance(x, V):
        return x.buf
    if isinstance(x, Buf):
        return x
    return None


class Prog:
    def __init__(self, nc):
        self.nc = nc
        self.stack = ExitStack()
        self.ops = {e: [] for e in (PE, ACT, DVE, POOL, SP)}
        self.know = {e: {} for e in (PE, ACT, DVE, POOL, SP)}
        self.sems = {}
        for e in COMPUTE:
            self.sems[e] = self.stack.enter_context(nc.semaphore("c_" + e))
        self.dsems = [self.stack.enter_context(nc.semaphore("d%d" % i)) for i in range(N_DMA_SEMS)]
        self.dcount = [0] * N_DMA_SEMS
        self.dlast_issuer = [None] * N_DMA_SEMS
        self.drr = 0
        self.n_alloc = 0
        self.out_tokens = []

    def sbuf(self, name, shape, dtype):
        t = self.stack.enter_context(self.nc.sbuf_tensor(name, list(shape), dtype))
        return t

    def psum(self, name, shape, dtype=F32):
        t = self.stack.enter_context(self.nc.psum_tensor(name, list(shape), dtype))
        return t

    def sbuf_buf(self, name, shape, dtype):
        t = self.sbuf(name, shape, dtype)
        return Buf(name, t[tuple(slice(None) for _ in shape)])

    def psum_buf(self, name, shape, dtype=F32):
        t = self.psum(name, shape, dtype)
        return Buf(name, t[tuple(slice(None) for _ in shape)])

    def dram(self, name, shape, dtype, kind="Internal"):
        t = self.nc.dram_tensor(name, list(shape), dtype, kind=kind)
        return t

    def _deps(self, eng, reads, writes):
        toks = []
        for r in reads:
            b = _buf(r)
            if b is None:
                continue
            for k, t in b.writers.items():
                toks.append((t, "raw"))
        for w in writes:
            b = _buf(w)
            if b is None:
                continue
            for k, t in b.writers.items():
                toks.append((t, "waw"))
            for k, t in b.readers.items():
                toks.append((t, "war"))
        need = {}
        for t, kind in toks:
            if t[0] == "c":
                src = t[1]
                if src == eng:
                    if eng == PE:
                        continue
                key = ("c", src)
            else:
                key = ("d", t[1])
            if need.get(key, -1) < t[2]:
                need[key] = t[2]
        waits = []
        kn = self.know[eng]
        for key, val in need.items():
            if kn.get(key, -1) >= val:
                continue
            kn[key] = val
            waits.append((key, val))
        return waits

    def _mark(self, tok, reads, writes):
        for r in reads:
            b = _buf(r)
            if b is None:
                continue
            k = tok[:2]
            b.readers[k] = tok
        for w in writes:
            b = _buf(w)
            if b is None:
                continue
            b.writers = {tok[:2]: tok}
            b.readers = {}

    def op(self, eng, fn, reads=(), writes=()):
        waits = self._deps(eng, reads, writes)
        idx = len(self.ops[eng])
        rec = dict(fn=fn, waits=waits, signal=False, dma=None)
        self.ops[eng].append(rec)
        for key, val in waits:
            if key[0] == "c":
                self.ops[key[1]][val]["signal"] = True
        tok = ("c", eng, idx)
        self._mark(tok, reads, writes)
        return tok

    def dma(self, out, in_, queue=SP, **kw):
        reads, writes = [in_], [out]
        waits = self._deps(queue, reads, writes)
        s = self.drr
        self.drr = (self.drr + 1) % N_DMA_SEMS
        prev = self.dcount[s]
        if prev > 0:
            kn = self.know[queue]
            if kn.get(("d", s), -1) < prev:
                kn[("d", s)] = prev
                waits.append((("d", s), prev))
        self.dcount[s] = prev + 1
        o, i = _ap(out), _ap(in_)
        rec = dict(fn=lambda e: e.dma_start(out=o, in_=i, **kw), waits=waits, signal=False,
                   dma=(s, prev + 1))
        for key, val in waits:
            if key[0] == "c":
                self.ops[key[1]][val]["signal"] = True
        self.ops[queue].append(rec)
        tok = ("d", s, prev + 1)
        self._mark(tok, reads, writes)
        return tok

    def final_wait(self, eng, bufs):
        waits = self._deps(eng, bufs, [])
        rec = dict(fn=None, waits=waits, signal=False, dma=None)
        for key, val in waits:
            if key[0] == "c":
                self.ops[key[1]][val]["signal"] = True
        self.ops[eng].append(rec)

    def emit(self):
        nc = self.nc
        sigord = {}
        for e in COMPUTE:
            c = 0
            for i, rec in enumerate(self.ops[e]):
                if rec["signal"]:
                    c += 1
                    sigord[(e, i)] = c
        engobj = {PE: "tensor", ACT: "scalar", DVE: "vector", POOL: "gpsimd", SP: "sync"}

        def run(engname):
            def body(eng):
                for i, rec in enumerate(self.ops[engname]):
                    for key, val in rec["waits"]:
                        if key[0] == "c":
                            eng.wait_ge(self.sems[key[1]], sigord[(key[1], val)])
                        else:
                            eng.wait_ge(self.dsems[key[1]], 16 * val)
                    if rec["fn"] is None:
                        continue
                    ins = rec["fn"](eng)
                    if rec["dma"] is not None:
                        ins.then_inc(self.dsems[rec["dma"][0]], 16)
                    elif rec["signal"]:
                        ins.then_inc(self.sems[engname], 1)
            return body

        with nc.Block() as block:
            block.tensor(run(PE))
            block.scalar(run(ACT))
            block.vector(run(DVE))
            block.gpsimd(run(POOL))
            block.sync(run(SP))
        self.stack.close()

    def mm(self, out, lhsT, rhs, start=True, stop=True, **kw):
        o, l, r = _ap(out), _ap(lhsT), _ap(rhs)
        return self.op(PE, lambda e: e.matmul(o, l, r, start=start, stop=stop, **kw),
                       reads=[lhsT, rhs], writes=[out])

    def tr(self, out, in_, ident):
        o, i, d = _ap(out), _ap(in_), _ap(ident)
        return self.op(PE, lambda e: e.transpose(o, i, d), reads=[in_, ident], writes=[out])

    def act(self, out, in_, func, bias=None, scale=None, accum_out=None, eng=ACT):
        o, i = _ap(out), _ap(in_)
        kw = {}
        reads = [in_]
        writes = [out]
        if bias is not None:
            kw["bias"] = _ap(bias)
            reads.append(bias)
        if scale is not None:
            kw["scale"] = _ap(scale)
            reads.append(scale)
        if accum_out is not None:
            kw["accum_out"] = _ap(accum_out)
            writes.append(accum_out)
        return self.op(ACT, lambda e: e.activation(o, i, func, **kw), reads=reads, writes=writes)

    def ts(self, eng, out, in0, s1, s2, op0, op1=None, accum_out=None):
        o, i = _ap(out), _ap(in0)
        a1, a2 = _ap(s1), _ap(s2)
        reads = [in0, s1, s2]
        writes = [out]
        kw = {}
        if op1 is not None:
            kw["op1"] = op1
        if accum_out is not None:
            kw["accum_out"] = _ap(accum_out)
            writes.append(accum_out)
        return self.op(eng, lambda e: e.tensor_scalar(o, i, a1, a2, op0, **kw), reads=reads, writes=writes)

    def tt(self, eng, out, in0, in1, op):
        o, a, b = _ap(out), _ap(in0), _ap(in1)
        return self.op(eng, lambda e: e.tensor_tensor(o, a, b, op), reads=[in0, in1], writes=[out])

    def stt(self, eng, out, in0, scalar, in1, op0, op1):
        o, a, s, b = _ap(out), _ap(in0), _ap(scalar), _ap(in1)
        return self.op(eng, lambda e: e.scalar_tensor_tensor(o, a, s, b, op0, op1),
                       reads=[in0, scalar, in1], writes=[out])

    def copy(self, eng, out, in_):
        o, i = _ap(out), _ap(in_)
        if eng == ACT:
            return self.op(ACT, lambda e: e.copy(o, i), reads=[in_], writes=[out])
        return self.op(eng, lambda e: e.tensor_copy(o, i), reads=[in_], writes=[out])

    def memset(self, eng, out, val):
        o = _ap(out)
        return self.op(eng, lambda e: e.memset(o, val), reads=[], writes=[out])

    def recip(self, out, in_):
        o, i = _ap(out), _ap(in_)
        return self.op(DVE, lambda e: e.reciprocal(o, i), reads=[in_], writes=[out])


ARENA_BYTES = 207 * 1024
VERBOSE = False
NSLOT = 33
NKV = 38
POOL_SIZES = (2, 4, 8, 16)


def _arena_setup(p):
    p.arena_t = p.sbuf("arena", [128, ARENA_BYTES // 4], F32)
    p.arena_off = 0

    def alloc(name, shape, dtype):
        esz = 4 if dtype == F32 else 2
        n = 1
        for s in shape[1:]:
            n *= s
        nbytes = (n * esz + 63) // 64 * 64
        off = p.arena_off
        assert off + nbytes <= ARENA_BYTES, ("arena overflow", name, off, nbytes)
        p.arena_off = off + nbytes
        p.arena_peak = max(getattr(p, "arena_peak", 0), p.arena_off)
        ap = p.arena_t[0:shape[0], off // 4:(off + nbytes) // 4]
        if dtype != F32:
            ap = ap.bitcast(dtype)
        ap = ap[:, 0:n]
        if len(shape) == 3:
            ap = ap.rearrange("p (a b) -> p a b", a=shape[1])
        elif len(shape) == 4:
            ap = ap.rearrange("p (a b c) -> p a b c", a=shape[1], b=shape[2])
        elif len(shape) == 5:
            ap = ap.rearrange("p (a b c d) -> p a b c d", a=shape[1], b=shape[2], c=shape[3])
        return Buf(name, ap)

    p.alloc = alloc


def _barrier(p):
    if VERBOSE:
        print("arena peak at barrier: %.1f KB" % (getattr(p, "arena_peak", 0) / 1024.0))
        p.arena_peak = 0
    for e in (PE, ACT, DVE, POOL, SP):
        waits = []
        kn = p.know[e]
        for src in COMPUTE:
            if src == e or not p.ops[src]:
                continue
            idx = len(p.ops[src]) - 1
            while idx >= 0 and p.ops[src][idx]["fn"] is None:
                idx -= 1
            if idx < 0:
                continue
            if kn.get(("c", src), -1) < idx:
                kn[("c", src)] = idx
                waits.append((("c", src), idx))
                p.ops[src][idx]["signal"] = True
        for s in range(N_DMA_SEMS):
            if p.dcount[s] > 0 and kn.get(("d", s), -1) < p.dcount[s]:
                kn[("d", s)] = p.dcount[s]
                waits.append((("d", s), p.dcount[s]))
        p.ops[e].append(dict(fn=None, waits=waits, signal=False, dma=None))


def _dbuf(t, *idx):
    return t


def build_program(debug=False, upto=99):
    nc = bass.Bass("TRN2", target_bir_lowering=False)
    p = Prog(nc)
    _arena_setup(p)
    alloc = p.alloc

    def din(name, shape, dt=F32):
        return nc.dram_tensor(name, list(shape), dt, kind="ExternalInput")

    def D(name, t):
        return Buf(name, t.ap())

    xb_t = din("xb", [8192, 1024]); xb = D("xb", xb_t)
    xkv_t = din("xkv", [NKV * 128, 1024]); xkv = D("xkv", xkv_t)
    ctx_t = din("ctxb", [256, 1024]); ctxb = D("ctxb", ctx_t)
    cvec_t = din("cvec", [128, 16]); cvec = D("cvec", cvec_t)
    gvec_t = din("gvec", [128, 32]); gvec = D("gvec", gvec_t)
    qkg_t = din("qkg", [128, 2]); qkg = D("qkg", qkg_t)
    wmod_t = din("w_mod", [2, 1024, 6144]); wmod = D("w_mod", wmod_t)
    bmod_t = din("b_mod", [2, 6144]); bmod = D("b_mod", bmod_t)
    n1g_t = din("norm1_g", [2, 1024]); n1g = D("norm1_g", n1g_t)
    psc_t = din("pool_scale", [1, 1024]); psc = D("pool_scale", psc_t)
    win_t = din("w_in", [1024, 2048]); win = D("w_in", win_t)
    wfour_t = din("w_four", [4, 128, 128]); wfour = D("w_four", wfour_t)
    wout_t = din("w_out", [1024, 1024]); wout = D("w_out", wout_t)
    wpool_t = din("w_pool", [4, 256, 256]); wpool = D("w_pool", wpool_t)
    w1_t = din("w_mlp1", [2, 1024, 4096]); w1d = D("w_mlp1", w1_t)
    w2_t = din("w_mlp2", [2, 4096, 1024]); w2d = D("w_mlp2", w2_t)
    biasT_t = din("biasT", [128, 7, 8, 128]); biasT = D("biasT", biasT_t)
    cmask_t = din("cmask", [128, 128]); cmask = D("cmask", cmask_t)
    vm_t = din("vm", [128, NSLOT * 14]); vmd = D("vm", vm_t)
    ident_t = din("ident", [128, 128], BF16); identd = D("ident", ident_t)
    blk_t = din("blk", [128, 128], BF16); blkd = D("blk", blk_t)
    dftc_t = din("dftc", [128, 256], BF16); dftcd = D("dftc", dftc_t)
    dft1_t = din("dft1", [128, 256], BF16); dft1d = D("dft1", dft1_t)
    gtab_t = din("gtab", [128, 2, 128, NSLOT], BF16); gtabd = D("gtab", gtab_t)
    pm_t = din("pm", [128, 4, 3, 128], BF16); pmd = D("pm", pm_t)
    pmf_t = din("pmf", [128, 4, 128], BF16); pmfd = D("pmf", pmf_t)
    pml_t = din("pml", [128, 4, 128], BF16); pmld = D("pml", pml_t)

    out_t = nc.dram_tensor("out33", [NSLOT * 128, 1024], F32, kind="ExternalOutput"); outd = D("out33", out_t)

    modrow_t = nc.dram_tensor("modrow", [2, 2, 6144], F32); modrow = D("modrow", modrow_t)
    ofd_t = nc.dram_tensor("ofd", [4, 128, NSLOT * 128], BF16)
    ofd = [Buf("ofd%d" % g, ofd_t.ap()[g]) for g in range(4)]
    hmid_t = nc.dram_tensor("hmid", [NSLOT * 128, 1024], F32)
    hmid = [Buf("hmid%d" % s, hmid_t.ap()[s * 128:(s + 1) * 128, :]) for s in range(NSLOT)]
    h1_t = nc.dram_tensor("h1", [NSLOT * 128, 1024], F32)
    h1 = [Buf("h1_%d" % s, h1_t.ap()[s * 128:(s + 1) * 128, :]) for s in range(NSLOT)]
    outs = [Buf("out_%d" % s, out_t.ap()[s * 128:(s + 1) * 128, :]) for s in range(NSLOT)]
    dbg = {}
    if debug:
        def dout(name, shape, dt=F32):
            t = nc.dram_tensor(name, list(shape), dt, kind="ExternalOutput")
            dbg[name] = D(name, t)
            return dbg[name]
        dout("d_mod", [2, 2, 6144])
        dout("d_fT", [128, 4, 8192], BF16)
        dout("d_of", [4, 128, NSLOT * 128], BF16)
        dout("d_hmid", [NSLOT * 128, 1024])
        dout("d_h1", [NSLOT * 128, 1024])
        dout("d_hmid1", [NSLOT * 128, 1024])
        dout("d_kT", [128, 4, NKV * 128], BF16)
        dout("d_ona", [NSLOT * 128, 512], BF16)

    banks = [p.psum_buf("bank%d" % i, [128, 512], F32) for i in range(8)]

    def bank_bf(i):
        return V(banks[i], banks[i].ap.bitcast(BF16))

    ident = alloc("ident", [128, 128], BF16)
    blk = alloc("blk", [128, 128], BF16)
    cv = alloc("cv", [128, 16], F32)
    sT = alloc("sT", [128, 16], F32)
    gv = alloc("gv", [128, 32], F32)
    qk_g = alloc("qk_g", [128, 2], F32)
    epsb = alloc("epsb", [128, 1], F32)
    modT = alloc("modT", [128, 3, 48], F32)
    avec = alloc("avec", [128, 5, 8], F32)
    p.dma(ident, identd)
    p.dma(blk, blkd)
    p.dma(cv, cvec)
    p.dma(gv, gvec)
    p.dma(qk_g, qkg)
    p.memset(DVE, epsb, 1e-6)

    mk0 = p.arena_off
    p.act(sT, cv, AF.Silu)
    wm = [alloc("wm%d" % i, [128, 3072], F32) for i in range(2)]
    mrow = alloc("mrow", [2, 6144], F32)
    brow = alloc("brow", [2, 6144], F32)
    it = 0
    for l in range(2):
        p.dma(brow, V(bmod, bmod_t.ap()[l, :].partition_broadcast(2)))
        for half in range(2):
            for c in range(8):
                w = wm[it % 2]; it += 1
                p.dma(w, V(wmod, wmod_t.ap()[l, c * 128:(c + 1) * 128, half * 3072:(half + 1) * 3072]))
                for n in range(6):
                    p.mm(banks[n][0:2, :], sT[:, c:c + 9:8], w[:, n * 512:(n + 1) * 512],
                         start=(c == 0), stop=(c == 7))
            for n in range(6):
                col = half * 3072 + n * 512
                p.tt(DVE, mrow[:, col:col + 512], banks[n][0:2, :], brow[:, col:col + 512], ALU.add)
        p.dma(V(modrow, modrow_t.ap()[l]), mrow)
        if debug:
            p.dma(V(dbg["d_mod"], dbg["d_mod"].ap[l]), mrow)
    for i, (l, r) in enumerate(((0, 0), (0, 1), (1, 0))):
        src = modrow_t.ap()[l, r, :].rearrange("(j q) -> q j", q=128)
        p.dma(modT[:, i, :], V(modrow, src), allow_slow_non_contiguous=True)
    def mk_a(dst, gcol, mi, scoff):
        p.stt(DVE, avec[:, dst, :], modT[:, mi, scoff:scoff + 8], 1.0, gv[:, gcol:gcol + 8], ALU.add, ALU.mult)
    mk_a(0, 0, 0, 8)
    mk_a(1, 0, 1, 8)
    mk_a(2, 16, 0, 32)
    mk_a(3, 8, 2, 8)
    mk_a(4, 24, 2, 32)
    _barrier(p)
    p.arena_off = mk0

    if upto == 0:
        p.final_wait(SP, outs + list(dbg.values()))
        p.emit()
        return nc

    def fe_a(src_views, work):
        nt = len(src_views)
        ss = work["ss"][work["i"] % 2]
        rs = work["rs"][work["i"] % 2]
        work["i"] += 1
        xts, xns = [], []
        for t, sv in enumerate(src_views):
            xt = work["xt"][work["xi"] % len(work["xt"])]; work["xi"] += 1
            xn = work["xn"][work["ni"] % len(work["xn"])]; work["ni"] += 1
            p.dma(xt, sv)
            p.act(xn, xt, AF.Square, accum_out=ss[:, t:t + 1])
            xts.append(xt); xns.append(xn)
        p.act(rs[:, 0:nt], ss[:, 0:nt], AF.Sqrt, bias=epsb, scale=1.0 / 1024)
        p.recip(rs[:, 0:nt], rs[:, 0:nt])
        for t in range(nt):
            p.ts(DVE, xns[t], xts[t], rs[:, t:t + 1], None, ALU.mult)
        return xns, xts

    def fe_b(xns, a_v, b_v, uT, work):
        for t, xn in enumerate(xns):
            bk = work["trbank"][work["ti"] % len(work["trbank"])]; work["ti"] += 1
            pv = bank_bf(bk)
            for c in range(8):
                p.tr(pv[:, c * 128:(c + 1) * 128], xn[:, c * 128:(c + 1) * 128], ident)
            for c in range(8):
                p.act(uT[:, c, t * 128:(t + 1) * 128], pv[:, c * 128:(c + 1) * 128], AF.Identity,
                      bias=b_v[:, c:c + 1], scale=a_v[:, c:c + 1])

    def front_end(src_views, a_v, b_v, uT, work, tok_major=None):
        xns, xts = fe_a(src_views, work)
        fe_b(xns, a_v, b_v, uT, work)
        return xts, None

    def mk_work(nxt, trbanks, nxn=4, ntu=2):
        return dict(xt=[alloc("xt%d" % i, [128, 1024], F32) for i in range(nxt)],
                    xn=[alloc("xn%d" % i, [128, 1024], BF16) for i in range(nxn)],
                    ss=[alloc("ss%d" % i, [128, 4], F32) for i in range(2)],
                    rs=[alloc("rs%d" % i, [128, 4], F32) for i in range(2)],
                    tu=[alloc("tu%d" % i, [128, 4, 128], F32) for i in range(ntu)],
                    trbank=trbanks, i=0, xi=0, ni=0, ti=0)

    def load_cast(dst_view, src_view, stg_list, cnt, shape_note=None):
        stg = stg_list[cnt[0] % len(stg_list)]
        eng = (DVE, POOL, ACT)[cnt[0] % 3]
        cnt[0] += 1
        return stg, eng

    fT = alloc("fT", [128, 4, 8192], BF16)
    mk1 = p.arena_off
    work = mk_work(6, [6, 7])
    winf_s = alloc("winf_s", [128, 8, 512], F32)
    winf = alloc("winf", [128, 8, 512], BF16)
    p.dma(winf_s, V(win, win_t.ap()[:, 0:512].rearrange("(c q) n -> q c n", q=128)))
    p.copy(DVE, winf, winf_s)
    uTs = [alloc("uT%d" % i, [128, 8, 512], BF16) for i in range(2)]
    for grp in range(16):
        uT = uTs[grp % 2]
        svs = [V(xb, xb_t.ap()[(grp * 4 + t) * 128:(grp * 4 + t + 1) * 128, :]) for t in range(4)]
        front_end(svs, avec[:, 0, :], modT[:, 0, 0:8], uT, work)
        for g in range(4):
            bk = banks[g % 4]
            for c in range(8):
                p.mm(bk, winf[:, c, g * 128:(g + 1) * 128], uT[:, c, :], start=(c == 0), stop=(c == 7))
            if g % 2 == 0:
                p.copy(DVE, fT[:, g, grp * 512:(grp + 1) * 512], bk)
            else:
                p.copy(ACT, fT[:, g, grp * 512:(grp + 1) * 512], bk)
    if debug:
        p.dma(dbg["d_fT"], fT)
    _barrier(p)
    p.arena_off = mk1

    if upto == 1:
        p.final_wait(SP, outs + list(dbg.values()))
        p.emit()
        return nc

    dftc = alloc("dftc", [128, 256], BF16)
    dft1 = alloc("dft1", [128, 256], BF16)
    gtab = alloc("gtab", [128, 2, 128, NSLOT], BF16)
    wf_s = alloc("wf_s", [128, 4, 128], F32)
    wf_b = alloc("wf_b", [128, 4, 128], BF16)
    Mg = alloc("Mg", [128, 4, 256], BF16)
    AB = alloc("AB", [128, 64, 256], BF16)
    P1 = [alloc("P1_%d" % i, [128, 128, 128], BF16) for i in range(2)]
    ofs = [alloc("ofs%d" % i, [128, NSLOT, 128], BF16) for i in range(2)]
    p.dma(dftc, dftcd)
    p.dma(dft1, dft1d)
    p.dma(gtab, gtabd)
    p.dma(wf_s, V(wfour, wfour_t.ap().rearrange("g c d -> c g d")))
    p.copy(DVE, wf_b, wf_s)
    for g in range(4):
        for cs in range(2):
            p.mm(banks[0][:, cs * 128:(cs + 1) * 128], dftc[:, cs * 128:(cs + 1) * 128], wf_b[:, g, :])
        p.copy(DVE, Mg[:, g, :], banks[0][:, 0:256])
    ev = 0
    for g in range(4):
        for n2p in range(32):
            bk = banks[n2p % 2]
            for k in range(2):
                n2 = n2p * 2 + k
                p.mm(bk[:, k * 256:(k + 1) * 256], fT[:, g, n2:8192:64], Mg[:, g, :])
            p.copy(DVE if n2p % 2 == 0 else ACT, AB[:, n2p * 2:n2p * 2 + 2, :].re("q a b -> q (a b)"), bk)
        ofsg = ofs[g % 2]
        for half in range(2):
            P1h = P1[half]
            for d4 in range(32):
                bk = banks[2 + d4 % 2]
                for k in range(4):
                    d = d4 * 4 + k
                    p.mm(bk[:, k * 128:(k + 1) * 128], AB[:, :, d:256:128].re("q a b -> q (a b)"),
                         dft1[:, half * 128:(half + 1) * 128])
                p.copy(DVE if d4 % 2 == 0 else ACT, P1h[:, d4 * 4:d4 * 4 + 4, :].re("q a b -> q (a b)"), bk)
            for k8 in range(8):
                bk = banks[4 + k8 % 2]
                for kk in range(8):
                    k1h = k8 * 8 + kk
                    k1 = half * 64 + k1h
                    for cs in range(2):
                        p.mm(bk[:, kk * NSLOT:(kk + 1) * NSLOT], P1h[:, :, cs * 64 + k1h],
                             gtab[:, cs, k1, :], start=(cs == 0), stop=(cs == 1))
                k1b = half * 64 + k8 * 8
                src = bk[:, 0:8 * NSLOT].re("q (a b) -> q b a", a=8)
                p.copy(DVE if k8 % 2 == 0 else POOL if False else ACT, ofsg[:, :, k1b:k1b + 8], src)
        p.dma(ofd[g], ofsg.v().re("q a b -> q (a b)"))
        if debug:
            p.dma(V(dbg["d_of"], dbg["d_of"].ap[g]), ofsg.v().re("q a b -> q (a b)"))
    _barrier(p)
    p.arena_off = mk1 - 0
    p.arena_off = mk0

    if upto == 2:
        p.final_wait(SP, outs + list(dbg.values()))
        p.emit()
        return nc

    kT = alloc("kT", [128, 4, NKV * 128], BF16)
    qT = alloc("qT", [128, 4, 36 * 128], BF16)
    Vx = alloc("Vx", [128, NKV, 8, 66], BF16)
    kcT = alloc("kcT", [128, 4, 256], BF16)
    Vc = alloc("Vc", [128, 2, 8, 66], BF16)
    mk3 = p.arena_off
    work = mk_work(5, [6, 7])
    wq_s = [alloc("wq_s%d" % i, [128, 512], F32) for i in range(3)]
    wqkv = alloc("wqkv", [128, 8, 1536], BF16)
    cc = 0
    for c in range(8):
        for j in range(3):
            stg = wq_s[cc % 3]
            p.dma(stg, V(win, win_t.ap()[c * 128:(c + 1) * 128, 512 + j * 512:1024 + j * 512]))
            p.copy((DVE, ACT)[cc % 2], wqkv[:, c, j * 512:(j + 1) * 512], stg)
            cc += 1
    p.memset(POOL, Vx[:, :, :, 64:66], 1.0)
    p.memset(POOL, Vc[:, :, :, 64:66], 1.0)
    uTs = [alloc("uT%d" % i, [128, 8, 512], BF16) for i in range(2)]
    sqb = [alloc("sqb%d" % i, [128, 512], BF16) for i in range(2)]
    rqk = [alloc("rqk%d" % i, [128, 512], F32) for i in range(2)]
    groups = [(i * 4, 4) for i in range(9)] + [(36, 2)]
    gi = 0

    def qk_proj(uT, ncol, woff, gcol, dstT, dcol):
        nonlocal gi
        for hp in range(4):
            bk = banks[hp % 2]
            for c in range(8):
                p.mm(bk[:, 0:ncol], wqkv[:, c, woff + hp * 128:woff + (hp + 1) * 128], uT[:, c, 0:ncol],
                     start=(c == 0), stop=(c == 7))
            sq = sqb[gi % 2]; rq = rqk[gi % 2]; gi += 1
            p.act(sq[:, 0:ncol], bk[:, 0:ncol], AF.Square)
            b2 = banks[2 + hp % 2]
            p.mm(b2[:, 0:ncol], blk, sq[:, 0:ncol])
            p.act(rq[:, 0:ncol], b2[:, 0:ncol], AF.Sqrt, bias=epsb, scale=1.0 / 64)
            p.recip(rq[:, 0:ncol], rq[:, 0:ncol])
            p.stt(DVE, dstT[:, hp, dcol:dcol + ncol], bk[:, 0:ncol], qk_g[:, gcol:gcol + 1], rq[:, 0:ncol],
                  ALU.mult, ALU.mult)

    def v_proj(uT, nt, dstV, slot0):
        for t in range(nt):
            bk = banks[4 + t % 2]
            for c in range(8):
                p.mm(bk, uT[:, c, t * 128:(t + 1) * 128], wqkv[:, c, 1024:1536], start=(c == 0), stop=(c == 7))
            p.copy(ACT if t % 2 == 0 else DVE, dstV[:, slot0 + t, :, 0:64], bk.v().re("q (h e) -> q h e", h=8))

    for gidx, (t0, nt) in enumerate(groups):
        uT = uTs[gidx % 2]
        svs = [V(xkv, xkv_t.ap()[(t0 + t) * 128:(t0 + t + 1) * 128, :]) for t in range(nt)]
        front_end(svs, avec[:, 0, :], modT[:, 0, 0:8], uT, work)
        ncol = nt * 128
        if t0 < 36:
            qk_proj(uT, ncol, 0, 0, qT, t0 * 128)
        qk_proj(uT, ncol, 512, 1, kT, t0 * 128)
        v_proj(uT, nt, Vx, t0)
    uT = uTs[0]
    svs = [V(ctxb, ctx_t.ap()[t * 128:(t + 1) * 128, :]) for t in range(2)]
    front_end(svs, avec[:, 1, :], modT[:, 1, 0:8], uT, work)
    qk_proj(uT, 256, 512, 1, kcT, 0)
    v_proj(uT, 2, Vc, 0)
    if debug:
        p.dma(dbg["d_kT"], kT)
    _barrier(p)
    p.arena_off = mk3

    if upto == 3:
        p.final_wait(SP, outs + list(dbg.values()))
        p.emit()
        return nc

    TE = alloc("TE", [128, 7, 8, 128], BF16)
    vm = alloc("vm", [128, NSLOT * 14], F32)
    cm = alloc("cm", [128, 128], F32)
    wo_b = alloc("wo_b", [128, 8, 1024], BF16)
    PT = [alloc("PT%d" % i, [128, 8, 8, 128], BF16) for i in range(2)]
    eS = [alloc("eS%d" % i, [128, 8, 128], BF16) for i in range(2)]
    xts = [alloc("xq%d" % i, [128, 1024], F32) for i in range(2)]
    hms = [alloc("hm%d" % i, [128, 1024], F32) for i in range(1)]
    g1bc = xts[0]
    qm = [alloc("qm%d" % i, [128, 4, 2, 128], BF16) for i in range(1)]
    ofT = [alloc("ofT%d" % i, [128, 4, 128], BF16) for i in range(2)]
    otok = alloc("otok", [128, 8, 64], BF16)
    onaT = alloc("onaT", [128, 4, 128], BF16)
    rden = alloc("rden", [128, 8], F32)
    p.dma(vm, vmd)
    p.dma(cm, cmask)
    p.dma(g1bc, V(modrow, modrow_t.ap()[0, 0, 2048:3072].partition_broadcast(128)))
    for c in range(8):
        stg = (hms[0], xts[1])[c % 2]
        p.dma(stg, V(wout, wout_t.ap()[c * 128:(c + 1) * 128, :]))
        p.tt(DVE, wo_b[:, c, :], stg, g1bc, ALU.mult)
    for dl in range(7):
        stg = (hms[0], xts[1])[dl % 2].v().re("q (a b) -> q a b", a=8)
        p.dma(stg, V(biasT, biasT_t.ap()[:, dl]))
        p.act(stg, stg, AF.Exp)
        p.tt(DVE, TE[:, dl], stg, V(cm, cm.ap.unsqueeze(1).broadcast_to([128, 8, 128])), ALU.mult)
    for qq in qm:
        p.memset(POOL, qq, 0.0)
    ei_box = [0]

    def slot_chunks(s):
        if s <= 1:
            dls = [-2, -1, 0, 1, 2, 3]
        elif s >= NSLOT - 2:
            dls = [-3, -2, -1, 0, 1, 2]
        else:
            dls = [-2, -1, 0, 1, 2]
        return [("w", dl) for dl in dls] + [("c", 0), ("c", 1)]

    def stage_S(s):
        chunks = slot_chunks(s)
        pt = PT[s % 2]
        qs = s + 3
        xq = xts[s % 2]; of_t = ofT[s % 2]
        qms = qm[0]
        p.copy(POOL, qms[0:64, :, 0, :], qT[0:64, :, qs * 128:(qs + 1) * 128])
        p.copy(POOL, qms[64:128, :, 1, :], qT[64:128, :, qs * 128:(qs + 1) * 128])
        p.dma(xq, V(xkv, xkv_t.ap()[qs * 128:(qs + 1) * 128, :]))
        for g in range(4):
            p.dma(of_t[:, g, :], V(ofd[g], ofd[g].ap[:, s * 128:(s + 1) * 128]))
        for j, (kind, dl) in enumerate(chunks):
            ei = ei_box[0]
            pair = (ei % 2) * 2
            e_s = eS[ei % 2]; ei_box[0] += 1
            for h in range(8):
                hp, hh = h // 2, h % 2
                bk = banks[pair + h // 4]
                if kind == "w":
                    ks = s + 3 + dl
                    lhs = kT[:, hp, ks * 128:(ks + 1) * 128]
                else:
                    lhs = kcT[:, hp, dl * 128:(dl + 1) * 128]
                p.mm(bk[:, (h % 4) * 128:(h % 4 + 1) * 128], lhs, qms[:, hp, hh, :])
            for hb in range(2):
                dst = e_s if kind == "w" else pt[:, j]
                p.act(dst[:, hb * 4:(hb + 1) * 4, :], banks[pair + hb].v().re("q (h e) -> q h e", h=4),
                      AF.Exp, scale=0.125)
            if kind == "w":
                for qr in range(2):
                    col = s * 14 + (dl + 3) * 2 + qr
                    p.stt(DVE, pt[:, j, :, qr * 64:(qr + 1) * 64],
                          e_s[:, :, qr * 64:(qr + 1) * 64], vm[:, col:col + 1],
                          TE[:, dl + 3, :, qr * 64:(qr + 1) * 64], ALU.mult, ALU.mult)

    stage_S(0)
    for s in range(NSLOT):
        if s + 1 < NSLOT:
            stage_S(s + 1)
        chunks = slot_chunks(s)
        pt = PT[s % 2]
        qs = s + 3
        xq = xts[s % 2]; hm = hms[0]; of_t = ofT[s % 2]
        nch = len(chunks)
        for h in range(8):
            bk = banks[4 + h // 4]
            for j, (kind, dl) in enumerate(chunks):
                if kind == "w":
                    rhs = Vx[:, s + 3 + dl, h, :]
                else:
                    rhs = Vc[:, dl, h, :]
                p.mm(bk[:, (h % 4) * 66:(h % 4 + 1) * 66], pt[:, j, h, :], rhs, start=(j == 0), stop=(j == nch - 1))
        for hb in range(2):
            bv = banks[4 + hb][:, 0:264].re("q (h e) -> q h e", h=4)
            p.recip(rden[:, hb * 4:(hb + 1) * 4], bv[:, :, 64])
            p.tt(DVE, otok[:, hb * 4:(hb + 1) * 4, :], bv[:, :, 0:64],
                 V(rden, rden.ap[:, hb * 4:(hb + 1) * 4].unsqueeze(2).broadcast_to([128, 4, 64])), ALU.mult)
        if debug:
            p.dma(V(dbg["d_ona"], dbg["d_ona"].ap[s * 128:(s + 1) * 128, :]), otok.v().re("q h e -> q (h e)"))
        ptr_v = bank_bf(6)
        for c in range(4):
            p.tr(ptr_v[:, c * 128:(c + 1) * 128], otok.v().re("q h e -> q (h e)")[:, c * 128:(c + 1) * 128], ident)
        p.copy(ACT, onaT.v().re("q a b -> q (a b)"), ptr_v[:, 0:512])
        for nh in range(2):
            bk = banks[6 + nh]
            for c in range(8):
                lhs = of_t[:, c, :] if c < 4 else onaT[:, c - 4, :]
                p.mm(bk, lhs, wo_b[:, c, nh * 512:(nh + 1) * 512], start=(c == 0), stop=(c == 7))
            p.tt(DVE, hm[:, nh * 512:(nh + 1) * 512], bk, xq[:, nh * 512:(nh + 1) * 512], ALU.add)
        p.dma(hmid[s], hm)
        if debug:
            p.dma(V(dbg["d_hmid"], dbg["d_hmid"].ap[s * 128:(s + 1) * 128, :]), hm)
    _barrier(p)
    p.arena_off = mk0

    if upto == 4:
        p.final_wait(SP, outs + list(dbg.values()))
        p.emit()
        return nc

    def mlp_phase(l, src_tiles, dst_tiles, a_idx, mod_i, dbg_out=None):
        mk = p.arena_off
        w1b = alloc("w1b", [128, 8, 4096], BF16)
        w2b = alloc("w2b", [128, 32, 1024], BF16)
        aT = alloc("aT", [128, 32, 384], BF16)
        aT_flat = aT.ap.rearrange("q a b -> q (a b)").bitcast(F32)
        stg = [Buf("stg%d" % i, aT_flat[:, i * 2048:(i + 1) * 2048]) for i in range(3)]
        rl = [alloc("rl%d" % i, [128, 384], BF16) for i in range(2)]
        uTm = [alloc("uTm%d" % i, [128, 8, 384], BF16) for i in range(2)]
        outb = [alloc("outb%d" % i, [128, 1024], F32) for i in range(2)]
        xr = [alloc("xr%d" % i, [128, 1024], F32) for i in range(2)]
        work = mk_work(3, [6, 7], nxn=3, ntu=1)
        g2bc = outb[0]
        p.dma(g2bc, V(modrow, modrow_t.ap()[l, 0, 5120:6144].partition_broadcast(128)))
        cc = 0
        for c in range(8):
            for hh in range(2):
                s_ = stg[cc % 3]
                p.dma(s_, V(w1d, w1_t.ap()[l, c * 128:(c + 1) * 128, hh * 2048:(hh + 1) * 2048]))
                p.copy((DVE, ACT)[cc % 2], w1b[:, c, hh * 2048:(hh + 1) * 2048], s_)
                cc += 1
        g2v = V(g2bc, g2bc.ap.unsqueeze(1).broadcast_to([128, 2, 1024]))
        for j2 in range(16):
            s_ = stg[cc % 3]
            p.dma(s_.v().re("q (a b) -> q a b", a=2),
                  V(w2d, w2_t.ap()[l, j2 * 256:(j2 + 1) * 256, :].rearrange("(a q) n -> q a n", q=128)))
            p.tt(DVE, w2b[:, j2 * 2:j2 * 2 + 2, :], s_.v().re("q (a b) -> q a b", a=2), g2v, ALU.mult)
            cc += 1
        for s_ in stg:
            aT.readers.update(s_.readers)
            aT.writers.update(s_.writers)
        ntile = len(src_tiles)
        ngrp = ntile // 3
        a_v, b_v = avec[:, a_idx, :], modT[:, mod_i, 24:32]
        xns0, _ = fe_a([src_tiles[t] for t in range(3)], work)
        fe_b(xns0, a_v, b_v, uTm[0], work)
        nxt = None
        for g in range(ngrp):
            t0 = g * 3
            uT = uTm[g % 2]
            if g + 1 < ngrp:
                nxt, _ = fe_a([src_tiles[t0 + 3 + t] for t in range(3)], work)
            for j in range(32):
                bk = banks[j % 4]
                for c in range(8):
                    p.mm(bk[:, 0:384], w1b[:, c, j * 128:(j + 1) * 128], uT[:, c, :], start=(c == 0), stop=(c == 7))
                r = rl[j % 2]
                p.act(r, bk[:, 0:384], AF.Relu)
                p.tt(DVE, aT[:, j, :], r, r, ALU.mult)
            if g + 1 < ngrp:
                fe_b(nxt, a_v, b_v, uTm[(g + 1) % 2], work)
            for t in range(3):
                ob = outb[(t0 + t) % 2]
                xx = xr[(t0 + t) % 2]
                p.dma(xx, src_tiles[t0 + t])
                for nh in range(2):
                    bk = banks[4 + nh]
                    for j in range(32):
                        p.mm(bk, aT[:, j, t * 128:(t + 1) * 128], w2b[:, j, nh * 512:(nh + 1) * 512],
                             start=(j == 0), stop=(j == 31))
                    p.tt(DVE, ob[:, nh * 512:(nh + 1) * 512], bk, xx[:, nh * 512:(nh + 1) * 512], ALU.add)
                p.dma(dst_tiles[t0 + t], ob)
                if dbg_out is not None:
                    p.dma(V(dbg_out, dbg_out.ap[(t0 + t) * 128:(t0 + t + 1) * 128, :]), ob)
        _barrier(p)
        p.arena_off = mk

    mlp_phase(0, [h.v() for h in hmid], h1, 2, 0, dbg.get("d_h1"))

    if upto == 5:
        p.final_wait(SP, outs + list(dbg.values()))
        p.emit()
        return nc

    mk5 = p.arena_off
    utok = [alloc("utok%d" % s, [128, 1024], BF16) for s in range(NSLOT)]
    a1bc = alloc("a1bc", [128, 1024], F32)
    b1bc = alloc("b1bc", [128, 1024], F32)
    gpbc = alloc("gpbc", [128, 1024], F32)
    tmp1 = alloc("tmp1", [128, 1024], F32)
    pm = alloc("pm", [128, 4, 3, 128], BF16)
    pmf = alloc("pmf", [128, 4, 128], BF16)
    pml = alloc("pml", [128, 4, 128], BF16)
    wp_s = alloc("wp_s", [128, 8, 256], F32)
    wp_b = alloc("wp_b", [128, 8, 256], BF16)
    ssl = alloc("ssl", [128, NSLOT], F32)
    rsl = alloc("rsl", [128, NSLOT], F32)
    x1 = [alloc("x1_%d" % i, [128, 1024], F32) for i in range(4)]
    tf = [alloc("tf%d" % i, [128, 1024], F32) for i in range(2)]
    pooledT = [alloc("pooledT%d" % i, [128, 8, 128], BF16) for i in range(2)]
    hm1 = [alloc("hm1_%d" % i, [128, 1024], F32) for i in range(2)]
    p.dma(pm, pmd); p.dma(pmf, pmfd); p.dma(pml, pmld)
    p.dma(wp_s, V(wpool, wpool_t.ap().rearrange("g (k q) n -> q (g k) n", q=128)))
    p.dma(a1bc, V(modrow, modrow_t.ap()[1, 0, 1024:2048].partition_broadcast(128)))
    p.dma(tmp1, V(n1g, n1g_t.ap()[1, :].partition_broadcast(128)))
    p.stt(DVE, a1bc, a1bc, 1.0, tmp1, ALU.add, ALU.mult)
    p.dma(b1bc, V(modrow, modrow_t.ap()[1, 0, 0:1024].partition_broadcast(128)))
    p.dma(gpbc, V(modrow, modrow_t.ap()[1, 0, 2048:3072].partition_broadcast(128)))
    tmp2 = alloc("tmp2", [128, 1024], F32)
    p.dma(tmp2, V(psc, psc_t.ap()[0, :].partition_broadcast(128)))
    p.tt(DVE, gpbc, gpbc, tmp2, ALU.mult)
    p.tt(DVE, wp_b.v().re("q (g k) n -> q g k n", g=4), wp_s.v().re("q (g k) n -> q g k n", g=4),
         V(gpbc, gpbc.ap.rearrange("q (g n) -> q g n", g=4).unsqueeze(2).broadcast_to([128, 4, 2, 256])), ALU.mult)
    for s in range(NSLOT):
        xt = x1[s % 4]
        p.dma(xt, h1[s])
        p.act(utok[s], xt, AF.Square, accum_out=ssl[:, s:s + 1])
        p.act(rsl[:, s:s + 1], ssl[:, s:s + 1], AF.Sqrt, bias=epsb, scale=1.0 / 1024)
        p.recip(rsl[:, s:s + 1], rsl[:, s:s + 1])
        t_ = tf[s % 2]
        p.stt(DVE, t_, xt, rsl[:, s:s + 1], a1bc, ALU.mult, ALU.mult)
        p.tt(DVE, utok[s], t_, b1bc, ALU.add)
    for s in range(NSLOT):
        pT = pooledT[s % 2]
        for c in range(8):
            w = c // 2
            bk = banks[(s % 2) * 2 + c // 4]
            dst = bk[:, (c % 4) * 128:(c % 4 + 1) * 128]
            parts = []
            if s > 0:
                parts.append((utok[s - 1], pm[:, w, 0, :]))
            if s == 0:
                parts.append((utok[s], pmf[:, w, :]))
            elif s == NSLOT - 1:
                parts.append((utok[s], pml[:, w, :]))
            else:
                parts.append((utok[s], pm[:, w, 1, :]))
            if s < NSLOT - 1:
                parts.append((utok[s + 1], pm[:, w, 2, :]))
            for i, (ut, mat) in enumerate(parts):
                p.mm(dst, ut[:, c * 128:(c + 1) * 128], mat, start=(i == 0), stop=(i == len(parts) - 1))
        for hb in range(2):
            p.copy(ACT if hb == 0 else DVE, pT[:, hb * 4:(hb + 1) * 4, :].re("q a b -> q (a b)"),
                   banks[(s % 2) * 2 + hb])
        xts1 = []
        for nh in range(2):
            bk = banks[4 + (s % 2) * 2 + nh]
            for gg in range(2):
                g = nh * 2 + gg
                for kk in range(2):
                    p.mm(bk[:, gg * 256:(gg + 1) * 256], pT[:, 2 * g + kk, :], wp_b[:, 2 * g + kk, :],
                         start=(kk == 0), stop=(kk == 1))
            xts1.append(bk)
        xt = x1[s % 4]
        p.dma(xt, h1[s])
        for nh in range(2):
            p.tt(DVE, hm1[s % 2][:, nh * 512:(nh + 1) * 512], xts1[nh], xt[:, nh * 512:(nh + 1) * 512], ALU.add)
        p.dma(hmid[s], hm1[s % 2])
        if debug:
            p.dma(V(dbg["d_hmid1"], dbg["d_hmid1"].ap[s * 128:(s + 1) * 128, :]), hm1[s % 2])
    _barrier(p)
    p.arena_off = mk5

    if upto == 6:
        p.final_wait(SP, outs + list(dbg.values()))
        p.emit()
        return nc

    mlp_phase(1, [h.v() for h in hmid], outs, 4, 2, None)

    p.final_wait(SP, outs + list(dbg.values()))
    p.emit()
    return nc


import ml_dtypes
from concourse.bass_utils import run_bass_kernel_spmd

_BF = ml_dtypes.bfloat16
_PROG_CACHE = {}


def _const_tables():
    t = {}
    t["ident"] = np.eye(128, dtype=np.float32).astype(_BF)
    blk = np.zeros((128, 128), np.float32)
    blk[:64, :64] = 1.0
    blk[64:, 64:] = 1.0
    t["blk"] = blk.astype(_BF)
    i = np.arange(128, dtype=np.float64)
    ang = 2 * np.pi * ((i[:, None] * i[None, :]) % 128) / 128.0
    t["dftc"] = np.concatenate([np.cos(ang) / 1024.0, np.sin(ang) / 1024.0], axis=1).astype(np.float32).astype(_BF)
    d1 = np.zeros((128, 2, 2, 64), np.float64)
    for half in range(2):
        d1[:, half, 0, :] = np.cos(ang[:, half * 64:(half + 1) * 64])
        d1[:, half, 1, :] = np.sin(ang[:, half * 64:(half + 1) * 64])
    t["dft1"] = d1.reshape(128, 256).astype(np.float32).astype(_BF)
    qc = np.arange(64)
    cs = np.clip(qc - 8, 0, 48)
    kc = np.arange(64)
    m = ((kc[:, None] >= cs[None, :]) & (kc[:, None] < cs[None, :] + 16)).astype(np.float32)
    t["cmask"] = np.tile(m, (2, 2)).astype(np.float32)
    pm = np.zeros((128, 4, 3, 128), np.float64)
    pmf = np.zeros((128, 4, 128), np.float64)
    pml = np.zeros((128, 4, 128), np.float64)
    tt = np.arange(128)
    for wi, w in enumerate(POOL_SIZES):
        for tq in range(128):
            lo, hi = tq - w // 2, tq + w - w // 2
            for tp in range(lo, hi):
                if tp < 0:
                    pm[tp + 128, wi, 0, tq] += 1.0 / w
                elif tp >= 128:
                    pm[tp - 128, wi, 2, tq] += 1.0 / w
                else:
                    pm[tp, wi, 1, tq] += 1.0 / w
            pm[tq, wi, 1, tq] -= 1.0
            lo_c = max(lo, 0)
            cnt = hi - lo_c
            for tp in range(lo_c, min(hi, 128)):
                pmf[tp, wi, tq] += 1.0 / cnt
            pmf[tq, wi, tq] -= 1.0
            hi_c = min(hi, 128)
            cnt = hi_c - lo
            for tp in range(max(lo, 0), hi_c):
                pml[tp, wi, tq] += 1.0 / cnt
            pml[tq, wi, tq] -= 1.0
    t["pm"] = pm.astype(np.float32).astype(_BF)
    t["pmf_first"] = pmf.astype(np.float32).astype(_BF)
    t["pml_last"] = pml.astype(np.float32).astype(_BF)
    t["pm_main"] = np.ascontiguousarray(pm[:, :, 1, :]).astype(np.float32).astype(_BF)
    return t


def _core_tables(hf):
    qb = 0 if hf == 0 else 31
    n2 = np.arange(64, dtype=np.int64)
    k1 = np.arange(128, dtype=np.int64)
    k2 = qb + np.arange(NSLOT, dtype=np.int64)
    k = k1[:, None] + 128 * k2[None, :]
    beta = 2 * np.pi * ((n2[:, None, None] * k[None]) % 8192) / 8192.0
    cb, sb = np.cos(beta), np.sin(beta)
    g = np.zeros((64, 2, 2, 128, NSLOT), np.float64)
    g[:, 0, 0] = cb
    g[:, 1, 0] = -sb
    g[:, 0, 1] = -sb
    g[:, 1, 1] = -cb
    gtab = g.reshape(128, 2, 128, NSLOT).astype(np.float32).astype(_BF)
    vm = np.zeros((128, NSLOT, 7, 2), np.float32)
    for s in range(NSLOT):
        gt = qb + s
        for di, dl in enumerate(range(-3, 4)):
            for qr in range(2):
                qrow = 2 * gt + qr
                rs = min(max(qrow - 4, 0), 120)
                for jr in range(2):
                    krow = 2 * (gt + dl) + jr
                    ok = (0 <= krow <= 127) and (rs <= krow < rs + 8) and (0 <= qrow <= 127)
                    vm[jr * 64:(jr + 1) * 64, s, di, qr] = 1.0 if ok else 0.0
    return gtab, vm.reshape(128, NSLOT * 14)


def _chunks(v):
    return np.ascontiguousarray(np.asarray(v, np.float32).reshape(8, 128).T)


def _prepare_inputs(inputs):
    f32 = lambda a: np.ascontiguousarray(np.asarray(a, dtype=np.float32))
    x = f32(inputs["x"]); c = f32(inputs["c"]); ctx = f32(inputs["ctx"]); c_ctx = f32(inputs["c_ctx"])
    rpb = f32(inputs["rpb"])[0]
    ct = _const_tables()
    jr = np.arange(128) // 64
    kc = np.arange(128) % 64
    qr = np.arange(128) // 64
    qc = np.arange(128) % 64
    dl = np.arange(-3, 4)
    dr = np.clip(2 * dl[None, :, None] + jr[:, None, None] - qr[None, None, :] + 7, 0, 14)
    dc = np.clip(kc[:, None] - qc[None, :] + 15, 0, 30)
    biasT = rpb[:, dr, dc[:, None, :]]
    biasT = np.ascontiguousarray(biasT.transpose(1, 2, 0, 3)).astype(np.float32)
    gvec = np.concatenate([_chunks(inputs["norm1_g"][0]), _chunks(inputs["norm1_g"][1]),
                           _chunks(inputs["norm2_g"][0]), _chunks(inputs["norm2_g"][1])], axis=1)
    qkg = np.stack([np.tile(f32(inputs["q_norm_g"])[0], 2), np.tile(f32(inputs["k_norm_g"])[0], 2)], axis=1)
    shared = {
        "gvec": np.ascontiguousarray(gvec), "qkg": np.ascontiguousarray(qkg.astype(np.float32)),
        "w_mod": f32(inputs["w_mod"]), "b_mod": f32(inputs["b_mod"]),
        "norm1_g": f32(inputs["norm1_g"]), "pool_scale": f32(inputs["pool_scale"]),
        "w_in": f32(inputs["w_in_even"])[0], "w_four": f32(inputs["w_four"])[0],
        "w_out": f32(inputs["w_out_even"])[0], "w_pool": f32(inputs["w_pool"])[0],
        "w_mlp1": f32(inputs["w_mlp1"]), "w_mlp2": f32(inputs["w_mlp2"]),
        "biasT": biasT, "cmask": ct["cmask"],
        "ident": ct["ident"], "blk": ct["blk"], "dftc": ct["dftc"], "dft1": ct["dft1"], "pm": ct["pm"],
    }
    per_hf = [_core_tables(0), _core_tables(1)]
    in_maps = []
    for core in range(8):
        b, hf = core // 2, core % 2
        qb = 0 if hf == 0 else 31
        xkv = np.zeros((NKV * 128, 1024), np.float32)
        lo_t, hi_t = qb - 3, qb - 3 + NKV
        a, e = max(lo_t, 0), min(hi_t, 64)
        xkv[(a - lo_t) * 128:(e - lo_t) * 128] = x[b, a * 128:e * 128]
        cvec = np.concatenate([_chunks(c[b]), _chunks(c_ctx)], axis=1)
        m = dict(shared)
        m.update({
            "xb": x[b], "xkv": xkv, "ctxb": ctx[b], "cvec": np.ascontiguousarray(cvec),
            "gtab": per_hf[hf][0], "vm": per_hf[hf][1],
            "pmf": ct["pmf_first"] if hf == 0 else ct["pm_main"],
            "pml": ct["pm_main"] if hf == 0 else ct["pml_last"],
        })
        in_maps.append(m)
    return in_maps


def kernel(**inputs):
    if "prog" not in _PROG_CACHE:
        _PROG_CACHE["prog"] = build_program(debug=False)
    nc = _PROG_CACHE["prog"]
    in_maps = _prepare_inputs(inputs)
    res = run_bass_kernel_spmd(nc, in_maps, core_ids=list(range(8)))
    out = np.empty((4, 8192, 1024), np.float32)
    for core in range(8):
        b, hf = core // 2, core % 2
        o = res.results[core]["out33"]
        if hf == 0:
            out[b, 0:4096] = o[0:4096]
        else:
            out[b, 4096:8192] = o[128:4224]
    return out
```
